# Optimizing a Trainium2 kernel written in Bass

```python
import math
import jax
import jax.numpy as jnp
from jax import lax
import numpy as np

D_MODEL = 1024
BATCH = 32
SEQ = 256
DEPTH = 2
DEC_BATCH = 4
DEC_SEQ = 2048
PAST_LEN = 256

GRID_W = 64
ROPE_BASE = 10000.0
NORM_EPS = 1e-6
SCAN_CHUNK = 128
Q_BLOCK = 128
CONV_W = 3
N_BRANCH = 4
SSD_HEADS = 8
SSD_HEAD_DIM = 64
SSD_INNER = SSD_HEADS * SSD_HEAD_DIM
SSD_GROUPS = 2
SSD_STATE = 64
SSD_CONV_CH = SSD_INNER + 2 * SSD_GROUPS * SSD_STATE
RET_HEADS = 4
RET_QK = 64
RET_V = 128
ATT_HEADS = 8
ATT_KV_HEADS = 2
ATT_HEAD_DIM = 64
MLA_HEADS = 8
MLA_Q_RANK = 384
MLA_KV_RANK = 256
MLA_NOPE = 64
MLA_ROPE = 32
MLA_V = 64
D_FF = 2816
IN_WIDTHS = (SSD_INNER, SSD_CONV_CH, 2 * SSD_HEADS,
             RET_HEADS * RET_QK, RET_HEADS * RET_QK, RET_HEADS * RET_V, RET_HEADS * RET_V,
             ATT_HEADS * ATT_HEAD_DIM, ATT_KV_HEADS * ATT_HEAD_DIM, ATT_KV_HEADS * ATT_HEAD_DIM,
             MLA_Q_RANK, MLA_KV_RANK + MLA_ROPE)
IN_W = sum(IN_WIDTHS)

kernel_name = 'hybrid_prefix_diffusion_step'


def _rmsnorm(x, w):
    xf = x.astype(jnp.float32)
    y = xf * lax.rsqrt(jnp.mean(xf * xf, axis=-1, keepdims=True) + NORM_EPS)
    return y.astype(x.dtype) * w


def _head_layernorm(x):
    xf = x.astype(jnp.float32)
    xc = xf - jnp.mean(xf, axis=-1, keepdims=True)
    return xc * lax.rsqrt(jnp.mean(xc * xc, axis=-1, keepdims=True) + NORM_EPS)


def _dwconv_centred(x, w, b):
    pad = w.shape[0] // 2
    L = x.shape[1]
    xp = jnp.pad(x, ((0, 0), (pad, pad), (0, 0)))
    y = b
    for j in range(w.shape[0]):
        y = y + xp[:, j:j + L, :] * w[j]
    return y


def _rotate(x, ang):
    x1, x2 = jnp.split(x.astype(jnp.float32), 2, axis=-1)
    cos = jnp.cos(ang)[None, :, None, :]
    sin = jnp.sin(ang)[None, :, None, :]
    return jnp.concatenate([x1 * cos - x2 * sin, x2 * cos + x1 * sin], axis=-1).astype(x.dtype)


def _rope_2d(x):
    L = x.shape[1]
    d_axis = x.shape[-1] // 2
    n_rows = L // GRID_W
    rows = jnp.repeat(jnp.arange(n_rows, dtype=jnp.float32), GRID_W)
    cols = jnp.tile(jnp.arange(GRID_W, dtype=jnp.float32), n_rows)
    inv_freq = ROPE_BASE ** (-jnp.arange(0, d_axis, 2, dtype=jnp.float32) / d_axis)
    xr = _rotate(x[..., :d_axis], rows[:, None] * inv_freq[None, :])
    xc = _rotate(x[..., d_axis:], cols[:, None] * inv_freq[None, :])
    return jnp.concatenate([xr, xc], axis=-1)


def _chunk_scan(q, k, v, log_a, s0):
    bsz, L, nh, dk = q.shape
    dv = v.shape[-1]
    nc = L // SCAN_CHUNK
    T = SCAN_CHUNK
    f32 = jnp.float32
    qc = q.astype(f32).reshape(bsz, nc, T, nh, dk)
    kc = k.astype(f32).reshape(bsz, nc, T, nh, dk)
    vc = v.astype(f32).reshape(bsz, nc, T, nh, dv)
    acum = jnp.cumsum(log_a.astype(f32).reshape(bsz, nc, T, nh), axis=2)
    diff = acum[:, :, :, None, :] - acum[:, :, None, :, :]
    lower = jnp.tril(jnp.ones((T, T), dtype=bool))[None, None, :, :, None]
    decay = jnp.exp(jnp.where(lower, diff, -jnp.inf))
    scores = jnp.einsum('bcihd,bcjhd->bcijh', qc, kc) * decay
    o_intra = jnp.einsum('bcijh,bcjhe->bcihe', scores, vc)
    w_end = jnp.exp(acum[:, :, -1:, :] - acum)
    chunk_states = jnp.einsum('bcjhd,bcjh,bcjhe->bchde', kc, w_end, vc)
    chunk_decay = jnp.exp(acum[:, :, -1, :])

    def step(s, inp):
        cs, cd = inp
        return s * cd[..., None, None] + cs, s

    s_final, s_prev = lax.scan(step, s0.astype(f32),
                               (chunk_states.transpose(1, 0, 2, 3, 4), chunk_decay.transpose(1, 0, 2)))
    s_prev = s_prev.transpose(1, 0, 2, 3, 4)
    o_inter = jnp.einsum('bcihd,bchde->bcihe', qc * jnp.exp(acum)[..., None], s_prev)
    return (o_intra + o_inter).reshape(bsz, L, nh, dv), s_final


def _bidir_scan(q, k_f, k_b, v, la_f, la_b, s0_f, s0_b):
    o_f, s_f = _chunk_scan(q, k_f, v, la_f, s0_f)
    o_b, s_b = _chunk_scan(jnp.flip(q, 1), jnp.flip(k_b, 1), jnp.flip(v, 1), jnp.flip(la_b, 1), s0_b)
    return o_f + jnp.flip(o_b, 1), s_f, s_b


def _attention(q, k, v, scale):
    bsz, lq, n_h, dk = q.shape
    n_g = k.shape[2]
    rep = n_h // n_g
    nb = lq // Q_BLOCK
    qb = q.reshape(bsz, nb, Q_BLOCK, n_g, rep, dk).transpose(1, 0, 2, 3, 4, 5)

    def block(q_blk):
        s = jnp.einsum('bqgrd,bkgd->bgrqk', q_blk, k, preferred_element_type=jnp.float32) * scale
        p = jax.nn.softmax(s, axis=-1).astype(v.dtype)
        return jnp.einsum('bgrqk,bkge->bqgre', p, v)

    o = lax.map(block, qb)
    return o.transpose(1, 0, 2, 3, 4, 5).reshape(bsz, lq, n_h, v.shape[-1])


def _token_mixer(h, lw, ctx):
    bsz, L, _ = h.shape
    latent = ctx is not None
    f32 = jnp.float32
    split_at = np.cumsum(IN_WIDTHS)[:-1].tolist()
    (z, xbc, dt_raw, rq, rk, rv, rg, aq, ak, av, mcq, mckv) = jnp.split(h @ lw['w_in'], split_at, axis=-1)

    xbc = jax.nn.silu(_dwconv_centred(xbc, lw['ssd_conv_w'], lw['ssd_conv_b']))
    xs, bm, cm = jnp.split(xbc, [SSD_INNER, SSD_INNER + SSD_GROUPS * SSD_STATE], axis=-1)
    xs = xs.reshape(bsz, L, SSD_HEADS, SSD_HEAD_DIM)
    rep = SSD_HEADS // SSD_GROUPS
    bh = jnp.repeat(bm.reshape(bsz, L, SSD_GROUPS, SSD_STATE), rep, axis=2)
    ch = jnp.repeat(cm.reshape(bsz, L, SSD_GROUPS, SSD_STATE), rep, axis=2)
    dt = jax.nn.softplus(dt_raw.reshape(bsz, L, 2, SSD_HEADS).astype(f32) + lw['ssd_dt_bias'].astype(f32))
    a = -jnp.exp(lw['ssd_a_log'].astype(f32))
    s0 = ctx['ssd'] if latent else jnp.zeros((bsz, 2, SSD_HEADS, SSD_STATE, SSD_HEAD_DIM), f32)
    y, ssd_f, ssd_b = _bidir_scan(ch, bh * dt[:, :, 0, :, None], bh * dt[:, :, 1, :, None], xs,
                                  dt[:, :, 0] * a[0], dt[:, :, 1] * a[1], s0[:, 0], s0[:, 1])
    y = y.astype(h.dtype) + lw['ssd_d'][:, None] * xs
    o_ssd = _rmsnorm(y.reshape(bsz, L, SSD_INNER) * jax.nn.silu(z), lw['ssd_norm_w'])

    rq = rq.reshape(bsz, L, RET_HEADS, RET_QK)
    rk = rk.reshape(bsz, L, RET_HEADS, RET_QK) * RET_QK ** -0.5
    rv = rv.reshape(bsz, L, RET_HEADS, RET_V)
    if latent:
        rq = _rope_2d(rq)
        rk = _rope_2d(rk)
    la = jax.nn.log_sigmoid(lw['ret_decay_logit'].astype(f32))
    la_f = jnp.broadcast_to(la[0], (bsz, L, RET_HEADS))
    la_b = jnp.broadcast_to(la[1], (bsz, L, RET_HEADS))
    r0 = ctx['ret'] if latent else jnp.zeros((bsz, 2, RET_HEADS, RET_QK, RET_V), f32)
    o, ret_f, ret_b = _bidir_scan(rq, rk, rk, rv, la_f, la_b, r0[:, 0], r0[:, 1])
    o_ret = (_head_layernorm(o).astype(h.dtype).reshape(bsz, L, RET_HEADS * RET_V)
             * lw['ret_gn_w'] * jax.nn.silu(rg))

    aq = _rmsnorm(aq.reshape(bsz, L, ATT_HEADS, ATT_HEAD_DIM), lw['att_q_norm'])
    ak = _rmsnorm(ak.reshape(bsz, L, ATT_KV_HEADS, ATT_HEAD_DIM), lw['att_k_norm'])
    av = av.reshape(bsz, L, ATT_KV_HEADS, ATT_HEAD_DIM)
    if latent:
        keys = jnp.concatenate([ctx['att_k'], _rope_2d(ak)], axis=1)
        vals = jnp.concatenate([ctx['att_v'], av], axis=1)
        q_att = _rope_2d(aq)
    else:
        keys, vals, q_att = ak, av, aq
    o_att = _attention(q_att, keys, vals, ATT_HEAD_DIM ** -0.5).reshape(bsz, L, ATT_HEADS * ATT_HEAD_DIM)

    qm = (_rmsnorm(mcq, lw['mla_q_norm']) @ lw['mla_w_uq']).reshape(bsz, L, MLA_HEADS, MLA_NOPE + MLA_ROPE)
    q_nope, q_rope = jnp.split(qm, [MLA_NOPE], axis=-1)
    ckv = _rmsnorm(mckv[..., :MLA_KV_RANK], lw['mla_kv_norm'])
    krope = mckv[..., MLA_KV_RANK:]
    if latent:
        q_rope = _rope_2d(q_rope)
        ckv_keys = jnp.concatenate([ctx['mla_ckv'], ckv], axis=1)
        kr_keys = jnp.concatenate([ctx['mla_krope'], _rope_2d(krope[:, :, None, :])[:, :, 0, :]], axis=1)
    else:
        ckv_keys, kr_keys = ckv, krope
    lk = ckv_keys.shape[1]
    kv = (ckv_keys @ lw['mla_w_ukv']).reshape(bsz, lk, MLA_HEADS, MLA_NOPE + MLA_V)
    k_nope, v_m = jnp.split(kv, [MLA_NOPE], axis=-1)
    k_m = jnp.concatenate([k_nope, jnp.broadcast_to(kr_keys[:, :, None, :], (bsz, lk, MLA_HEADS, MLA_ROPE))], axis=-1)
    q_m = jnp.concatenate([q_nope, q_rope], axis=-1)
    o_mla = _attention(q_m, k_m, v_m, (MLA_NOPE + MLA_ROPE) ** -0.5).reshape(bsz, L, MLA_HEADS * MLA_V)

    g = jax.nn.sigmoid((h @ lw['w_merge'] + lw['b_merge']).astype(f32)).astype(h.dtype)
    g = g.reshape(bsz, L, N_BRANCH, D_MODEL)
    merged = (g[:, :, 0] * (o_ssd @ lw['w_br_ssd']) + g[:, :, 1] * (o_ret @ lw['w_br_ret'])
              + g[:, :, 2] * (o_att @ lw['w_br_att']) + g[:, :, 3] * (o_mla @ lw['w_br_mla']))
    out = merged @ lw['w_out']
    if latent:
        return out, None
    cache = {
        'ssd': jnp.stack([ssd_f, ssd_b], axis=1).astype(h.dtype),
        'ret': jnp.stack([ret_f, ret_b], axis=1).astype(h.dtype),
        'att_k': ak, 'att_v': av, 'mla_ckv': ckv, 'mla_krope': krope,
    }
    return out, cache


def _conv_ffn(h, lw):
    u = _dwconv_centred(h @ lw['w_ffn_up'], lw['ffn_conv_w'], lw['ffn_conv_b'])
    up, gate = jnp.split(u, 2, axis=-1)
    return (jax.nn.silu(gate) * up) @ lw['w_ffn_down']


def _trunk_layer(x, mod, lw, ctx):
    sh_a, sc_a, g_a, sh_f, sc_f, g_f = jnp.split(mod[:, None, :], 6, axis=-1)
    h = _rmsnorm(x, lw['g_pre_mix']) * (1.0 + sc_a) + sh_a
    m, cache = _token_mixer(h, lw, ctx)
    x = x + g_a * _rmsnorm(m, lw['g_post_mix'])
    h = _rmsnorm(x, lw['g_pre_ffn']) * (1.0 + sc_f) + sh_f
    x = x + g_f * _rmsnorm(_conv_ffn(h, lw), lw['g_post_ffn'])
    return x, cache


def setup_inputs(seed: int = 0) -> dict:
    key = jax.random.key(seed)
    ks = jax.random.split(key, 48)
    f32 = jnp.float32

    def nrm(i, shape, scale=1.0):
        return jax.random.normal(ks[i], shape, f32) * scale

    def gain(i, shape):
        return 1.0 + 0.02 * jax.random.normal(ks[i], shape, f32)

    a_log = jnp.log(jax.random.uniform(ks[10], (DEPTH, 2, SSD_HEADS), f32, 1.0, 16.0))
    dt0 = jnp.exp(jax.random.uniform(ks[11], (DEPTH, 2, SSD_HEADS), f32, math.log(1e-3), math.log(1e-1)))
    dt_bias = dt0 + jnp.log(-jnp.expm1(-dt0))
    gamma = 1.0 - 2.0 ** (-5.0 - jnp.arange(RET_HEADS, dtype=f32))
    ret_logit = (jnp.log(gamma) - jnp.log1p(-gamma))[None, None, :] + nrm(12, (DEPTH, 2, RET_HEADS), 0.1)
    return {
        'x_prompt': nrm(0, (BATCH, SEQ, D_MODEL)),
        'x_sample': nrm(1, (DEC_BATCH, DEC_SEQ, D_MODEL)),
        'state_ssd': nrm(2, (DEC_BATCH, DEPTH, 2, SSD_HEADS, SSD_STATE, SSD_HEAD_DIM), 0.5),
        'state_ret': nrm(3, (DEC_BATCH, DEPTH, 2, RET_HEADS, RET_QK, RET_V), 0.5),
        'cache_att_k': nrm(4, (DEC_BATCH, DEPTH, PAST_LEN, ATT_KV_HEADS, ATT_HEAD_DIM)),
        'cache_att_v': nrm(5, (DEC_BATCH, DEPTH, PAST_LEN, ATT_KV_HEADS, ATT_HEAD_DIM)),
        'cache_mla_ckv': nrm(6, (DEC_BATCH, DEPTH, PAST_LEN, MLA_KV_RANK)),
        'cache_mla_krope': nrm(7, (DEC_BATCH, DEPTH, PAST_LEN, MLA_ROPE)),
        'c': nrm(8, (DEC_BATCH, D_MODEL)),
        'c_ctx': nrm(9, (D_MODEL,)),
        'w_mod': nrm(13, (DEPTH, D_MODEL, 6 * D_MODEL), 0.5 * D_MODEL ** -0.5),
        'b_mod': nrm(14, (DEPTH, 6 * D_MODEL), 0.01),
        'g_pre_mix': gain(15, (DEPTH, D_MODEL)),
        'g_post_mix': gain(16, (DEPTH, D_MODEL)),
        'g_pre_ffn': gain(17, (DEPTH, D_MODEL)),
        'g_post_ffn': gain(18, (DEPTH, D_MODEL)),
        'w_in': nrm(19, (DEPTH, D_MODEL, IN_W), D_MODEL ** -0.5),
        'ssd_conv_w': nrm(20, (DEPTH, CONV_W, SSD_CONV_CH), CONV_W ** -0.5),
        'ssd_conv_b': nrm(21, (DEPTH, SSD_CONV_CH), 0.01),
        'ssd_dt_bias': dt_bias,
        'ssd_a_log': a_log,
        'ssd_d': gain(22, (DEPTH, SSD_HEADS)),
        'ssd_norm_w': gain(23, (DEPTH, SSD_INNER)),
        'ret_decay_logit': ret_logit,
        'ret_gn_w': gain(24, (DEPTH, RET_HEADS * RET_V)),
        'att_q_norm': gain(25, (DEPTH, ATT_HEAD_DIM)),
        'att_k_norm': gain(26, (DEPTH, ATT_HEAD_DIM)),
        'mla_q_norm': gain(27, (DEPTH, MLA_Q_RANK)),
        'mla_w_uq': nrm(28, (DEPTH, MLA_Q_RANK, MLA_HEADS * (MLA_NOPE + MLA_ROPE)), MLA_Q_RANK ** -0.5),
        'mla_kv_norm': gain(29, (DEPTH, MLA_KV_RANK)),
        'mla_w_ukv': nrm(30, (DEPTH, MLA_KV_RANK, MLA_HEADS * (MLA_NOPE + MLA_V)), MLA_KV_RANK ** -0.5),
        'w_br_ssd': nrm(31, (DEPTH, SSD_INNER, D_MODEL), SSD_INNER ** -0.5),
        'w_br_ret': nrm(32, (DEPTH, RET_HEADS * RET_V, D_MODEL), (RET_HEADS * RET_V) ** -0.5),
        'w_br_att': nrm(33, (DEPTH, ATT_HEADS * ATT_HEAD_DIM, D_MODEL), (ATT_HEADS * ATT_HEAD_DIM) ** -0.5),
        'w_br_mla': nrm(34, (DEPTH, MLA_HEADS * MLA_V, D_MODEL), (MLA_HEADS * MLA_V) ** -0.5),
        'w_merge': nrm(35, (DEPTH, D_MODEL, N_BRANCH * D_MODEL), D_MODEL ** -0.5),
        'b_merge': nrm(36, (DEPTH, N_BRANCH * D_MODEL), 0.01),
        'w_out': nrm(37, (DEPTH, D_MODEL, D_MODEL), D_MODEL ** -0.5),
        'w_ffn_up': nrm(38, (DEPTH, D_MODEL, 2 * D_FF), D_MODEL ** -0.5),
        'ffn_conv_w': nrm(39, (DEPTH, CONV_W, 2 * D_FF), CONV_W ** -0.5),
        'ffn_conv_b': nrm(40, (DEPTH, 2 * D_FF), 0.01),
        'w_ffn_down': nrm(41, (DEPTH, D_FF, D_MODEL), D_FF ** -0.5),
    }


def reference(x_prompt, x_sample, state_ssd, state_ret, cache_att_k, cache_att_v, cache_mla_ckv,
              cache_mla_krope, c, c_ctx, w_mod, b_mod, g_pre_mix, g_post_mix, g_pre_ffn, g_post_ffn,
              w_in, ssd_conv_w, ssd_conv_b, ssd_dt_bias, ssd_a_log, ssd_d, ssd_norm_w, ret_decay_logit,
              ret_gn_w, att_q_norm, att_k_norm, mla_q_norm, mla_w_uq, mla_kv_norm, mla_w_ukv, w_br_ssd,
              w_br_ret, w_br_att, w_br_mla, w_merge, b_merge, w_out, w_ffn_up, ffn_conv_w, ffn_conv_b,
              w_ffn_down):
    y_prompt = x_prompt
    y_sample = x_sample
    st_ssd, st_ret, ck_k, ck_v, ck_c, ck_r = [], [], [], [], [], []
    for i in range(DEPTH):
        lw = {
            'g_pre_mix': g_pre_mix[i], 'g_post_mix': g_post_mix[i],
            'g_pre_ffn': g_pre_ffn[i], 'g_post_ffn': g_post_ffn[i],
            'w_in': w_in[i], 'ssd_conv_w': ssd_conv_w[i], 'ssd_conv_b': ssd_conv_b[i],
            'ssd_dt_bias': ssd_dt_bias[i], 'ssd_a_log': ssd_a_log[i], 'ssd_d': ssd_d[i],
            'ssd_norm_w': ssd_norm_w[i], 'ret_decay_logit': ret_decay_logit[i], 'ret_gn_w': ret_gn_w[i],
            'att_q_norm': att_q_norm[i], 'att_k_norm': att_k_norm[i],
            'mla_q_norm': mla_q_norm[i], 'mla_w_uq': mla_w_uq[i],
            'mla_kv_norm': mla_kv_norm[i], 'mla_w_ukv': mla_w_ukv[i],
            'w_br_ssd': w_br_ssd[i], 'w_br_ret': w_br_ret[i], 'w_br_att': w_br_att[i], 'w_br_mla': w_br_mla[i],
            'w_merge': w_merge[i], 'b_merge': b_merge[i], 'w_out': w_out[i],
            'w_ffn_up': w_ffn_up[i], 'ffn_conv_w': ffn_conv_w[i], 'ffn_conv_b': ffn_conv_b[i],
            'w_ffn_down': w_ffn_down[i],
        }
        mod_ctx = (jax.nn.silu(c_ctx) @ w_mod[i] + b_mod[i])[None, :]
        y_prompt, cache = _trunk_layer(y_prompt, mod_ctx, lw, None)
        st_ssd.append(cache['ssd'])
        st_ret.append(cache['ret'])
        ck_k.append(cache['att_k'])
        ck_v.append(cache['att_v'])
        ck_c.append(cache['mla_ckv'])
        ck_r.append(cache['mla_krope'])
        mod_lat = jax.nn.silu(c) @ w_mod[i] + b_mod[i]
        ctx = {
            'ssd': state_ssd[:, i], 'ret': state_ret[:, i],
            'att_k': cache_att_k[:, i], 'att_v': cache_att_v[:, i],
            'mla_ckv': cache_mla_ckv[:, i], 'mla_krope': cache_mla_krope[:, i],
        }
        y_sample, _ = _trunk_layer(y_sample, mod_lat, lw, ctx)
    new_state_ssd = jnp.stack(st_ssd, axis=1)
    new_state_ret = jnp.stack(st_ret, axis=1)
    new_cache_att_k = jnp.stack(ck_k, axis=1)
    new_cache_att_v = jnp.stack(ck_v, axis=1)
    new_cache_mla_ckv = jnp.stack(ck_c, axis=1)
    new_cache_mla_krope = jnp.stack(ck_r, axis=1)
    return (y_prompt, y_sample, new_state_ssd, new_state_ret, new_cache_att_k, new_cache_att_v,
            new_cache_mla_ckv, new_cache_mla_krope)
```

```python
import numpy as np
from contextlib import ExitStack
import concourse.bass as bass
import concourse.mybir as mybir
from concourse.bass_utils import run_bass_kernel_spmd

F32 = mybir.dt.float32
BF16 = mybir.dt.bfloat16
AF = mybir.ActivationFunctionType
ALU = mybir.AluOpType
AX = mybir.AxisListType

T = 2048
NT = 16
NG = 4
DEPTH = 2
EPS = 1e-6
NEG = -30000.0
BIG = 512.0

_VOFF = {}
_NV = 0


def _vreg(name, n):
    global _NV
    _VOFF[name] = _NV
    _NV += n


for _n, _c in [("b_mod", 48), ("g_pre_mix", 8), ("g_post_mix", 8), ("g_pre_ffn", 8), ("g_post_ffn", 8),
               ("ssd_cw", 18), ("ssd_cb", 6), ("ffn_cw", 132), ("ffn_cb", 44), ("b_merge", 32),
               ("qn", 1), ("qnp", 1), ("kn", 1), ("knp", 1), ("mqn", 3), ("mkvn", 2),
               ("dt_bias", 16), ("a_log", 16), ("ssd_d", 8), ("ret_logit", 8),
               ("ssd_nw", 512), ("ret_gw", 512)]:
    _vreg(_n, _c)

_COFF = {"ident": 0, "triU": 128, "ntriS": 256, "nmU": 384, "nmL": 512, "ones": 640, "idx": 768}
_NC = 772
_CBOFF = {"ident": 0, "blk64": 128, "o384": 256, "o256": 384, "o128": 512}
_NCB = 1152

ENGS = ("pe", "act", "dve", "pool", "sp")


class Op:
    __slots__ = ("eng", "fn", "deps", "signal", "sigval", "slot", "dmaval")


class Prog:
    def __init__(self, nc, es):
        self.nc = nc
        self.es = es
        self.ops = {e: [] for e in ENGS}
        self.lastw = {}
        self.rd = {}
        self.esem = {e: es.enter_context(nc.semaphore("sem_" + e)) for e in ENGS}
        self.ecnt = {e: 0 for e in ENGS}
        self.dsem = {}
        self.dcnt = {}
        self.pend_dma = []
        self.waited = {e: {} for e in ENGS}
        self.out_dmas = []
        self.sealed = {}
        self.nops = 0

    def add(self, eng, fn, r=(), w=(), slot=None):
        op = Op()
        op.eng = eng
        op.fn = fn
        op.signal = False
        op.sigval = None
        op.slot = slot
        op.dmaval = None
        deps = []
        for k in r:
            d = self.lastw.get(k)
            if d is not None:
                deps.append(d)
        for k in w:
            d = self.lastw.get(k)
            if d is not None:
                deps.append(d)
            rr = self.rd.get(k)
            if rr:
                deps.extend(rr[0].values())
                deps.extend(rr[1])
        op.deps = [d for d in deps if d is not op and not (d.eng == "pe" and eng == "pe" and d.slot is None
                                                          and slot is None)]
        for k in w:
            self.lastw[k] = op
            self.rd[k] = [{}, []]
        for k in r:
            rr = self.rd.setdefault(k, [{}, []])
            if slot is not None:
                rr[1].append(op)
            else:
                rr[0][eng] = op
        if slot is not None:
            if slot not in self.dsem:
                self.dsem[slot] = self.es.enter_context(self.nc.semaphore("d_" + slot))
                self.dcnt[slot] = 0
            self.dcnt[slot] += 16
            op.dmaval = self.dcnt[slot]
            self.pend_dma.append(op)
        self.ops[eng].append(op)
        self.nops += 1
        return op

    def seal(self, slot):
        tot = self.dcnt.get(slot)
        if tot is None:
            return
        grp = [op for op in self.pend_dma if op.slot == slot and op.dmaval > self.sealed.get(slot, 0)]
        gs = set(id(o) for o in grp)
        for op in grp:
            op.dmaval = tot
            op.deps = [d for d in op.deps if id(d) not in gs]
        self.sealed[slot] = tot

    def flush(self):
        lasts = [self.ops[e][-1] for e in ENGS if self.ops[e]]
        b1 = Op()
        b1.eng, b1.fn, b1.signal, b1.sigval, b1.slot, b1.dmaval = "sp", None, False, None, None, None
        b1.deps = [d for d in lasts] + list(self.pend_dma)
        self.ops["sp"].append(b1)
        for e in ENGS:
            if e == "sp":
                continue
            b = Op()
            b.eng, b.fn, b.signal, b.sigval, b.slot, b.dmaval = e, None, False, None, None, None
            b.deps = [b1]
            self.ops[e].append(b)
        for e in ENGS:
            for op in self.ops[e]:
                for d in op.deps:
                    if d.slot is None:
                        d.signal = True
        for e in ENGS:
            for op in self.ops[e]:
                if op.slot is None and op.signal:
                    self.ecnt[e] += 1
                    op.sigval = self.ecnt[e]
        nc = self.nc
        with nc.Block() as block:
            def mk(ename):
                def body(eng):
                    wt = self.waited[ename]
                    for op in self.ops[ename]:
                        for d in op.deps:
                            if d.slot is not None:
                                sem, val = self.dsem[d.slot], d.dmaval
                                key = "d_" + d.slot
                            else:
                                sem, val = self.esem[d.eng], d.sigval
                                key = d.eng
                            if wt.get(key, 0) >= val:
                                continue
                            eng.wait_ge(sem, val)
                            wt[key] = val
                        if op.fn is None:
                            if op.signal:
                                eng.nop().then_inc(self.esem[ename], 1)
                            continue
                        ins = op.fn(eng)
                        if op.slot is not None:
                            ins.then_inc(self.dsem[op.slot], 16)
                        elif op.signal:
                            ins.then_inc(self.esem[ename], 1)
                return body
            block.tensor(mk("pe"))
            block.scalar(mk("act"))
            block.vector(mk("dve"))
            block.gpsimd(mk("pool"))
            block.sync(mk("sp"))
        self.ops = {e: [] for e in ENGS}
        self.lastw = {}
        self.rd = {}
        self.pend_dma = []


def _rope_tables(L, dim):
    d_axis = dim // 2
    n_rows = L // 64
    rows = np.repeat(np.arange(n_rows, dtype=np.float32), 64)
    cols = np.tile(np.arange(64, dtype=np.float32), n_rows)
    inv = (np.float32(10000.0) ** (-np.arange(0, d_axis, 2, dtype=np.float32) / np.float32(d_axis))).astype(np.float32)
    ar = rows[:, None] * inv[None, :]
    ac = cols[:, None] * inv[None, :]
    cos = np.concatenate([np.cos(ar), np.cos(ar), np.cos(ac), np.cos(ac)], axis=1).astype(np.float32)
    sin = np.concatenate([-np.sin(ar), np.sin(ar), -np.sin(ac), np.sin(ac)], axis=1).astype(np.float32)
    return cos.T.copy(), sin.T.copy()


def _perm(dim):
    q = dim // 4
    return np.concatenate([np.arange(q, 2 * q), np.arange(0, q), np.arange(3 * q, 4 * q), np.arange(2 * q, 3 * q)])


def _pk(a):
    K, N = a.shape
    return np.ascontiguousarray(a.reshape(K // 128, 128, N).transpose(1, 0, 2))


def _col8(v):
    return np.ascontiguousarray(v.reshape(-1, 128).T)


def _host_prep(inp):
    f = np.float32
    A = {k: np.asarray(v, dtype=f) for k, v in inp.items()}
    shared = {}
    p64, p32 = _perm(64), _perm(32)
    win, wuq, wuk, wuv, vecs = [], [], [], [], []
    offs = np.cumsum([0, 512, 768, 16, 256, 256, 512, 512, 512, 128, 128, 384, 288])
    for l in range(DEPTH):
        W = A["w_in"][l]
        z, xbc, dtr, rq, rk, rv, rg, aq, ak, av, mcq, mckv = [W[:, offs[i]:offs[i + 1]] for i in range(12)]

        def hp(m, p):
            nh = m.shape[1] // len(p)
            return m.reshape(1024, nh, len(p))[:, :, p].reshape(1024, -1)
        ckv, kr = mckv[:, :256], mckv[:, 256:]
        ext = np.concatenate([z, xbc, dtr,
                              rq, hp(rq, p64), rk, hp(rk, p64), rv, rg,
                              aq, hp(aq, p64), ak, hp(ak, p64), av,
                              mcq, ckv, kr, hp(kr, p32)], axis=1)
        assert ext.shape[1] == 5456
        win.append(_pk(ext))
        U = A["mla_w_uq"][l].reshape(384, 8, 96)
        ua = np.zeros((384, 8, 192), f)
        ua[:, :, 0:96] = U
        ua[:, :, 96 + 64:192] = U[:, :, 64:96][:, :, p32]
        wuq.append(_pk(ua.reshape(384, 1536)))
        KV = A["mla_w_ukv"][l].reshape(256, 8, 128)
        wuk.append(_pk(np.ascontiguousarray(KV[:, :, :64]).reshape(256, 512)))
        wuv.append(_pk(np.ascontiguousarray(KV[:, :, 64:]).reshape(256, 512)))
        V = np.zeros((128, _NV), f)

        def put(name, arr):
            arr = np.asarray(arr, f)
            V[:arr.shape[0], _VOFF[name]:_VOFF[name] + arr.shape[1]] = arr
        put("b_mod", A["b_mod"][l].reshape(48, 128).T)
        for nm in ("g_pre_mix", "g_post_mix", "g_pre_ffn", "g_post_ffn"):
            put(nm, _col8(A[nm][l]))
        cw = A["ssd_conv_w"][l]
        put("ssd_cw", np.concatenate([cw[j].reshape(6, 128).T for j in range(3)], axis=1))
        put("ssd_cb", A["ssd_conv_b"][l].reshape(6, 128).T)
        fw = A["ffn_conv_w"][l]
        put("ffn_cw", np.concatenate([fw[j].reshape(44, 128).T for j in range(3)], axis=1))
        put("ffn_cb", A["ffn_conv_b"][l].reshape(44, 128).T)
        put("b_merge", A["b_merge"][l].reshape(32, 128).T)
        qn, kn = A["att_q_norm"][l], A["att_k_norm"][l]
        put("qn", qn[:, None]); put("qnp", qn[p64][:, None]); put("kn", kn[:, None]); put("knp", kn[p64][:, None])
        put("mqn", A["mla_q_norm"][l].reshape(3, 128).T)
        put("mkvn", A["mla_kv_norm"][l].reshape(2, 128).T)
        bc = lambda v: np.broadcast_to(np.asarray(v, f).reshape(1, -1), (128, np.asarray(v).size))
        put("dt_bias", bc(A["ssd_dt_bias"][l])); put("a_log", bc(A["ssd_a_log"][l]))
        put("ssd_d", bc(A["ssd_d"][l])); put("ret_logit", bc(A["ret_decay_logit"][l]))
        put("ssd_nw", bc(A["ssd_norm_w"][l])); put("ret_gw", bc(A["ret_gn_w"][l]))
        vecs.append(V)
    shared["win"] = np.stack(win)
    shared["wuq"] = np.stack(wuq)
    shared["wuk"] = np.stack(wuk)
    shared["wuv"] = np.stack(wuv)
    shared["vecs"] = np.stack(vecs)
    shared["wmod"] = np.stack([_pk(A["w_mod"][l]) for l in range(DEPTH)])
    shared["wmerge"] = np.stack([_pk(A["w_merge"][l]) for l in range(DEPTH)])
    shared["wout"] = np.stack([_pk(A["w_out"][l]) for l in range(DEPTH)])
    shared["wup"] = np.stack([_pk(A["w_ffn_up"][l]) for l in range(DEPTH)])
    shared["wdown"] = np.stack([_pk(A["w_ffn_down"][l]) for l in range(DEPTH)])
    shared["wbr01"] = np.stack([np.stack([_pk(A[nm][l]) for nm in ("w_br_ssd", "w_br_ret", "w_br_att", "w_br_mla")])
                                for l in range(DEPTH)])
    C = np.zeros((128, _NC), f)
    ii = np.arange(128)
    C[:, 0:128] = np.eye(128)
    C[:, 128:256] = (ii[:, None] <= ii[None, :])
    C[:, 256:384] = -1.0 * (ii[:, None] < ii[None, :])
    C[:, 384:512] = np.where(ii[:, None] <= ii[None, :], 0.0, NEG)
    C[:, 512:640] = np.where(ii[:, None] >= ii[None, :], 0.0, NEG)
    C[:, 640:768] = 1.0
    shared["consts"] = C
    CB = np.zeros((128, _NCB), f)
    CB[:, 0:128] = np.eye(128)
    CB[0:64, 128:192] = 1.0 / 64
    CB[64:128, 192:256] = 1.0 / 64
    CB[:, 256:384] = 1.0 / 384
    CB[:, 384:512] = 1.0 / 256
    CB[:, 512:640] = 1.0 / 128
    CB[:, 640:768] = (ii[:, None] <= ii[None, :])
    CB[:, 768:896] = -1.0 * (ii[:, None] < ii[None, :])
    CB[:, 896:1024] = np.where(ii[:, None] <= ii[None, :], 0.0, NEG)
    CB[:, 1024:1152] = np.where(ii[:, None] >= ii[None, :], 0.0, NEG)
    shared["constb"] = CB

    cos64, sin64 = _rope_tables(T, 64)
    cos32, sin32 = _rope_tables(T, 32)
    percore = []
    for c in range(8):
        prompt = c < 4
        d = {}
        if prompt:
            d["x"] = np.ascontiguousarray(A["x_prompt"][8 * c:8 * c + 8].reshape(T, 1024))
            d["cond"] = _col8(A["c_ctx"])
            d["s_ssd"] = np.zeros((2, 2, 128, 512), f)
            d["s_ret"] = np.zeros((2, 2, 128, 512), f)
            d["ckT"] = np.zeros((2, 2, 64, 256), f)
            d["cv"] = np.zeros((2, 256, 128), f)
            d["cckvT"] = np.zeros((2, 256, 256), f)
            d["ckrT"] = np.zeros((2, 32, 256), f)
            r64 = np.zeros((128, 2, T), f); r64[:, 0] = 1.0
            r32 = np.zeros((128, 2, T), f); r32[:, 0] = 1.0
            aq = np.zeros((9, T), f)
            ak = np.zeros((9, T + 256), f)
            for s in range(8):
                aq[s, 256 * s:256 * (s + 1)] = BIG
                ak[s, 256 + 256 * s:256 + 256 * (s + 1)] = 1.0
            aq[8] = -BIG
            ak[8] = 1.0
            fl = np.zeros((128, 40), f)
            fl[:, 0:16] = (np.arange(16) % 2 == 1)
            fl[:, 16:32] = (np.arange(16) % 2 == 0)
            fl[:, 32] = 1.0
        else:
            b = c - 4
            d["x"] = np.ascontiguousarray(A["x_sample"][b])
            d["cond"] = _col8(A["c"][b])
            ss = A["state_ssd"][b]
            S = np.zeros((2, 2, 128, 512), f)
            for h in range(8):
                g = h // 4
                S[:, :, g * 64:(g + 1) * 64, h * 64:(h + 1) * 64] = ss[:, :, h]
            d["s_ssd"] = S
            sr = A["state_ret"][b]
            R = np.zeros((2, 2, 128, 512), f)
            for h in range(4):
                kt, lh = h // 2, h % 2
                R[:, :, lh * 64:(lh + 1) * 64, kt * 256 + lh * 128: kt * 256 + (lh + 1) * 128] = sr[:, :, h]
            d["s_ret"] = R
            d["ckT"] = np.ascontiguousarray(A["cache_att_k"][b].transpose(0, 2, 3, 1))
            d["cv"] = np.ascontiguousarray(A["cache_att_v"][b].reshape(2, 256, 128))
            d["cckvT"] = np.ascontiguousarray(A["cache_mla_ckv"][b].transpose(0, 2, 1))
            d["ckrT"] = np.ascontiguousarray(A["cache_mla_krope"][b].transpose(0, 2, 1))
            r64 = np.zeros((128, 2, T), f)
            r64[0:64, 0] = cos64; r64[64:128, 0] = cos64; r64[0:64, 1] = sin64; r64[64:128, 1] = sin64
            r32 = np.zeros((128, 2, T), f)
            r32[0:32, 0] = cos32; r32[64:96, 0] = cos32; r32[0:32, 1] = sin32; r32[64:96, 1] = sin32
            aq = np.zeros((9, T), f)
            ak = np.zeros((9, T + 256), f)
            fl = np.zeros((128, 40), f)
            fl[:, 0:32] = 1.0
        d["rope64"] = r64
        d["rope32"] = r32
        d["augq"] = aq
        d["augk"] = ak
        d["flags"] = fl
        percore.append(d)
    return shared, percore


def build_nc(dbg=None):
    nc = bass.Bass("TRN2", target_bir_lowering=False)
    es = ExitStack()
    with es:
        _build(nc, es, dbg)
    return nc


def _build(nc, es, dbg):
    dbg = dbg or {}
    P = Prog(nc, es)

    def din(name, shape):
        return nc.dram_tensor(name, list(shape), F32, kind="ExternalInput").ap()

    def dout(name, shape):
        return nc.dram_tensor(name, list(shape), F32, kind="ExternalOutput").ap()

    D = {}
    for nm, shp in [("x", (T, 1024)), ("cond", (128, 8)), ("s_ssd", (2, 2, 128, 512)), ("s_ret", (2, 2, 128, 512)),
                    ("ckT", (2, 2, 64, 256)), ("cv", (2, 256, 128)), ("cckvT", (2, 256, 256)), ("ckrT", (2, 32, 256)),
                    ("rope64", (128, 2, T)), ("rope32", (128, 2, T)), ("augq", (9, T)), ("augk", (9, T + 256)),
                    ("flags", (128, 40)),
                    ("win", (2, 128, 8, 5456)), ("wuq", (2, 128, 3, 1536)), ("wuk", (2, 128, 2, 512)),
                    ("wuv", (2, 128, 2, 512)), ("vecs", (2, 128, _NV)), ("wmod", (2, 128, 8, 6144)),
                    ("wmerge", (2, 128, 8, 4096)), ("wout", (2, 128, 8, 1024)), ("wup", (2, 128, 8, 5632)),
                    ("wdown", (2, 128, 22, 1024)), ("wbr01", (2, 4, 128, 4, 1024)),
                    ("consts", (128, _NC)), ("constb", (128, _NCB))]:
        D[nm] = din(nm, shp)
    O = {}
    for nm, shp in [("y", (T, 1024)), ("o_ssd", (2, 2, 8, 128, 512)), ("o_ret", (2, 2, 8, 128, 512)),
                    ("o_ck", (2, T, 128)), ("o_cv", (2, T, 128)), ("o_ckv", (2, T, 256)), ("o_kr", (2, T, 32))]:
        O[nm] = dout(nm, shp)
    xa = nc.dram_tensor("xa_scr", [T, 1024], F32, kind="Internal").ap()
    xb = nc.dram_tensor("xb_scr", [T, 1024], F32, kind="Internal").ap()

    _uid = [0]

    class Scope:
        def __init__(self):
            self.st = ExitStack()

        def __enter__(self):
            self.st.__enter__()
            return self

        def sb(self, name, shape, dt=F32):
            _uid[0] += 1
            return self.st.enter_context(nc.sbuf_tensor("s%d_%s" % (_uid[0], name), list(shape), dt))

        def __exit__(self, *a):
            P.flush()
            return self.st.__exit__(*a)

    psf = [es.enter_context(nc.psum_tensor("psf%d" % i, [128, 512], F32)) for i in range(7)]
    psb = es.enter_context(nc.psum_tensor("psb7", [128, 1024], BF16))
    pools = {}

    def setpools(**kw):
        pools.clear()
        for k, v in kw.items():
            pools[k] = [list(v), 0]

    def bank(pool):
        p = pools[pool]
        b = p[0][p[1] % len(p[0])]
        p[1] += 1
        return b

    def pk(b):
        return "ps%d" % b

    def act(out, in_, func, r, w, bias=None, scale=None, accum=None):
        kw = {}
        if bias is not None:
            kw["bias"] = bias
        if scale is not None:
            kw["scale"] = scale
        if accum is not None:
            kw["accum_out"] = accum
        return P.add("act", lambda e: e.activation(out, in_, func, **kw), r, w)

    def tt(eng, out, a, b, op, r, w):
        return P.add(eng, lambda e: e.tensor_tensor(out, a, b, op), r, w)

    def ts(eng, out, a, s1, op0, r, w, s2=None, op1=None):
        if op1 is None:
            return P.add(eng, lambda e: e.tensor_scalar(out, a, s1, None, op0), r, w)
        return P.add(eng, lambda e: e.tensor_scalar(out, a, s1, s2, op0, op1), r, w)

    def stt(eng, out, a, s, b, op0, op1, r, w):
        return P.add(eng, lambda e: e.scalar_tensor_tensor(out, a, s, b, op0, op1), r, w)

    def cp(eng, out, in_, r, w):
        return P.add(eng, lambda e: e.tensor_copy(out, in_), r, w)

    def recip(out, in_, r, w):
        return P.add("dve", lambda e: e.reciprocal(out, in_), r, w)

    def mset(eng, ap, val, w):
        return P.add(eng, lambda e: e.memset(ap, val), (), w)

    def mm(out, lhsT, rhs, st, sp, r, w):
        return P.add("pe", lambda e: e.matmul(out, lhsT, rhs, start=st, stop=sp), r, w)

    def tp(out, in_, ident, r, w):
        return P.add("pe", lambda e: e.transpose(out, in_, ident), r, w)

    def dma(eng, out, in_, r, w, slot):
        return P.add(eng, lambda e: e.dma_start(out=out, in_=in_), r, w, slot=slot)

    def rsum(out, in_, r, w):
        return P.add("dve", lambda e: e.reduce_sum(out, in_, AX.X), r, w)

    MUL, ADD, SUB = ALU.mult, ALU.add, ALU.subtract

    def psb_(name, shape, dt=F32):
        return es.enter_context(nc.sbuf_tensor("p_" + name, list(shape), dt))

    cf = psb_("cf", [128, _NC])
    cb = psb_("cb", [128, _NCB], BF16)
    vec = psb_("vec", [128, 2, _NV])
    flg = psb_("flg", [128, 40])
    hT = psb_("hT", [128, 8, T], BF16)
    modt = psb_("modt", [128, 2, 48])
    der = psb_("der", [128, 2, 48])
    junk = psb_("junk", [128, 1024])

    identf = cf[:, 0:128]
    triU = cf[:, 128:256]
    ntriS = cf[:, 256:384]
    nmU = cf[:, 384:512]
    nmL = cf[:, 512:640]
    onesf = cf[:, 640:768]
    identb = cb[:, 0:128]
    blk64 = cb[:, 128:256]
    o384 = cb[:, 256:384]
    o256 = cb[:, 384:512]
    triUb = cb[:, 640:768]
    ntriSb = cb[:, 768:896]
    nmUb = cb[:, 896:1024]
    nmLb = cb[:, 1024:1152]

    def V(l, name, n=None, rows=128):
        o = _VOFF[name]
        return vec[0:rows, l, o:o + (n if n is not None else 1)]

    dma("sp", cf[:], D["consts"][:, :], (), ["cf"], "cf")
    dma("pool", cb[:], D["constb"][:, :], (), ["cb"], "cb")
    dma("sp", vec[:, 0, :], D["vecs"][0], (), ["vec"], "vec0")
    dma("sp", vec[:, 1, :], D["vecs"][1], (), ["vec"], "vec1")
    dma("sp", flg[:], D["flags"][:, :], (), ["flg"], "flg")
    P.flush()

    def phase_mod(after):
        with Scope() as S:
            setpools(m=[0, 1])
            cond = S.sb("cond", [128, 8])
            scb = S.sb("scb", [128, 8], BF16)
            wm = [S.sb("wm%d" % i, [128, 8, 1536], BF16) for i in range(2)]
            dma("sp", cond[:], D["cond"][:, :], (), ["cond"], "cond")
            act(scb[:], cond[:], AF.Silu, ["cond"], ["scb"])
            it = 0
            for l in range(DEPTH):
                b = bank("m")
                for pc in range(4):
                    w_ = wm[it % 2]
                    wk = "wm%d" % (it % 2)
                    it += 1
                    dma("pool", w_[:], D["wmod"][l, :, :, pc * 1536:(pc + 1) * 1536], (), [wk], wk)
                    for nn in range(12):
                        n = pc * 12 + nn
                        for kc in range(8):
                            mm(psf[b][:, n:n + 1], w_[:, kc, nn * 128:(nn + 1) * 128], scb[:, kc:kc + 1], kc == 0, kc == 7,
                               [wk, "scb"], [pk(b)])
                tt("dve", modt[:, l, :], psf[b][:, 0:48], V(l, "b_mod", 48), ADD, [pk(b), "vec"], ["modt"])
                for (dst, gname, sc0, gt0, pname) in ((0, "g_pre_mix", 8, 16, "g_post_mix"), (8, "g_pre_ffn", 32, 40, "g_post_ffn")):
                    stt("dve", der[:, l, dst:dst + 8], modt[:, l, sc0:sc0 + 8], 1.0, V(l, gname, 8), ADD, MUL,
                        ["modt", "vec"], ["der"])
                    tt("dve", der[:, l, 16 + dst:24 + dst], modt[:, l, gt0:gt0 + 8], V(l, pname, 8), MUL,
                       ["modt", "vec"], ["der"])
            after()

    def make_row(l, col0, dst, S):
        for hh in range(2):
            b = bank("m")
            for k4 in range(4):
                kc = hh * 4 + k4
                mm(psf[b][:, k4 * 128:(k4 + 1) * 128], der[:, l, col0 + kc:col0 + kc + 1].to_broadcast([128, 128]),
                   identf, True, True, ["der", "cf"], [pk(b)])
            cp("dve", dst[:, hh * 512:(hh + 1) * 512], psf[b][:, :], [pk(b)], ["row"])

    def phase_norm(src, l, gcol, shcol):
        with Scope() as S:
            setpools(t=[0, 1, 2, 3])
            xt = S.sb("xt", [128, 8, 1024])
            st_ = S.sb("nst", [128, 8, 4])

            def s1(g):
                for j4 in range(4):
                    t = 4 * g + j4
                    j = t % 8
                    xk = "xt%d" % j
                    dma("sp", xt[:, j, :], src[t * 128:(t + 1) * 128, :], (), [xk], xk)
                    mset("dve", st_[:, j, 0:1], 0.0, ["st%d" % j])
                    act(junk[:, :], xt[:, j, :], AF.Square, [xk], ["junk", "st%d" % j], accum=st_[:, j, 0:1])
                    act(st_[:, j, 1:2], st_[:, j, 0:1], AF.Sqrt, ["st%d" % j], ["st%d" % j], bias=EPS, scale=1.0 / 1024)
                    recip(st_[:, j, 2:3], st_[:, j, 1:2], ["st%d" % j], ["st%d" % j])
                    ts("dve", xt[:, j, :], xt[:, j, :], st_[:, j, 2:3], MUL, [xk, "st%d" % j], [xk])

            def s2(g):
                for kc in range(8):
                    b = bank("t")
                    for j4 in range(4):
                        j = (4 * g + j4) % 8
                        tp(psf[b][:, j4 * 128:(j4 + 1) * 128], xt[:, j, kc * 128:(kc + 1) * 128], identf,
                           ["xt%d" % j, "cf"], [pk(b)])
                    if kc % 2 == 0:
                        act(hT[:, kc, g * 512:(g + 1) * 512], psf[b][:, :], AF.Identity, [pk(b), "der", "modt"],
                            ["h%d" % g], scale=gcol[:, kc:kc + 1], bias=shcol[:, kc:kc + 1])
                    else:
                        ts("dve", hT[:, kc, g * 512:(g + 1) * 512], psf[b][:, :], gcol[:, kc:kc + 1], MUL,
                           [pk(b), "der", "modt"], ["h%d" % g], s2=shcol[:, kc:kc + 1], op1=ADD)

            s1(0)
            for g in range(NG):
                if g + 1 < NG:
                    s1(g + 1)
                s2(g)

    WP = 256

    def load_w(Wt, src, ncols, order=None):
        npieces = (ncols + WP - 1) // WP
        for i in (order if order is not None else range(npieces)):
            c0, c1 = i * WP, min(ncols, (i + 1) * WP)
            dma("pool", Wt[:, :, c0:c1], src[:, :, c0:c1], (), ["W_%d" % i], "W_%d" % i)

    def wkeys(c0, n):
        return ["W_%d" % i for i in range(c0 // WP, (c0 + n - 1) // WP + 1)]

    def proj_fm(wt, wk, c0, M, g, pool):
        b = bank(pool)
        wks = wkeys(c0, M) if wk == "W" else [wk]
        for kc in range(8):
            mm(psf[b][0:M, :], wt[:, kc, c0:c0 + M], hT[:, kc, g * 512:(g + 1) * 512], kc == 0, kc == 7,
               wks + ["h%d" % g], [pk(b)])
        return b

    def proj_tm(wt, wk, c0, N, t, b, o0=0):
        wks = wkeys(c0, N) if wk == "W" else [wk]
        for kc in range(8):
            mm(psf[b][:, o0:o0 + N], hT[:, kc, t * 128:(t + 1) * 128], wt[:, kc, c0:c0 + N], kc == 0, kc == 7,
               wks + ["h%d" % (t // 4)], [pk(b)])

    def conv(raw, rk_, w0, w1, w2, bcol, nw0, nw2, y, yk):
        act(y[:, :], raw[:, 1:T + 1], AF.Identity, [rk_, "vec"], [yk], scale=w1, bias=bcol)
        stt("dve", y[:, :], raw[:, 0:T], w0, y[:, :], MUL, ADD, [rk_, yk, "vec"], [yk])
        stt("dve", y[:, :], raw[:, 2:T + 2], w2, y[:, :], MUL, ADD, [rk_, yk, "vec"], [yk])
        yv = y[:, :].rearrange("p (m s) -> p m s", s=256)
        rv_ = raw[:, 0:T].rearrange("p (m s) -> p m s", s=256)
        stt("dve", yv[:, 1:8, 0:1], rv_[:, 1:8, 0:1], nw0, yv[:, 1:8, 0:1], MUL, ADD, [rk_, yk, "nw"], [yk])
        rv2 = raw[:, 2:T + 2].rearrange("p (m s) -> p m s", s=256)
        stt("dve", yv[:, 0:7, 255:256], rv2[:, 0:7, 255:256], nw2, yv[:, 0:7, 255:256], MUL, ADD, [rk_, yk, "nw"], [yk])

    def scan(S, l, nh, nkt, dv, QT, KT, KTM, Vt, la, lnw, s_in, s_out, epilogue, const_decay=False):
        hkt = nh // nkt
        hph = hkt // 2
        n2, n4, n6 = 2 * nh, 4 * nh, 6 * nh
        sm = S.sb("sm", [128, NT, n4])
        bia = S.sb("bia", [128, NT, n2])
        ex = S.sb("ex", [128, NT, n6])
        ar = ex
        dcc = S.sb("dcc", [128, NT, n2])
        tE = S.sb("tE", [128, NT, nh])
        Sbf = S.sb("Sbf", [128, NT, 512], BF16)
        Sst = S.sb("Sst", [128, 2, 512])
        Sbb = S.sb("Sbb", [128, 2, 512], BF16)
        nb_ = 1 if const_decay else 2
        Dm2 = [S.sb("Dm%d" % i, [128, n2, 128]) for i in range(nb_)]
        LL2 = [S.sb("LL%d" % i, [128, nh, 128]) for i in range(nb_)]
        MT2 = [S.sb("MT%d" % i, [128, nh, 128], BF16) for i in range(2)]
        Oa2 = [S.sb("Oa%d" % i, [128, 512]) for i in range(2)]
        tmpo2 = [S.sb("tmpo%d" % i, [128, 512]) for i in range(2)]
        tmps = S.sb("tmps", [128, 512])
        tmps2 = [S.sb("tmq%d" % i, [128, 512]) for i in range(2)]
        Vw = S.sb("Vw", [128, 2, 512], BF16)

        la_hi = S.sb("la_hi", [128, NT, n2], BF16)
        la_lo = S.sb("la_lo", [128, NT, n2], BF16)
        cp("dve", la_hi[:, :, :], la[:, :, :], ["la"], ["lahl"])
        tt("dve", bia[:, :, :], la[:, :, :], la_hi[:, :, :], SUB, ["la", "lahl"], ["bia"])
        cp("dve", la_lo[:, :, :], bia[:, :, :], ["bia"], ["lahl"])
        setpools(row=[0])
        bA = bank("row")
        for c in range(NT):
            mm(psf[bA][:, c * n4:c * n4 + n2], triU, la[:, c, :], True, True, ["cf", "la"], [pk(bA)])
            mm(psf[bA][:, c * n4 + n2:(c + 1) * n4], onesf, la[:, c, :], True, True, ["cf", "la"], [pk(bA)])
        act(sm[:, :, :], psf[bA][:, 0:NT * n4].rearrange("p (c q) -> p c q", c=NT), AF.Copy, [pk(bA)], ["sm"])
        Af, Ab = sm[:, :, 0:nh], sm[:, :, nh:n2]
        tf_, tb_ = sm[:, :, n2:n2 + nh], sm[:, :, n2 + nh:n4]
        tt("dve", tE[:, :, :], Ab, la[:, :, nh:n2], SUB, ["sm", "la"], ["tE"])
        if lnw is not None:
            tt("dve", bia[:, :, 0:nh], lnw[:, :, 0:nh], Af, SUB, ["sm", "la"], ["bia"])
            tt("dve", bia[:, :, nh:n2], lnw[:, :, nh:n2], tE[:, :, :], ADD, ["tE", "la"], ["bia"])
        else:
            ts("dve", bia[:, :, 0:nh], Af, -1.0, MUL, ["sm"], ["bia"])
            cp("dve", bia[:, :, nh:n2], tE[:, :, :], ["tE"], ["bia"])
        cp("dve", ar[:, :, 0:nh], Af, ["sm"], ["ar0"])
        tt("dve", ar[:, :, nh:n2], tb_, tE[:, :, :], SUB, ["sm", "tE"], ["ar1"])
        tt("dve", ar[:, :, n2:n2 + nh], bia[:, :, 0:nh], tf_, ADD, ["bia", "sm"], ["ar2"])
        cp("dve", ar[:, :, n2 + nh:n4], bia[:, :, nh:n2], ["bia"], ["ar3"])
        cp("dve", ar[:, :, n4:n6], sm[:, :, n2:n4], ["sm"], ["ar4"])
        act(ex[:, :, :], ar[:, :, :], AF.Exp, ["ar0", "ar1", "ar2", "ar3", "ar4"], ["ex", "ar0", "ar1", "ar2", "ar3", "ar4"])
        tt("dve", dcc[:, :, 0:nh], ex[:, :, n4:n4 + nh], flg[:, 0:16].unsqueeze(2).to_broadcast([128, NT, nh]), MUL,
           ["ex", "flg"], ["dcc"])
        tt("dve", dcc[:, :, nh:n2], ex[:, :, n4 + nh:n6], flg[:, 16:32].unsqueeze(2).to_broadcast([128, NT, nh]), MUL,
           ["ex", "flg"], ["dcc"])

        def bc3(ap2):
            return ap2.unsqueeze(2).to_broadcast([128, nh, dv])

        def v3(ap):
            return ap.rearrange("p (h e) -> p h e", h=nh)

        w_ = hkt * dv

        def st_pre(c, wcol0):
            j = c % 2
            vk = "Vw%d" % j
            tt("dve", v3(Vw[:, j, :]), v3(Vt[:, c, :]), bc3(ex[:, c, wcol0:wcol0 + nh]), MUL, ["V", "ex"], [vk])
            b = bank("s")
            for kt in range(nkt):
                mm(psf[b][:, kt * w_:(kt + 1) * w_], KTM(kt, c), Vw[:, j, kt * w_:(kt + 1) * w_], True, True,
                   ["KTM", vk], [pk(b)])
            return b

        def st_post(c, d, cur, b, last_out):
            tt("dve", v3(tmps[:]), v3(Sst[:, cur, :]), bc3(dcc[:, c, d * nh:(d + 1) * nh]), MUL,
               ["S%d" % cur, "dcc"], ["tmps"])
            tt("dve", Sst[:, 1 - cur, :], tmps[:], psf[b][:, :], ADD, ["tmps", pk(b)], ["S%d" % (1 - cur)])
            if last_out is not None:
                dma("sp", last_out, Sst[:, 1 - cur, :], ["S%d" % (1 - cur)], [], "so%d" % (1 - cur))

        if dbg.get("ssd_stop") == 2:
            return
        setpools(s=[1, 2, 3])
        dma("sp", Sst[:, 0, :], s_in[0], (), ["S0"], "sin")
        cur = 0
        bn = st_pre(0, n2)
        for c in range(NT):
            bcur = bn
            if c + 1 < NT:
                bn = st_pre(c + 1, n2)
            act(Sbf[:, c, :], Sst[:, cur, :], AF.Copy, ["S%d" % cur, "flg"], ["Sbf%d" % c], scale=flg[:, c:c + 1])
            st_post(c, 0, cur, bcur, s_out[0, c // 2] if c % 2 == 1 else None)
            cur = 1 - cur
        if dbg.get("ssd_stop") == 3:
            return
        setpools(row=[0, 1], g=[2], o=[3], f=[4], bb=[5], s=[6])
        dma("sp", Sst[:, cur, :], s_in[1], (), ["S%d" % cur], "sin")
        state = {"cur": cur}

        def stageA(c):
            cur = state["cur"]
            sk = "Sbb%d" % (c % 2)
            act(Sbb[:, c % 2, :], Sst[:, cur, :], AF.Copy, ["S%d" % cur, "flg"], [sk], scale=flg[:, 16 + c:17 + c])
            bs_ = st_pre(c, n2 + nh)
            st_post(c, 1, cur, bs_, s_out[1, c // 2] if c % 2 == 0 else None)
            state["cur"] = 1 - cur
            return sk

        def stageR(c):
            pb = 0 if const_decay else c % 2
            Dm, LL = Dm2[pb], LL2[pb]
            dmk, llk = "Dm%d" % pb, "LL%d" % pb
            for q0 in range(0, n2, 4):
                b = bank("row")
                for q in range(q0, q0 + 4):
                    o_ = psf[b][:, (q - q0) * 128:(q - q0 + 1) * 128]
                    tr_ = triUb if q < nh else ntriSb
                    mm(o_, la_hi[:, c, q:q + 1].to_broadcast([128, 128]), tr_, True, False, ["lahl", "cb"], [pk(b)])
                    mm(o_, la_lo[:, c, q:q + 1].to_broadcast([128, 128]), tr_, False, False, ["lahl", "cb"], [pk(b)])
                    mm(o_, identb, nmUb if q < nh else nmLb, False, True, ["cb"], [pk(b)])
                tt("dve", Dm[:, q0:q0 + 4, :], psf[b][:, :].rearrange("p (q i) -> p q i", q=4),
                   bia[:, c, q0:q0 + 4].unsqueeze(2).to_broadcast([128, 4, 128]), ADD, [pk(b), "bia"], [dmk + "_%d" % q0])
            dks = [dmk + "_%d" % q0 for q0 in range(0, n2, 4)]
            act(Dm[:, :, :], Dm[:, :, :], AF.Exp, dks, dks)
            tt("dve", LL[:, :, :], Dm[:, 0:nh, :], Dm[:, nh:n2, :], ADD, dks, [llk])

        def stageA2(c, sk):
            pb = c % 2
            pl = 0 if const_decay else pb
            LL, MT, Oa, tmpo = LL2[pl], MT2[pb], Oa2[pb], tmpo2[pb]
            llk, mtk, oak, tmk = "LL%d" % pl, "MT%d" % pb, "Oa%d" % pb, "tmpo%d" % pb
            bg = bank("g")
            for kt in range(nkt):
                for hf in range(2):
                    i_ = kt * 2 + hf
                    mm(psf[bg][:, i_ * 128:(i_ + 1) * 128], KT(kt, c), QT(kt, hf, c), True, True, ["QK"], [pk(bg)])
            for kt in range(nkt):
                for hf in range(2):
                    i_ = kt * 2 + hf
                    h0 = kt * hkt + hf * hph
                    if hph > 1:
                        tt("dve", MT[:, h0:h0 + hph, :], LL[:, h0:h0 + hph, :],
                           psf[bg][:, i_ * 128:(i_ + 1) * 128].unsqueeze(1).to_broadcast([128, hph, 128]), MUL,
                           [llk, pk(bg)], [mtk + "_%d" % i_])
                    else:
                        tt("dve", MT[:, h0, :], LL[:, h0, :], psf[bg][:, i_ * 128:(i_ + 1) * 128], MUL,
                           [llk, pk(bg)], [mtk + "_%d" % i_])
            bo, bf_, bb_ = bank("o"), bank("f"), bank("bb")
            for h in range(nh):
                mm(psf[bo][:, h * dv:(h + 1) * dv], MT[:, h, :], Vt[:, c, h * dv:(h + 1) * dv], True, True,
                   [mtk + "_%d" % ((h // hkt) * 2 + (h % hkt) // hph), "V"], [pk(bo)])
            for kt in range(nkt):
                for hf in range(2):
                    c0 = kt * hkt * dv + hf * hph * dv
                    c1 = c0 + hph * dv
                    mm(psf[bf_][:, c0:c1], QT(kt, hf, c), Sbf[:, c, c0:c1], True, True, ["QK", "Sbf%d" % c], [pk(bf_)])
                    mm(psf[bb_][:, c0:c1], QT(kt, hf, c), Sbb[:, c % 2, c0:c1], True, True, ["QK", sk], [pk(bb_)])
            act(Oa[:], psf[bo][:, :], AF.Copy, [pk(bo)], [oak])
            tt("dve", v3(tmpo[:]), v3(psf[bf_][:, :]), bc3(ex[:, c, 0:nh]), MUL, [pk(bf_), "ex"], [tmk])
            tt("dve", v3(tmps2[pb][:]), v3(psf[bb_][:, :]), bc3(ex[:, c, nh:n2]), MUL, [pk(bb_), "ex"], ["tq%d" % pb])
            return (Oa, tmpo, oak, tmk, pb)

        def stageB(c, ctx):
            Oa, tmpo, oak, tmk, pb = ctx
            tt("dve", tmpo[:], tmpo[:], tmps2[pb][:], ADD, [tmk, "tq%d" % pb], [tmk])
            tt("dve", Oa[:], Oa[:], tmpo[:], ADD, [oak, tmk], [oak])
            epilogue(c, Oa, tmpo, oak, tmk)

        prev = None
        stageR(NT - 1)
        if not const_decay:
            stageR(NT - 2)
        for c in range(NT - 1, -1, -1):
            sk = stageA(c)
            ctx = stageA2(c, sk)
            if c >= 2 and not const_decay:
                stageR(c - 2)
            if prev is not None:
                stageB(*prev)
            prev = (c, ctx)
        stageB(*prev)

    def to_fm(S, c, oTM, ok, dst, dk):
        for k in range(4):
            tp(psb[:, k * 128:(k + 1) * 128], oTM[:, k * 128:(k + 1) * 128], identb, [ok, "cb"], ["psb"])
        cp("dve", dst[:, 0:4, c * 128:(c + 1) * 128], psb[:, 0:512].rearrange("p (k t) -> p k t", k=4), ["psb"], [dk])

    def phase_ssd(l, o_ssd):
        with Scope() as S1:
            xsTM = S1.sb("xsTM", [128, NT, 512], BF16)
            BT = S1.sb("BT", [128, T], BF16)
            CT = [S1.sb("CT%d" % i, [128, T], BF16) for i in range(2)]
            mset("pool", CT[0][64:128, :], 0.0, ["QK"])
            mset("pool", CT[1][0:64, :], 0.0, ["QK"])
            BTM = S1.sb("BTM", [128, NT, 128], BF16)
            zs = S1.sb("zs", [128, NT, 512], BF16)
            la = S1.sb("la", [128, NT, 16])
            lnw = S1.sb("lnw", [128, NT, 16])
            nw = S1.sb("nw", [128, 12])
            with Scope() as S:
                setpools(a=[0, 1, 2, 3], t=[4, 5])
                W = S.sb("W", [128, 8, 1296], BF16)
                raw = [S.sb("raw%d" % i, [128, T + 2]) for i in range(2)]
                ycv = S.sb("ycv", [128, T])
                xact = S.sb("xact", [128, T], BF16)
                dtr = S.sb("dtr", [128, NT, 16])
                ea = S.sb("ea", [128, 16])
                load_w(W, D["win"][l, :, :, 0:1296], 1296, order=[2, 0, 1, 3, 4, 5])
                for i in range(2):
                    mset("pool", raw[i][:, 0:1], 0.0, ["raw%d" % i])
                    mset("pool", raw[i][:, T + 1:T + 2], 0.0, ["raw%d" % i])
                stt("dve", nw[:, 0:6], V(l, "ssd_cw", 6), -1.0, flg[:, 32:33].to_broadcast([128, 6]), MUL, MUL,
                    ["vec", "flg"], ["nw"])
                stt("dve", nw[:, 6:12], vec[:, l, _VOFF["ssd_cw"] + 12:_VOFF["ssd_cw"] + 18], -1.0,
                    flg[:, 32:33].to_broadcast([128, 6]), MUL, MUL, ["vec", "flg"], ["nw"])
                cw0 = _VOFF["ssd_cw"]
                zq = []
                for ch in range(6):
                    rw = raw[ch % 2]
                    rk_ = "raw%d" % (ch % 2)
                    for g in range(NG):
                        b = proj_fm(W, "W", 512 + ch * 128, 128, g, "a")
                        act(rw[:, 1 + g * 512:1 + (g + 1) * 512], psf[b][:, :], AF.Copy, [pk(b)], [rk_])
                    conv(rw, rk_, vec[:, l, cw0 + ch:cw0 + ch + 1], vec[:, l, cw0 + 6 + ch:cw0 + 7 + ch],
                         vec[:, l, cw0 + 12 + ch:cw0 + 13 + ch], V(l, "ssd_cb", 6)[:, ch:ch + 1],
                         nw[:, ch:ch + 1], nw[:, 6 + ch:7 + ch], ycv, "ycv")
                    for t in range(ch * 3, min(NT, ch * 3 + 3)):
                        bz = bank("a")
                        proj_tm(W, "W", 0, 512, t, bz)
                        zq.append((t, bz))
                    dstx = xact if ch < 4 else BT
                    dk = "xact" if ch < 4 else ("QK")
                    if ch < 5:
                        act(dstx[:, :], ycv[:, :], AF.Silu, ["ycv"], [dk])
                    else:
                        act(CT[0][0:64, :], ycv[0:64, :], AF.Silu, ["ycv"], [dk])
                        act(CT[1][64:128, :], ycv[64:128, :], AF.Silu, ["ycv"], [dk])
                    while zq:
                        t_, bz_ = zq.pop(0)
                        act(zs[:, t_, :], psf[bz_][:, :], AF.Silu, [pk(bz_)], ["zs"])
                    if ch < 5:
                        for t0 in range(0, NT, 8):
                            for j in range(8):
                                tp(psb[:, j * 128:(j + 1) * 128], dstx[:, (t0 + j) * 128:(t0 + j + 1) * 128], identb,
                                   [dk, "cb"], ["psb"])
                            if ch < 4:
                                cp("dve", xsTM[:, t0:t0 + 8, ch * 128:(ch + 1) * 128],
                                   psb[:, :].rearrange("p (j t) -> p j t", j=8), ["psb"], ["V"])
                            else:
                                cp("dve", BTM[:, t0:t0 + 8, :], psb[:, :].rearrange("p (j t) -> p j t", j=8),
                                   ["psb"], ["KTM"])
                b = bank("t")
                for t in range(NT):
                    proj_tm(W, "W", 1280, 16, t, b, o0=t * 16)
                tt("dve", dtr[:, :, :], psf[b][:, 0:256].rearrange("p (t q) -> p t q", t=NT),
                   V(l, "dt_bias", 16).unsqueeze(1).to_broadcast([128, NT, 16]), ADD, [pk(b), "vec"], ["dtr"])
                act(dtr[:, :, :], dtr[:, :, :], AF.Exp, ["dtr"], ["dtr"])
                act(dtr[:, :, :], dtr[:, :, :], AF.Ln, ["dtr"], ["dtr"], bias=1.0)
                act(lnw[:, :, :], dtr[:, :, :], AF.Ln, ["dtr"], ["la"])
                act(ea[:], V(l, "a_log", 16), AF.Exp, ["vec"], ["ea"])
                stt("dve", la[:, :, :], dtr[:, :, :], -1.0, ea[:].unsqueeze(1).to_broadcast([128, NT, 16]), MUL, MUL,
                    ["dtr", "ea"], ["la"])
                for t in range(18, NT):
                    b = bank("a")
                    proj_tm(W, "W", 0, 512, t, b)
                    act(zs[:, t, :], psf[b][:, :], AF.Silu, [pk(b)], ["zs"])
            if dbg.get("ssd_stop") == 1:
                return
            with Scope() as S:
                st2 = S.sb("st2", [128, 4])
                oTM2 = [S.sb("oTM%d" % i, [128, 512], BF16) for i in range(2)]

                def epi(c, Oa, tmpo, oak, tmk):
                    oTM, otk = oTM2[c % 2], "oTM%d" % (c % 2)
                    tt("dve", tmpo[:].rearrange("p (h e) -> p h e", h=8), xsTM[:, c, :].rearrange("p (h e) -> p h e", h=8),
                       V(l, "ssd_d", 8).unsqueeze(2).to_broadcast([128, 8, 64]), MUL, ["V", "vec"], [tmk])
                    tt("dve", Oa[:], Oa[:], tmpo[:], ADD, [oak, tmk], [oak])
                    tt("dve", Oa[:], Oa[:], zs[:, c, :], MUL, [oak, "zs"], [oak])
                    mset("dve", st2[:, 0:1], 0.0, ["st2"])
                    act(junk[:, 0:512], Oa[:], AF.Square, [oak], ["junk", "st2"], accum=st2[:, 0:1])
                    act(st2[:, 1:2], st2[:, 0:1], AF.Ln, ["st2"], ["st2"], bias=EPS, scale=1.0 / 512)
                    act(st2[:, 2:3], st2[:, 1:2], AF.Exp, ["st2"], ["st2"], scale=-0.5)
                    stt("dve", oTM[:], Oa[:], st2[:, 2:3], V(l, "ssd_nw", 512), MUL, MUL, [oak, "st2", "vec"], [otk])
                    to_fm(S, c, oTM, otk, o_ssd, "o_ssd")
                scan(S, l, 8, 1, 64,
                     lambda kt, hf, c: CT[hf][:, c * 128:(c + 1) * 128],
                     lambda kt, c: BT[:, c * 128:(c + 1) * 128],
                     lambda kt, c: BTM[:, c, :], xsTM, la, lnw, D["s_ssd"][l], O["o_ssd"][l], epi)

    def rope_fm(S, b_x, b_p, rows, g, rope, dst, p, dk, pre=1.0, tmp=None):
        t1, t2 = tmp
        act(t1[rows, :], psf[b_x][rows, :], AF.Copy, [pk(b_x)], ["t1"], scale=pre)
        act(t2[rows, :], psf[b_p][rows, :], AF.Copy, [pk(b_p)], ["t2"], scale=pre)
        tt("dve", t1[rows, :], t1[rows, :], rope[rows, 0, g * 512:(g + 1) * 512], MUL, ["t1", "rope"], ["t1"])
        tt("pool", t2[rows, :], t2[rows, :], rope[rows, 1, g * 512:(g + 1) * 512], MUL, ["t2", "rope"], ["t2"])
        gs = slice(g * 512, (g + 1) * 512)
        if isinstance(dst, list):
            tt("dve", dst[0][0:64, p, gs], t1[0:64, :], t2[0:64, :], ADD, ["t1", "t2"], [dk])
            tt("dve", dst[1][64:128, p, gs], t1[64:128, :], t2[64:128, :], ADD, ["t1", "t2"], [dk])
        else:
            tt("dve", dst[:, p, gs], t1[rows, :], t2[rows, :], ADD, ["t1", "t2"], [dk])

    def phase_ret(l, o_ret):
        with Scope() as S1:
            rqT = [S1.sb("rqT%d" % i, [128, 2, T], BF16) for i in range(2)]
            mset("pool", rqT[0][64:128, :, :], 0.0, ["QK"])
            mset("pool", rqT[1][0:64, :, :], 0.0, ["QK"])
            rkT = S1.sb("rkT", [128, 2, T], BF16)
            rkTM = S1.sb("rkTM", [128, NT, 256], BF16)
            rvTM = S1.sb("rvTM", [128, NT, 512], BF16)
            rgs = S1.sb("rgs", [128, NT, 512], BF16)
            la = S1.sb("la", [128, NT, 8])
            with Scope() as S:
                setpools(a=[0, 1, 2, 3], t=[4, 5])
                W = S.sb("W", [128, 8, 1024], BF16)
                rope = S.sb("rope", [128, 2, T], BF16)
                t1 = S.sb("t1", [128, 512])
                t2 = S.sb("t2", [128, 512])
                l8 = S.sb("l8", [128, 8])
                dma("pool", rope[:], D["rope64"][:, :, :], (), ["rope"], "rope")
                load_w(W, D["win"][l, :, :, 1296:2320], 1024)
                Wb = S.sb("Wb", [128, 8, 1024], BF16)
                for i_ in range(2):
                    dma("pool", Wb[:, :, i_ * 512:(i_ + 1) * 512], D["win"][l, :, :, 2320 + i_ * 512:2320 + (i_ + 1) * 512],
                        (), ["Wb%d" % i_], "Wb%d" % i_)
                act(l8[:], V(l, "ret_logit", 8), AF.Exp, ["vec"], ["l8"], scale=-1.0)
                act(l8[:], l8[:], AF.Ln, ["l8"], ["l8"], bias=1.0)
                ts("dve", la[:, :, :], l8[:].unsqueeze(1).to_broadcast([128, NT, 8]), -1.0, MUL, ["l8"], ["la"])
                for (c0, dst, pre) in ((0, rqT, 1.0), (512, rkT, 0.125)):
                    for p in range(2):
                        for g in range(NG):
                            bx = proj_fm(W, "W", c0 + p * 128, 128, g, "a")
                            bp = proj_fm(W, "W", c0 + 256 + p * 128, 128, g, "a")
                            rope_fm(S, bx, bp, slice(0, 128), g, rope, dst, p, "QK", pre, (t1, t2))
                for p in range(2):
                    for t0 in range(0, NT, 8):
                        for j in range(8):
                            tp(psb[:, j * 128:(j + 1) * 128], rkT[:, p, (t0 + j) * 128:(t0 + j + 1) * 128], identb,
                               ["QK", "cb"], ["psb"])
                        cp("dve", rkTM[:, t0:t0 + 8, p * 128:(p + 1) * 128], psb[:, :].rearrange("p (j t) -> p j t", j=8),
                           ["psb"], ["KTM"])
                for t in range(NT):
                    b = bank("a")
                    proj_tm(Wb, "Wb0", 0, 512, t, b)
                    cp("dve", rvTM[:, t, :], psf[b][:, :], [pk(b)], ["V"])
                    b = bank("a")
                    proj_tm(Wb, "Wb1", 512, 512, t, b)
                    act(rgs[:, t, :], psf[b][:, :], AF.Silu, [pk(b)], ["rgs"])
            with Scope() as S:
                st4 = S.sb("st4", [128, 4, 4])
                oTM2 = [S.sb("oTM%d" % i, [128, 512], BF16) for i in range(2)]

                def epi(c, Oa, tmpo, oak, tmk):
                    oTM, otk = oTM2[c % 2], "oTM%d" % (c % 2)
                    o3 = Oa[:].rearrange("p (h e) -> p h e", h=4)
                    rsum(st4[:, 0, :], o3, [oak], ["st4"])
                    act(junk[:, 0:512], Oa[:], AF.Square, [oak], ["junk"])
                    rsum(st4[:, 1, :], junk[:, 0:512].rearrange("p (h e) -> p h e", h=4), ["junk"], ["st4"])
                    ts("dve", st4[:, 0, :], st4[:, 0, :], 1.0 / 128, MUL, ["st4"], ["st4"])
                    tt("dve", st4[:, 2, :], st4[:, 0, :], st4[:, 0, :], MUL, ["st4"], ["st4"])
                    stt("dve", st4[:, 1, :], st4[:, 1, :], 1.0 / 128, st4[:, 2, :], MUL, SUB, ["st4"], ["st4"])
                    act(st4[:, 2, :], st4[:, 1, :], AF.Ln, ["st4"], ["st4"], bias=EPS)
                    act(st4[:, 3, :], st4[:, 2, :], AF.Exp, ["st4"], ["st4"], scale=-0.5)
                    tt("dve", o3, o3, st4[:, 0, :].unsqueeze(2).to_broadcast([128, 4, 128]), SUB, [oak, "st4"], [oak])
                    tt("dve", o3, o3, st4[:, 3, :].unsqueeze(2).to_broadcast([128, 4, 128]), MUL, [oak, "st4"], [oak])
                    tt("dve", Oa[:], Oa[:], V(l, "ret_gw", 512), MUL, [oak, "vec"], [oak])
                    tt("dve", oTM[:], Oa[:], rgs[:, c, :], MUL, [oak, "rgs"], [otk])
                    to_fm(S, c, oTM, otk, o_ret, "o_ret")
                scan(S, l, 4, 2, 128,
                     lambda kt, hf, c: rqT[hf][:, kt, c * 128:(c + 1) * 128],
                     lambda kt, c: rkT[:, kt, c * 128:(c + 1) * 128],
                     lambda kt, c: rkTM[:, c, kt * 128:(kt + 1) * 128], rvTM, la, None,
                     D["s_ret"][l], O["o_ret"][l], epi, const_decay=True)

    def attention(S, nheads, krows, scale, QTf, KTf, Vf, dst, par):
        setpools(s=[0, 1, 2, 3], o=[4, 5], n=[6])
        NPT = 7
        DEPTH_ = 5
        PT = [S.sb("PT%d" % i, [128, 512], BF16) for i in range(NPT)]
        Osb2 = [S.sb("Osb%d" % i, [128, 512]) for i in range(2)]
        rec2 = [S.sb("rec%d" % i, [128, 512]) for i in range(2)]
        seq = [(h, qg, kb) for h in range(nheads) for qg in range(NG) for kb in range(18)]
        cur_o = {}
        fin_q = []
        FIN_DELAY = 10

        def emit_pv(n):
            h, qg, kb = seq[n]
            odd = par(h)
            if kb == 0:
                cur_o[(h, qg)] = bank("o")
            bo = cur_o[(h, qg)]
            pt, ptk = PT[n % NPT], "PT%d" % (n % NPT)
            v_ = Vf(h, kb)
            if odd:
                mm(psf[bo][0:128, :], v_[:, 64:192], pt[:], kb == 0, kb == 17, ["Vx", ptk], [pk(bo)])
            else:
                mm(psf[bo][0:65, :], v_[:, 0:65], pt[:], kb == 0, kb == 17, ["Vx", ptk], [pk(bo)])
            if kb < 17:
                return
            pi = (h * NG + qg) % 2
            rec, Osb = rec2[pi], Osb2[pi]
            rk_, ok_ = "rec%d" % pi, "Osb%d" % pi
            if odd:
                recip(rec[0:1, :], psf[bo][0:1, :], [pk(bo)], [rk_])
                R_ = slice(64, 128)
            else:
                recip(rec[64:65, :], psf[bo][64:65, :], [pk(bo)], [rk_])
                R_ = slice(0, 64)
            act(Osb[R_, :], psf[bo][R_, :], AF.Copy, [pk(bo)], [ok_])

            def fin():
                bn = bank("n")
                if odd:
                    mm(psf[bn][0:128, :], onesf[0:1, 0:128], rec[0:1, :], True, True, [rk_, "cf"], [pk(bn)])
                else:
                    mm(psf[bn][0:64, :], onesf[64:65, 0:64], rec[64:65, :], True, True, [rk_, "cf"], [pk(bn)])
                tt("dve", dst(h, qg), Osb[R_, :], psf[bn][R_, :], MUL, [ok_, pk(bn)], ["oatt"])
            fin_q.append((n + FIN_DELAY, fin))

        for n, (h, qg, kb) in enumerate(seq):
            q_, k_ = QTf(h), KTf(h)
            bs = bank("s")
            mm(psf[bs][:, :], k_[0:krows, kb * 128:(kb + 1) * 128], q_[0:krows, qg * 512:(qg + 1) * 512], True, True,
               ["Q", "K", "Qaug", "Kaug"], [pk(bs)])
            act(PT[n % NPT][:], psf[bs][:, :], AF.Exp, [pk(bs)], ["PT%d" % (n % NPT)], scale=scale)
            if n >= DEPTH_:
                emit_pv(n - DEPTH_)
                while fin_q and fin_q[0][0] <= n - DEPTH_:
                    fin_q.pop(0)[1]()
        for n in range(max(0, len(seq) - DEPTH_), len(seq)):
            emit_pv(n)
        while fin_q:
            fin_q.pop(0)[1]()

    def headnorm(S, b_x, rows, ones_ap, nacc, tmp):
        sq, rs = tmp
        bm = bank("n")
        for i, b in enumerate(b_x):
            act(sq[rows, i, :], psf[b][rows, :], AF.Square, [pk(b)], ["sq"])
        for i, b in enumerate(b_x):
            mm(psf[bm][rows, :], ones_ap, sq[rows, i, :], i == 0, i == len(b_x) - 1, ["sq", "cb"], [pk(bm)])
        act(rs[rows, :], psf[bm][rows, :], AF.Ln, [pk(bm)], ["rs"], bias=EPS)
        act(rs[rows, :], rs[rows, :], AF.Exp, ["rs"], ["rs"], scale=-0.5)

    def phase_gqa(l, o_att):
      for hb in range(2):
        with Scope() as S1:
            QTb = S1.sb("QTb", [128, 4, T], BF16)
            KTb = S1.sb("KTb", [128, T + 256], BF16)
            Vx = S1.sb("Vx", [128, 18, 192], BF16)
            with Scope() as S:
                setpools(a=[0, 1, 2, 3], n=[4], t=[5, 6])
                W = S.sb("W", [128, 8, 1408], BF16)
                rope = S.sb("rope", [128, 2, T], BF16)
                t1 = S.sb("t1", [128, 512]); t2 = S.sb("t2", [128, 512])
                sq = S.sb("sq", [128, 1, 512], BF16); rs = S.sb("rs", [128, 512])
                kst = S.sb("kst", [128, NT, 64]); vst = S.sb("vst", [128, NT, 64])
                dma("pool", rope[:], D["rope64"][:, :, :], (), ["rope"], "rope")
                load_w(W, D["win"][l, :, :, 3344:4752], 1408, order=[hb, 2 + hb, 4, 5])
                for j in range(4):
                    dma("pool", QTb[64:73, j, :], D["augq"][:, :], (), ["Qaug"], "aug")
                dma("pool", KTb[64:73, :], D["augk"][:, :], (), ["Kaug"], "aug")
                dma("pool", KTb[0:64, 0:256], D["ckT"][l, hb], (), ["Kaug"], "aug")
                mset("pool", Vx[:, :, 64:65], 1.0, ["Vx"])
                mset("pool", Vx[:, :, 65:128], 0.0, ["Vx"])
                dma("pool", Vx[:, 0:2, 0:64], D["cv"][l][:, hb * 64:(hb + 1) * 64].rearrange("(kb p) d -> p kb d", p=128),
                    (), ["Vx"], "aug")
                dma("pool", Vx[:, 0:2, 128:192], D["cv"][l][:, hb * 64:(hb + 1) * 64].rearrange("(kb p) d -> p kb d", p=128),
                    (), ["Vx"], "aug")
                P.seal("aug")
                R = slice(0, 64)
                for (isk, hl, c0, cp0, nname, npname) in [(False, j, 0, 512, "qn", "qnp") for j in range(4)] + \
                        [(True, 0, 1024, 1152, "kn", "knp")]:
                    hg = (hb * 4 + hl) if not isk else hb
                    for g in range(NG):
                        bx = proj_fm(W, "W", c0 + hg * 64, 64, g, "a")
                        bp = proj_fm(W, "W", cp0 + hg * 64, 64, g, "a")
                        headnorm(S, [bx], R, blk64[0:64, 0:64], 1, (sq, rs))
                        act(t1[R, :], psf[bx][R, :], AF.Copy, [pk(bx), "vec"], ["t1"], scale=V(l, nname, 1, 64))
                        act(t2[R, :], psf[bp][R, :], AF.Copy, [pk(bp), "vec"], ["t2"], scale=V(l, npname, 1, 64))
                        tt("dve", t1[R, :], t1[R, :], rs[R, :], MUL, ["t1", "rs"], ["t1"])
                        tt("dve", t2[R, :], t2[R, :], rs[R, :], MUL, ["t2", "rs"], ["t2"])
                        if isk:
                            bt = bank("t")
                            for j in range(4):
                                tp(psf[bt][:, j * 64:(j + 1) * 64], t1[R, j * 128:(j + 1) * 128], identf[0:64, 0:64],
                                   ["t1", "cf"], [pk(bt)])
                            cp("dve", kst[:, g * 4:(g + 1) * 4, :], psf[bt][:, 0:256].rearrange("p (j d) -> p j d", j=4),
                               [pk(bt)], ["kst"])
                        tt("dve", t1[R, :], t1[R, :], rope[R, 0, g * 512:(g + 1) * 512], MUL, ["t1", "rope"], ["t1"])
                        tt("pool", t2[R, :], t2[R, :], rope[R, 1, g * 512:(g + 1) * 512], MUL, ["t2", "rope"], ["t2"])
                        if isk:
                            tt("dve", KTb[R, 256 + g * 512:256 + (g + 1) * 512], t1[R, :], t2[R, :], ADD, ["t1", "t2"], ["K"])
                        else:
                            tt("dve", QTb[R, hl, g * 512:(g + 1) * 512], t1[R, :], t2[R, :], ADD, ["t1", "t2"], ["Q"])
                dma("sp", O["o_ck"][l][:, hb * 64:(hb + 1) * 64].rearrange("(t p) f -> p t f", p=128), kst[:, :, :],
                    ["kst"], [], "kst")
                for t in range(NT):
                    b = bank("a")
                    proj_tm(W, "W", 1280 + hb * 64, 64, t, b)
                    act(vst[:, t, :], psf[b][:, 0:64], AF.Copy, [pk(b)], ["vst"])
                    cp("dve", Vx[:, 2 + t, 0:64], psf[b][:, 0:64], [pk(b)], ["Vx"])
                    cp("dve", Vx[:, 2 + t, 128:192], psf[b][:, 0:64], [pk(b)], ["Vx"])
                dma("sp", O["o_cv"][l][:, hb * 64:(hb + 1) * 64].rearrange("(t p) f -> p t f", p=128), vst[:, :, :],
                    ["vst"], [], "vst")
            with Scope() as S:
                attention(S, 4, 73, 0.125, lambda j: QTb[:, j, :], lambda j: KTb[:, :],
                          lambda j, kb: Vx[:, kb, :],
                          lambda j, qg: o_att[(j % 2) * 64:(j % 2) * 64 + 64, (hb * 4 + j) // 2, qg * 512:(qg + 1) * 512],
                          lambda j: j % 2)

    def phase_mla(l, o_mla):
        with Scope() as S1:
            mcqn = S1.sb("mcqn", [128, 3, T], BF16)
            ckvT = S1.sb("ckvT", [128, 2, T + 256], BF16)
            krT = S1.sb("krT", [128, T + 256], BF16)
            rope = S1.sb("rope", [128, 2, T], BF16)
            wuq = S1.sb("wuq", [128, 3, 1536], BF16)
            wuk = S1.sb("wuk", [128, 2, 512], BF16)
            wuv = S1.sb("wuv", [128, 2, 512], BF16)
            dma("pool", rope[:], D["rope32"][:, :, :], (), ["rope"], "rope")
            dma("pool", wuq[:], D["wuq"][l], (), ["wu"], "wuq")
            dma("pool", wuk[:], D["wuk"][l], (), ["wu"], "wuk")
            dma("pool", wuv[:], D["wuv"][l], (), ["wu"], "wuv")
            for c in range(2):
                dma("pool", ckvT[:, c, 0:256], D["cckvT"][l, c * 128:(c + 1) * 128, :], (), ["ckvT"], "ck%d" % c)
            dma("pool", krT[0:32, 0:256], D["ckrT"][l], (), ["krT"], "ckr")
            with Scope() as S:
                setpools(a=[0, 1, 2], n=[3], t=[4, 5], x=[6])
                W = S.sb("W", [128, 8, 704], BF16)
                t1 = S.sb("t1", [128, 512]); t2 = S.sb("t2", [128, 512])
                sq = S.sb("sq", [128, 3, 512], BF16); rs = S.sb("rs", [128, 512])
                cst = S.sb("cst", [128, 4, 256]); kst = S.sb("kst", [128, NT, 32])
                load_w(W, D["win"][l, :, :, 4752:5456], 704)
                A_ = slice(0, 128)
                for g in range(NG):
                    bs = [proj_fm(W, "W", c * 128, 128, g, "a") for c in range(3)]
                    headnorm(S, bs, A_, o384, 3, (sq, rs))
                    for c in range(3):
                        stt("dve", mcqn[:, c, g * 512:(g + 1) * 512], psf[bs[c]][:, :], V(l, "mqn", 3)[:, c:c + 1], rs[:, :],
                            MUL, MUL, [pk(bs[c]), "rs", "vec"], ["mcqn"])
                    bs = [proj_fm(W, "W", 384 + c * 128, 128, g, "a") for c in range(2)]
                    headnorm(S, bs, A_, o256, 2, (sq, rs))
                    for c in range(2):
                        stt("dve", t1[:, :], psf[bs[c]][:, :], V(l, "mkvn", 2)[:, c:c + 1], rs[:, :], MUL, MUL,
                            [pk(bs[c]), "rs", "vec"], ["t1"])
                        cp("pool", ckvT[:, c, 256 + g * 512:256 + (g + 1) * 512], t1[:, :], ["t1"], ["ckvT"])
                        bt = bank("t")
                        for j in range(4):
                            tp(psf[bt][:, j * 128:(j + 1) * 128], t1[:, j * 128:(j + 1) * 128], identf, ["t1", "cf"], [pk(bt)])
                        cp("dve", cst[:, :, c * 128:(c + 1) * 128], psf[bt][:, :].rearrange("p (j d) -> p j d", j=4),
                           [pk(bt)], ["cst"])
                    dma("sp", O["o_ckv"][l, g * 512:(g + 1) * 512, :].rearrange("(t p) f -> p t f", p=128), cst[:, :, :],
                        ["cst"], [], "cst")
                    R = slice(0, 32)
                    bx = proj_fm(W, "W", 640, 32, g, "x")
                    act(t1[R, :], psf[bx][R, :], AF.Copy, [pk(bx)], ["t1"])
                    bp = proj_fm(W, "W", 672, 32, g, "x")
                    act(t2[R, :], psf[bp][R, :], AF.Copy, [pk(bp)], ["t2"])
                    bt = bank("t")
                    for j in range(4):
                        tp(psf[bt][:, j * 32:(j + 1) * 32], t1[R, j * 128:(j + 1) * 128], identf[0:32, 0:32], ["t1", "cf"], [pk(bt)])
                    cp("dve", kst[:, g * 4:(g + 1) * 4, :], psf[bt][:, 0:128].rearrange("p (j d) -> p j d", j=4), [pk(bt)], ["kst"])
                    tt("dve", t1[R, :], t1[R, :], rope[R, 0, g * 512:(g + 1) * 512], MUL, ["t1", "rope"], ["t1"])
                    tt("pool", t2[R, :], t2[R, :], rope[R, 1, g * 512:(g + 1) * 512], MUL, ["t2", "rope"], ["t2"])
                    tt("dve", krT[R, 256 + g * 512:256 + (g + 1) * 512], t1[R, :], t2[R, :], ADD, ["t1", "t2"], ["krT"])
                dma("sp", O["o_kr"][l].rearrange("(t p) f -> p t f", p=128), kst[:, :, :], ["kst"], [], "kst")
            for hb in range(4):
                with Scope() as S2:
                    QM = S2.sb("QM", [128, 2, T], BF16)
                    KM = S2.sb("KM", [128, 2, T + 256], BF16)
                    VM = S2.sb("VM", [128, 18, 2, 192], BF16)
                    if True:
                        S = S2
                        setpools(a=[0, 1, 2, 3], b=[4, 5])
                        t1 = S.sb("t1", [128, 512]); t2 = S.sb("t2", [128, 512])
                        mset("pool", VM[:, :, :, 64:65], 1.0, ["Vx"])
                        mset("pool", VM[:, :, :, 65:128], 0.0, ["Vx"])
                        for j in range(2):
                            dma("pool", QM[96:105, j, :], D["augq"][:, :], (), ["Qaug"], "aug")
                            dma("pool", KM[96:105, j, :], D["augk"][:, :], (), ["Kaug"], "aug")
                        P.seal("aug")
                        for j in range(2):
                            h = hb * 2 + j
                            dma("sp", KM[64:96, j, :], krT[0:32, :], ["krT"], ["Kaug"], "krc%d" % j)
                            for kg in range(5):
                                n_ = 512 if kg < 4 else 256
                                b = bank("a")
                                for c in range(2):
                                    mm(psf[b][0:64, 0:n_], wuk[:, c, h * 64:(h + 1) * 64], ckvT[:, c, kg * 512:kg * 512 + n_],
                                       c == 0, c == 1, ["wu", "ckvT"], [pk(b)])
                                act(KM[0:64, j, kg * 512:kg * 512 + n_], psf[b][0:64, 0:n_], AF.Copy, [pk(b)], ["K"])
                            for g in range(NG):
                                ba = bank("a")
                                bb2 = bank("b")
                                for c in range(3):
                                    mm(psf[ba][0:96, :], wuq[:, c, h * 192:h * 192 + 96], mcqn[:, c, g * 512:(g + 1) * 512],
                                       c == 0, c == 2, ["wu", "mcqn"], [pk(ba)])
                                for c in range(3):
                                    mm(psf[bb2][0:96, :], wuq[:, c, h * 192 + 96:h * 192 + 192], mcqn[:, c, g * 512:(g + 1) * 512],
                                       c == 0, c == 2, ["wu", "mcqn"], [pk(bb2)])
                                act(QM[0:64, j, g * 512:(g + 1) * 512], psf[ba][0:64, :], AF.Copy, [pk(ba)], ["Q"])
                                R2 = slice(64, 96)
                                tt("dve", t1[R2, :], psf[ba][R2, :], rope[R2, 0, g * 512:(g + 1) * 512], MUL, [pk(ba), "rope"], ["t1"])
                                tt("dve", t2[R2, :], psf[bb2][R2, :], rope[R2, 1, g * 512:(g + 1) * 512], MUL, [pk(bb2), "rope"], ["t2"])
                                tt("dve", QM[R2, j, g * 512:(g + 1) * 512], t1[R2, :], t2[R2, :], ADD, ["t1", "t2"], ["Q"])
                        for kb in range(18):
                            b = bank("a")
                            for c in range(2):
                                mm(psf[b][:, 0:128], ckvT[:, c, kb * 128:(kb + 1) * 128], wuv[:, c, hb * 128:(hb + 1) * 128],
                                   c == 0, c == 1, ["wu", "ckvT"], [pk(b)])
                            cp("dve", VM[:, kb, :, 0:64], psf[b][:, 0:128].rearrange("p (j d) -> p j d", j=2), [pk(b)], ["Vx"])
                            cp("dve", VM[:, kb, :, 128:192], psf[b][:, 0:128].rearrange("p (j d) -> p j d", j=2), [pk(b)], ["Vx"])
                    with Scope() as S:
                        attention(S, 2, 105, 96.0 ** -0.5, lambda j: QM[:, j, :], lambda j: KM[:, j, :],
                                  lambda j, kb: VM[:, kb, j, :],
                                  lambda j, qg: o_mla[j * 64:j * 64 + 64, hb, qg * 512:(qg + 1) * 512],
                                  lambda j: j)

    def resid_tile(S, t, banks, lhs_fn, nk, rhs_fn, src, dst, grow, bufs, rkeys):
        xt, st_ = bufs
        j = t % xt.shape[1]
        xk = "xr%d" % j
        dma("sp", xt[:, j, :], src[t * 128:(t + 1) * 128, :], (), [xk], xk)
        for hf in range(2):
            b = banks[hf]
            for k in range(nk):
                mm(psf[b][:, :], lhs_fn(k, t), rhs_fn(k, hf), k == 0, k == nk - 1, rkeys, [pk(b)])
        sk = "rst%d" % j
        mset("dve", st_[:, j, 0:2], 0.0, [sk])
        for hf in range(2):
            act(junk[:, hf * 512:(hf + 1) * 512], psf[banks[hf]][:, :], AF.Square, [pk(banks[hf])], ["junk", sk],
                accum=st_[:, j, hf:hf + 1])
        tt("dve", st_[:, j, 2:3], st_[:, j, 0:1], st_[:, j, 1:2], ADD, [sk], [sk])
        act(st_[:, j, 3:4], st_[:, j, 2:3], AF.Sqrt, [sk], [sk], bias=EPS, scale=1.0 / 1024)
        recip(st_[:, j, 4:5], st_[:, j, 3:4], [sk], [sk])
        for hf in range(2):
            stt("dve", junk[:, hf * 512:(hf + 1) * 512], psf[banks[hf]][:, :], st_[:, j, 4:5], grow[:, hf * 512:(hf + 1) * 512],
                MUL, MUL, [pk(banks[hf]), sk, "row"], ["junk"])
        tt("dve", xt[:, j, :], xt[:, j, :], junk[:, :], ADD, [xk, "junk"], [xk])
        dma("sp", dst[t * 128:(t + 1) * 128, :], xt[:, j, :], [xk], [], xk)

    def phase_merge(l, obr, src, dst):
        with Scope() as S1:
            merged = S1.sb("merged", [128, 8, T], BF16)
            with Scope() as S:
                setpools(g=[0, 1, 2], p=[3, 4, 5])
                wms = [S.sb("wms%d" % i, [128, 4, 8, 128], BF16) for i in range(2)]
                wb01 = [S.sb("wb01%d" % i, [128, 4, 4, 128], BF16) for i in range(2)]
                Gs = S.sb("Gs", [128, 512]); acc = S.sb("acc", [128, 512]); tm = S.sb("tm", [128, 512])
                for n in range(8):
                    i = n % 2
                    wk = "wmg%d" % i
                    for b_ in range(4):
                        dma("pool", wms[i][:, b_, :, :], D["wmerge"][l, :, :, b_ * 1024 + n * 128:b_ * 1024 + (n + 1) * 128],
                            (), [wk], wk)
                    for b_ in range(4):
                        dma("pool", wb01[i][:, b_, :, :], D["wbr01"][l, b_, :, :, n * 128:(n + 1) * 128], (), [wk], wk)
                    P.seal(wk)
                    for g in range(NG):
                        gs = slice(g * 512, (g + 1) * 512)
                        for b_ in range(4):
                            bg = bank("g")
                            for kc in range(8):
                                mm(psf[bg][:, :], wms[i][:, b_, kc, :], hT[:, kc, gs], kc == 0, kc == 7, [wk, "h%d" % g], [pk(bg)])
                            act(Gs[:], psf[bg][:, :], AF.Sigmoid, [pk(bg), "vec"], ["Gs"],
                                bias=V(l, "b_merge", 32)[:, b_ * 8 + n:b_ * 8 + n + 1])
                            bp = bank("p")
                            for kc in range(4):
                                mm(psf[bp][:, :], wb01[i][:, b_, kc, :], obr[b_][:, kc, gs], kc == 0, kc == 3, [wk, "obr"], [pk(bp)])
                            if b_ == 0:
                                tt("dve", acc[:], Gs[:], psf[bp][:, :], MUL, ["Gs", pk(bp)], ["acc"])
                            else:
                                tt("dve", tm[:], Gs[:], psf[bp][:, :], MUL, ["Gs", pk(bp)], ["tm"])
                                if b_ < 3:
                                    tt("dve", acc[:], acc[:], tm[:], ADD, ["acc", "tm"], ["acc"])
                                else:
                                    tt("dve", merged[:, n, gs], acc[:], tm[:], ADD, ["acc", "tm"], ["merged"])
            with Scope() as S:
                setpools(m=[0, 1], r=[2, 3, 4, 5])
                wo = S.sb("wo", [128, 8, 1024], BF16)
                grow = S.sb("grow", [128, 1024])
                xt = S.sb("xr", [128, 4, 1024]); st_ = S.sb("rst", [128, 4, 8])
                dma("pool", wo[:], D["wout"][l], (), ["wo"], "wo")
                make_row(l, 16, grow, S)
                for t in range(NT):
                    banks = [bank("r"), bank("r")]
                    resid_tile(S, t, banks, lambda k, t_: merged[:, k, t_ * 128:(t_ + 1) * 128], 8,
                               lambda k, hf: wo[:, k, hf * 512:(hf + 1) * 512], src, dst, grow, (xt, st_), ["merged", "wo"])

    def phase_ffn(l, src, dst):
        with Scope() as S1:
            actT = S1.sb("actT", [128, 22, T], BF16)
            with Scope() as S:
                setpools(a=[0, 1, 2, 3, 4, 5])
                wu = [S.sb("wu%d" % i, [128, 8, 2, 128], BF16) for i in range(2)]
                raw = [S.sb("raw%d" % i, [128, T + 2]) for i in range(2)]
                yu = S.sb("yu", [128, T]); yg = S.sb("yg", [128, T])
                nw = S.sb("nwf", [128, 88])
                for i in range(2):
                    mset("pool", raw[i][:, 0:1], 0.0, ["raw%d" % i])
                    mset("pool", raw[i][:, T + 1:T + 2], 0.0, ["raw%d" % i])
                cw0 = _VOFF["ffn_cw"]
                stt("dve", nw[:, 0:44], vec[:, l, cw0:cw0 + 44], -1.0, flg[:, 32:33].to_broadcast([128, 44]), MUL, MUL,
                    ["vec", "flg"], ["nw"])
                stt("dve", nw[:, 44:88], vec[:, l, cw0 + 88:cw0 + 132], -1.0, flg[:, 32:33].to_broadcast([128, 44]), MUL, MUL,
                    ["vec", "flg"], ["nw"])
                for c in range(22):
                    i = c % 2
                    wk = "wu%d" % i
                    dma("pool", wu[i][:, :, 0, :], D["wup"][l, :, :, c * 128:(c + 1) * 128], (), [wk], wk)
                    dma("pool", wu[i][:, :, 1, :], D["wup"][l, :, :, 2816 + c * 128:2816 + (c + 1) * 128], (), [wk], wk)
                    P.seal(wk)
                    for which in range(2):
                        ch = c + 22 * which
                        rw, rk_ = raw[which], "raw%d" % which
                        for g in range(NG):
                            b = bank("a")
                            for kc in range(8):
                                mm(psf[b][:, :], wu[i][:, kc, which, :], hT[:, kc, g * 512:(g + 1) * 512], kc == 0, kc == 7,
                                   [wk, "h%d" % g], [pk(b)])
                            act(rw[:, 1 + g * 512:1 + (g + 1) * 512], psf[b][:, :], AF.Copy, [pk(b)], [rk_])
                        y_, yk = (yu, "yu") if which == 0 else (yg, "yg")
                        conv(rw, rk_, vec[:, l, cw0 + ch:cw0 + ch + 1], vec[:, l, cw0 + 44 + ch:cw0 + 45 + ch],
                             vec[:, l, cw0 + 88 + ch:cw0 + 89 + ch], V(l, "ffn_cb", 44)[:, ch:ch + 1],
                             nw[:, ch:ch + 1], nw[:, 44 + ch:45 + ch], y_, yk)
                    act(yg[:, :], yg[:, :], AF.Silu, ["yg"], ["yg"])
                    tt("dve", actT[:, c, :], yu[:, :], yg[:, :], MUL, ["yu", "yg"], ["actT"])
            with Scope() as S:
                setpools(m=[0, 1], r=[2, 3, 4, 5])
                wd = S.sb("wd", [128, 22, 1024], BF16)
                grow = S.sb("grow", [128, 1024])
                xt = S.sb("xr", [128, 2, 1024]); st_ = S.sb("rst", [128, 2, 8])
                for q in range(2):
                    dma("pool", wd[:, q * 11:(q + 1) * 11, :], D["wdown"][l, :, q * 11:(q + 1) * 11, :], (), ["wd"], "wd%d" % q)
                make_row(l, 24, grow, S)
                for t in range(NT):
                    banks = [bank("r"), bank("r")]
                    resid_tile(S, t, banks, lambda k, t_: actT[:, k, t_ * 128:(t_ + 1) * 128], 22,
                               lambda k, hf: wd[:, k, hf * 512:(hf + 1) * 512], src, dst, grow, (xt, st_), ["actT", "wd"])

    dbg = dbg or {}
    stop = dbg.get("stop")
    nlayers = dbg.get("layers", DEPTH)

    def dump(name, t, keys=()):
        shp = list(t.shape)
        dd = nc.dram_tensor("dbg_" + name, shp, F32, kind="ExternalOutput").ap()
        idx = tuple(slice(None) for _ in shp)
        dma("pool", dd[idx], t[idx], list(keys), [], "dbg_" + name)

    xin = D["x"]
    for l in range(nlayers):
        if l == 0:
            phase_mod(lambda: phase_norm(xin, 0, der[:, 0, 0:8], modt[:, 0, 0:8]))
        else:
            phase_norm(xin, l, der[:, l, 0:8], modt[:, l, 0:8])
        if stop == "norm":
            dump("h", hT); P.flush(); return
        with Scope() as SA:
            o_ssd = SA.sb("o_ssd", [128, 4, T], BF16)
            if "ssd" not in dbg.get("skip", ()):
                phase_ssd(l, o_ssd)
            if stop == "ssd":
                dump("o_ssd", o_ssd); P.flush(); return
            with Scope() as SB:
                o_ret = SB.sb("o_ret", [128, 4, T], BF16)
                if "ret" not in dbg.get("skip", ()):
                    phase_ret(l, o_ret)
                if stop == "ret":
                    dump("o_ret", o_ret); P.flush(); return
                with Scope() as SC:
                    o_mla = SC.sb("o_mla", [128, 4, T], BF16)
                    if "mla" not in dbg.get("skip", ()):
                        phase_mla(l, o_mla)
                    if stop == "mla":
                        dump("o_mla", o_mla); P.flush(); return
                    with Scope() as SD:
                        o_att = SD.sb("o_att", [128, 4, T], BF16)
                        if "gqa" not in dbg.get("skip", ()):
                            phase_gqa(l, o_att)
                        if stop == "gqa":
                            dump("o_att", o_att); P.flush(); return
                        if stop == "mixers":
                            dump("o_ssd", o_ssd); dump("o_ret", o_ret); dump("o_mla", o_mla); dump("o_att", o_att)
                            P.flush(); return
                        phase_merge(l, [o_ssd, o_ret, o_att, o_mla], xin, xa)
        if stop == "merge":
            P.flush(); return
        phase_norm(xa, l, der[:, l, 8:16], modt[:, l, 24:32])
        dst = xb if l == 0 else O["y"]
        if nlayers == 1:
            dst = O["y"]
        phase_ffn(l, xa, dst)
        xin = xb
    P.flush()


_NC_CACHE = {}


def kernel(**inputs):
    shared, percore = _host_prep(inputs)
    if "nc" not in _NC_CACHE:
        _NC_CACHE["nc"] = build_nc()
    nc = _NC_CACHE["nc"]
    in_maps = [dict(shared, **percore[c]) for c in range(8)]
    res = run_bass_kernel_spmd(nc, in_maps, core_ids=list(range(8)))
    R = res.results
    f = np.float32
    y_prompt = np.stack([R[c]["y"] for c in range(4)]).reshape(32, 256, 1024).astype(f)
    y_sample = np.stack([R[c]["y"] for c in range(4, 8)]).reshape(4, 2048, 1024).astype(f)
    st_ssd = np.zeros((32, 2, 2, 8, 64, 64), f)
    st_ret = np.zeros((32, 2, 2, 4, 64, 128), f)
    ck = np.zeros((32, 2, 256, 2, 64), f)
    cv = np.zeros((32, 2, 256, 2, 64), f)
    cc = np.zeros((32, 2, 256, 256), f)
    cr = np.zeros((32, 2, 256, 32), f)
    for c in range(4):
        os_, or_ = R[c]["o_ssd"], R[c]["o_ret"]
        for h in range(8):
            g = h // 4
            st_ssd[8 * c:8 * c + 8, :, :, h] = os_[:, :, :, g * 64:(g + 1) * 64, h * 64:(h + 1) * 64].transpose(2, 0, 1, 3, 4)
        for h in range(4):
            kt, lh = h // 2, h % 2
            st_ret[8 * c:8 * c + 8, :, :, h] = or_[:, :, :, lh * 64:(lh + 1) * 64,
                                                   kt * 256 + lh * 128:kt * 256 + (lh + 1) * 128].transpose(2, 0, 1, 3, 4)
        ck[8 * c:8 * c + 8] = R[c]["o_ck"].reshape(2, 8, 256, 2, 64).transpose(1, 0, 2, 3, 4)
        cv[8 * c:8 * c + 8] = R[c]["o_cv"].reshape(2, 8, 256, 2, 64).transpose(1, 0, 2, 3, 4)
        cc[8 * c:8 * c + 8] = R[c]["o_ckv"].reshape(2, 8, 256, 256).transpose(1, 0, 2, 3)
        cr[8 * c:8 * c + 8] = R[c]["o_kr"].reshape(2, 8, 256, 32).transpose(1, 0, 2, 3)
    return (y_prompt, y_sample, st_ssd, st_ret, ck, cv, cc, cr)
```

```python
import numpy as np
from contextlib import ExitStack
import concourse.bass as bass
import concourse.mybir as mybir
from concourse.bass_utils import run_bass_kernel_spmd

F32 = mybir.dt.float32
BF16 = mybir.dt.bfloat16
AF = mybir.ActivationFunctionType
ALU = mybir.AluOpType
AX = mybir.AxisListType

T = 2048
NT = 16
NG = 4
DEPTH = 2
EPS = 1e-6
NEG = -30000.0
BIG = 512.0

_VOFF = {}
_NV = 0


def _vreg(name, n):
    global _NV
    _VOFF[name] = _NV
    _NV += n


for _n, _c in [("b_mod", 48), ("g_pre_mix", 8), ("g_post_mix", 8), ("g_pre_ffn", 8), ("g_post_ffn", 8),
               ("ssd_cw", 18), ("ssd_cb", 6), ("ffn_cw", 132), ("ffn_cb", 44), ("b_merge", 32),
               ("qn", 1), ("qnp", 1), ("kn", 1), ("knp", 1), ("mqn", 3), ("mkvn", 2),
               ("dt_bias", 16), ("a_log", 16), ("ssd_d", 8), ("ret_logit", 8),
               ("ssd_nw", 512), ("ret_gw", 512)]:
    _vreg(_n, _c)

_COFF = {"ident": 0, "triU": 128, "ntriS": 256, "nmU": 384, "nmL": 512, "ones": 640, "idx": 768}
_NC = 772
_CBOFF = {"ident": 0, "blk64": 128, "o384": 256, "o256": 384, "o128": 512}
_NCB = 1152

ENGS = ("pe", "act", "dve", "pool", "sp")


class Op:
    __slots__ = ("eng", "fn", "deps", "signal", "sigval", "slot", "dmaval")


class Prog:
    def __init__(self, nc, es):
        self.nc = nc
        self.es = es
        self.ops = {e: [] for e in ENGS}
        self.lastw = {}
        self.rd = {}
        self.esem = {e: es.enter_context(nc.semaphore("sem_" + e)) for e in ENGS}
        self.ecnt = {e: 0 for e in ENGS}
        self.dsem = {}
        self.dcnt = {}
        self.pend_dma = []
        self.waited = {e: {} for e in ENGS}
        self.out_dmas = []
        self.sealed = {}
        self.nops = 0

    def add(self, eng, fn, r=(), w=(), slot=None):
        op = Op()
        op.eng = eng
        op.fn = fn
        op.signal = False
        op.sigval = None
        op.slot = slot
        op.dmaval = None
        deps = []
        for k in r:
            d = self.lastw.get(k)
            if d is not None:
                deps.append(d)
        for k in w:
            d = self.lastw.get(k)
            if d is not None:
                deps.append(d)
            rr = self.rd.get(k)
            if rr:
                deps.extend(rr[0].values())
                deps.extend(rr[1])
        op.deps = [d for d in deps if d is not op and not (d.eng == "pe" and eng == "pe" and d.slot is None
                                                          and slot is None)]
        for k in w:
            self.lastw[k] = op
            self.rd[k] = [{}, []]
        for k in r:
            rr = self.rd.setdefault(k, [{}, []])
            if slot is not None:
                rr[1].append(op)
            else:
                rr[0][eng] = op
        if slot is not None:
            if slot not in self.dsem:
                self.dsem[slot] = self.es.enter_context(self.nc.semaphore("d_" + slot))
                self.dcnt[slot] = 0
            self.dcnt[slot] += 16
            op.dmaval = self.dcnt[slot]
            self.pend_dma.append(op)
        self.ops[eng].append(op)
        self.nops += 1
        return op

    def seal(self, slot):
        tot = self.dcnt.get(slot)
        if tot is None:
            return
        grp = [op for op in self.pend_dma if op.slot == slot and op.dmaval > self.sealed.get(slot, 0)]
        gs = set(id(o) for o in grp)
        for op in grp:
            op.dmaval = tot
            op.deps = [d for d in op.deps if id(d) not in gs]
        self.sealed[slot] = tot

    def flush(self):
        lasts = [self.ops[e][-1] for e in ENGS if self.ops[e]]
        b1 = Op()
        b1.eng, b1.fn, b1.signal, b1.sigval, b1.slot, b1.dmaval = "sp", None, False, None, None, None
        b1.deps = [d for d in lasts] + list(self.pend_dma)
        self.ops["sp"].append(b1)
        for e in ENGS:
            if e == "sp":
                continue
            b = Op()
            b.eng, b.fn, b.signal, b.sigval, b.slot, b.dmaval = e, None, False, None, None, None
            b.deps = [b1]
            self.ops[e].append(b)
        for e in ENGS:
            for op in self.ops[e]:
                for d in op.deps:
                    if d.slot is None:
                        d.signal = True
        for e in ENGS:
            for op in self.ops[e]:
                if op.slot is None and op.signal:
                    self.ecnt[e] += 1
                    op.sigval = self.ecnt[e]
        nc = self.nc
        with nc.Block() as block:
            def mk(ename):
                def body(eng):
                    wt = self.waited[ename]
                    for op in self.ops[ename]:
                        for d in op.deps:
                            if d.slot is not None:
                                sem, val = self.dsem[d.slot], d.dmaval
                                key = "d_" + d.slot
                            else:
                                sem, val = self.esem[d.eng], d.sigval
                                key = d.eng
                            if wt.get(key, 0) >= val:
                                continue
                            eng.wait_ge(sem, val)
                            wt[key] = val
                        if op.fn is None:
                            if op.signal:
                                eng.nop().then_inc(self.esem[ename], 1)
                            continue
                        ins = op.fn(eng)
                        if op.slot is not None:
                            ins.then_inc(self.dsem[op.slot], 16)
                        elif op.signal:
                            ins.then_inc(self.esem[ename], 1)
                return body
            block.tensor(mk("pe"))
            block.scalar(mk("act"))
            block.vector(mk("dve"))
            block.gpsimd(mk("pool"))
            block.sync(mk("sp"))
        self.ops = {e: [] for e in ENGS}
        self.lastw = {}
        self.rd = {}
        self.pend_dma = []


def _rope_tables(L, dim):
    d_axis = dim // 2
    n_rows = L // 64
    rows = np.repeat(np.arange(n_rows, dtype=np.float32), 64)
    cols = np.tile(np.arange(64, dtype=np.float32), n_rows)
    inv = (np.float32(10000.0) ** (-np.arange(0, d_axis, 2, dtype=np.float32) / np.float32(d_axis))).astype(np.float32)
    ar = rows[:, None] * inv[None, :]
    ac = cols[:, None] * inv[None, :]
    cos = np.concatenate([np.cos(ar), np.cos(ar), np.cos(ac), np.cos(ac)], axis=1).astype(np.float32)
    sin = np.concatenate([-np.sin(ar), np.sin(ar), -np.sin(ac), np.sin(ac)], axis=1).astype(np.float32)
    return cos.T.copy(), sin.T.copy()


def _perm(dim):
    q = dim // 4
    return np.concatenate([np.arange(q, 2 * q), np.arange(0, q), np.arange(3 * q, 4 * q), np.arange(2 * q, 3 * q)])


def _pk(a):
    K, N = a.shape
    return np.ascontiguousarray(a.reshape(K // 128, 128, N).transpose(1, 0, 2))


def _col8(v):
    return np.ascontiguousarray(v.reshape(-1, 128).T)


def _host_prep(inp):
    f = np.float32
    A = {k: np.asarray(v, dtype=f) for k, v in inp.items()}
    shared = {}
    p64, p32 = _perm(64), _perm(32)
    win, wuq, wuk, wuv, vecs = [], [], [], [], []
    offs = np.cumsum([0, 512, 768, 16, 256, 256, 512, 512, 512, 128, 128, 384, 288])
    for l in range(DEPTH):
        W = A["w_in"][l]
        z, xbc, dtr, rq, rk, rv, rg, aq, ak, av, mcq, mckv = [W[:, offs[i]:offs[i + 1]] for i in range(12)]

        def hp(m, p):
            nh = m.shape[1] // len(p)
            return m.reshape(1024, nh, len(p))[:, :, p].reshape(1024, -1)
        ckv, kr = mckv[:, :256], mckv[:, 256:]
        ext = np.concatenate([z, xbc, dtr,
                              rq, hp(rq, p64), rk, hp(rk, p64), rv, rg,
                              aq, hp(aq, p64), ak, hp(ak, p64), av,
                              mcq, ckv, kr, hp(kr, p32)], axis=1)
        assert ext.shape[1] == 5456
        win.append(_pk(ext))
        U = A["mla_w_uq"][l].reshape(384, 8, 96)
        ua = np.zeros((384, 8, 192), f)
        ua[:, :, 0:96] = U
        ua[:, :, 96 + 64:192] = U[:, :, 64:96][:, :, p32]
        wuq.append(_pk(ua.reshape(384, 1536)))
        KV = A["mla_w_ukv"][l].reshape(256, 8, 128)
        wuk.append(_pk(np.ascontiguousarray(KV[:, :, :64]).reshape(256, 512)))
        wuv.append(_pk(np.ascontiguousarray(KV[:, :, 64:]).reshape(256, 512)))
        V = np.zeros((128, _NV), f)

        def put(name, arr):
            arr = np.asarray(arr, f)
            V[:arr.shape[0], _VOFF[name]:_VOFF[name] + arr.shape[1]] = arr
        put("b_mod", A["b_mod"][l].reshape(48, 128).T)
        for nm in ("g_pre_mix", "g_post_mix", "g_pre_ffn", "g_post_ffn"):
            put(nm, _col8(A[nm][l]))
        cw = A["ssd_conv_w"][l]
        put("ssd_cw", np.concatenate([cw[j].reshape(6, 128).T for j in range(3)], axis=1))
        put("ssd_cb", A["ssd_conv_b"][l].reshape(6, 128).T)
        fw = A["ffn_conv_w"][l]
        put("ffn_cw", np.concatenate([fw[j].reshape(44, 128).T for j in range(3)], axis=1))
        put("ffn_cb", A["ffn_conv_b"][l].reshape(44, 128).T)
        put("b_merge", A["b_merge"][l].reshape(32, 128).T)
        qn, kn = A["att_q_norm"][l], A["att_k_norm"][l]
        put("qn", qn[:, None]); put("qnp", qn[p64][:, None]); put("kn", kn[:, None]); put("knp", kn[p64][:, None])
        put("mqn", A["mla_q_norm"][l].reshape(3, 128).T)
        put("mkvn", A["mla_kv_norm"][l].reshape(2, 128).T)
        bc = lambda v: np.broadcast_to(np.asarray(v, f).reshape(1, -1), (128, np.asarray(v).size))
        put("dt_bias", bc(A["ssd_dt_bias"][l])); put("a_log", bc(A["ssd_a_log"][l]))
        put("ssd_d", bc(A["ssd_d"][l])); put("ret_logit", bc(A["ret_decay_logit"][l]))
        put("ssd_nw", bc(A["ssd_norm_w"][l])); put("ret_gw", bc(A["ret_gn_w"][l]))
        vecs.append(V)
    shared["win"] = np.stack(win)
    shared["wuq"] = np.stack(wuq)
    shared["wuk"] = np.stack(wuk)
    shared["wuv"] = np.stack(wuv)
    shared["vecs"] = np.stack(vecs)
    shared["wmod"] = np.stack([_pk(A["w_mod"][l]) for l in range(DEPTH)])
    shared["wmerge"] = np.stack([_pk(A["w_merge"][l]) for l in range(DEPTH)])
    shared["wout"] = np.stack([_pk(A["w_out"][l]) for l in range(DEPTH)])
    shared["wup"] = np.stack([_pk(A["w_ffn_up"][l]) for l in range(DEPTH)])
    shared["wdown"] = np.stack([_pk(A["w_ffn_down"][l]) for l in range(DEPTH)])
    shared["wbr01"] = np.stack([np.stack([_pk(A[nm][l]) for nm in ("w_br_ssd", "w_br_ret", "w_br_att", "w_br_mla")])
                                for l in range(DEPTH)])
    C = np.zeros((128, _NC), f)
    ii = np.arange(128)
    C[:, 0:128] = np.eye(128)
    C[:, 128:256] = (ii[:, None] <= ii[None, :])
    C[:, 256:384] = -1.0 * (ii[:, None] < ii[None, :])
    C[:, 384:512] = np.where(ii[:, None] <= ii[None, :], 0.0, NEG)
    C[:, 512:640] = np.where(ii[:, None] >= ii[None, :], 0.0, NEG)
    C[:, 640:768] = 1.0
    shared["consts"] = C
    CB = np.zeros((128, _NCB), f)
    CB[:, 0:128] = np.eye(128)
    CB[0:64, 128:192] = 1.0 / 64
    CB[64:128, 192:256] = 1.0 / 64
    CB[:, 256:384] = 1.0 / 384
    CB[:, 384:512] = 1.0 / 256
    CB[:, 512:640] = 1.0 / 128
    CB[:, 640:768] = (ii[:, None] <= ii[None, :])
    CB[:, 768:896] = -1.0 * (ii[:, None] < ii[None, :])
    CB[:, 896:1024] = np.where(ii[:, None] <= ii[None, :], 0.0, NEG)
    CB[:, 1024:1152] = np.where(ii[:, None] >= ii[None, :], 0.0, NEG)
    shared["constb"] = CB

    cos64, sin64 = _rope_tables(T, 64)
    cos32, sin32 = _rope_tables(T, 32)
    percore = []
    for c in range(8):
        prompt = c < 4
        d = {}
        if prompt:
            d["x"] = np.ascontiguousarray(A["x_prompt"][8 * c:8 * c + 8].reshape(T, 1024))
            d["cond"] = _col8(A["c_ctx"])
            d["s_ssd"] = np.zeros((2, 2, 128, 512), f)
            d["s_ret"] = np.zeros((2, 2, 128, 512), f)
            d["ckT"] = np.zeros((2, 2, 64, 256), f)
            d["cv"] = np.zeros((2, 256, 128), f)
            d["cckvT"] = np.zeros((2, 256, 256), f)
            d["ckrT"] = np.zeros((2, 32, 256), f)
            r64 = np.zeros((128, 2, T), f); r64[:, 0] = 1.0
            r32 = np.zeros((128, 2, T), f); r32[:, 0] = 1.0
            aq = np.zeros((9, T), f)
            ak = np.zeros((9, T + 256), f)
            for s in range(8):
                aq[s, 256 * s:256 * (s + 1)] = BIG
                ak[s, 256 + 256 * s:256 + 256 * (s + 1)] = 1.0
            aq[8] = -BIG
            ak[8] = 1.0
            fl = np.zeros((128, 40), f)
            fl[:, 0:16] = (np.arange(16) % 2 == 1)
            fl[:, 16:32] = (np.arange(16) % 2 == 0)
            fl[:, 32] = 1.0
        else:
            b = c - 4
            d["x"] = np.ascontiguousarray(A["x_sample"][b])
            d["cond"] = _col8(A["c"][b])
            ss = A["state_ssd"][b]
            S = np.zeros((2, 2, 128, 512), f)
            for h in range(8):
                g = h // 4
                S[:, :, g * 64:(g + 1) * 64, h * 64:(h + 1) * 64] = ss[:, :, h]
            d["s_ssd"] = S
            sr = A["state_ret"][b]
            R = np.zeros((2, 2, 128, 512), f)
            for h in range(4):
                kt, lh = h // 2, h % 2
                R[:, :, lh * 64:(lh + 1) * 64, kt * 256 + lh * 128: kt * 256 + (lh + 1) * 128] = sr[:, :, h]
            d["s_ret"] = R
            d["ckT"] = np.ascontiguousarray(A["cache_att_k"][b].transpose(0, 2, 3, 1))
            d["cv"] = np.ascontiguousarray(A["cache_att_v"][b].reshape(2, 256, 128))
            d["cckvT"] = np.ascontiguousarray(A["cache_mla_ckv"][b].transpose(0, 2, 1))
            d["ckrT"] = np.ascontiguousarray(A["cache_mla_krope"][b].transpose(0, 2, 1))
            r64 = np.zeros((128, 2, T), f)
            r64[0:64, 0] = cos64; r64[64:128, 0] = cos64; r64[0:64, 1] = sin64; r64[64:128, 1] = sin64
            r32 = np.zeros((128, 2, T), f)
            r32[0:32, 0] = cos32; r32[64:96, 0] = cos32; r32[0:32, 1] = sin32; r32[64:96, 1] = sin32
            aq = np.zeros((9, T), f)
            ak = np.zeros((9, T + 256), f)
            fl = np.zeros((128, 40), f)
            fl[:, 0:32] = 1.0
        d["rope64"] = r64
        d["rope32"] = r32
        d["augq"] = aq
        d["augk"] = ak
        d["flags"] = fl
        percore.append(d)
    return shared, percore


def build_nc(dbg=None):
    nc = bass.Bass("TRN2", target_bir_lowering=False)
    es = ExitStack()
    with es:
        _build(nc, es, dbg)
    return nc


def _build(nc, es, dbg):
    dbg = dbg or {}
    P = Prog(nc, es)

    def din(name, shape):
        return nc.dram_tensor(name, list(shape), F32, kind="ExternalInput").ap()

    def dout(name, shape):
        return nc.dram_tensor(name, list(shape), F32, kind="ExternalOutput").ap()

    D = {}
    for nm, shp in [("x", (T, 1024)), ("cond", (128, 8)), ("s_ssd", (2, 2, 128, 512)), ("s_ret", (2, 2, 128, 512)),
                    ("ckT", (2, 2, 64, 256)), ("cv", (2, 256, 128)), ("cckvT", (2, 256, 256)), ("ckrT", (2, 32, 256)),
                    ("rope64", (128, 2, T)), ("rope32", (128, 2, T)), ("augq", (9, T)), ("augk", (9, T + 256)),
                    ("flags", (128, 40)),
                    ("win", (2, 128, 8, 5456)), ("wuq", (2, 128, 3, 1536)), ("wuk", (2, 128, 2, 512)),
                    ("wuv", (2, 128, 2, 512)), ("vecs", (2, 128, _NV)), ("wmod", (2, 128, 8, 6144)),
                    ("wmerge", (2, 128, 8, 4096)), ("wout", (2, 128, 8, 1024)), ("wup", (2, 128, 8, 5632)),
                    ("wdown", (2, 128, 22, 1024)), ("wbr01", (2, 4, 128, 4, 1024)),
                    ("consts", (128, _NC)), ("constb", (128, _NCB))]:
        D[nm] = din(nm, shp)
    O = {}
    for nm, shp in [("y", (T, 1024)), ("o_ssd", (2, 2, 8, 128, 512)), ("o_ret", (2, 2, 8, 128, 512)),
                    ("o_ck", (2, T, 128)), ("o_cv", (2, T, 128)), ("o_ckv", (2, T, 256)), ("o_kr", (2, T, 32))]:
        O[nm] = dout(nm, shp)
    xa = nc.dram_tensor("xa_scr", [T, 1024], F32, kind="Internal").ap()
    xb = nc.dram_tensor("xb_scr", [T, 1024], F32, kind="Internal").ap()

    _uid = [0]

    class Scope:
        def __init__(self):
            self.st = ExitStack()

        def __enter__(self):
            self.st.__enter__()
            return self

        def sb(self, name, shape, dt=F32):
            _uid[0] += 1
            return self.st.enter_context(nc.sbuf_tensor("s%d_%s" % (_uid[0], name), list(shape), dt))

        def __exit__(self, *a):
            P.flush()
            return self.st.__exit__(*a)

    psf = [es.enter_context(nc.psum_tensor("psf%d" % i, [128, 512], F32)) for i in range(7)]
    psb = es.enter_context(nc.psum_tensor("psb7", [128, 1024], BF16))
    pools = {}

    def setpools(**kw):
        pools.clear()
        for k, v in kw.items():
            pools[k] = [list(v), 0]

    def bank(pool):
        p = pools[pool]
        b = p[0][p[1] % len(p[0])]
        p[1] += 1
        return b

    def pk(b):
        return "ps%d" % b

    def act(out, in_, func, r, w, bias=None, scale=None, accum=None):
        kw = {}
        if bias is not None:
            kw["bias"] = bias
        if scale is not None:
            kw["scale"] = scale
        if accum is not None:
            kw["accum_out"] = accum
        return P.add("act", lambda e: e.activation(out, in_, func, **kw), r, w)

    def tt(eng, out, a, b, op, r, w):
        return P.add(eng, lambda e: e.tensor_tensor(out, a, b, op), r, w)

    def ts(eng, out, a, s1, op0, r, w, s2=None, op1=None):
        if op1 is None:
            return P.add(eng, lambda e: e.tensor_scalar(out, a, s1, None, op0), r, w)
        return P.add(eng, lambda e: e.tensor_scalar(out, a, s1, s2, op0, op1), r, w)

    def stt(eng, out, a, s, b, op0, op1, r, w):
        return P.add(eng, lambda e: e.scalar_tensor_tensor(out, a, s, b, op0, op1), r, w)

    def cp(eng, out, in_, r, w):
        return P.add(eng, lambda e: e.tensor_copy(out, in_), r, w)

    def recip(out, in_, r, w):
        return P.add("dve", lambda e: e.reciprocal(out, in_), r, w)

    def mset(eng, ap, val, w):
        return P.add(eng, lambda e: e.memset(ap, val), (), w)

    def mm(out, lhsT, rhs, st, sp, r, w):
        return P.add("pe", lambda e: e.matmul(out, lhsT, rhs, start=st, stop=sp), r, w)

    def tp(out, in_, ident, r, w):
        return P.add("pe", lambda e: e.transpose(out, in_, ident), r, w)

    def dma(eng, out, in_, r, w, slot):
        return P.add(eng, lambda e: e.dma_start(out=out, in_=in_), r, w, slot=slot)

    def rsum(out, in_, r, w):
        return P.add("dve", lambda e: e.reduce_sum(out, in_, AX.X), r, w)

    MUL, ADD, SUB = ALU.mult, ALU.add, ALU.subtract

    def psb_(name, shape, dt=F32):
        return es.enter_context(nc.sbuf_tensor("p_" + name, list(shape), dt))

    cf = psb_("cf", [128, _NC])
    cb = psb_("cb", [128, _NCB], BF16)
    vec = psb_("vec", [128, 2, _NV])
    flg = psb_("flg", [128, 40])
    hT = psb_("hT", [128, 8, T], BF16)
    modt = psb_("modt", [128, 2, 48])
    der = psb_("der", [128, 2, 48])
    junk = psb_("junk", [128, 1024])

    identf = cf[:, 0:128]
    triU = cf[:, 128:256]
    ntriS = cf[:, 256:384]
    nmU = cf[:, 384:512]
    nmL = cf[:, 512:640]
    onesf = cf[:, 640:768]
    identb = cb[:, 0:128]
    blk64 = cb[:, 128:256]
    o384 = cb[:, 256:384]
    o256 = cb[:, 384:512]
    triUb = cb[:, 640:768]
    ntriSb = cb[:, 768:896]
    nmUb = cb[:, 896:1024]
    nmLb = cb[:, 1024:1152]

    def V(l, name, n=None, rows=128):
        o = _VOFF[name]
        return vec[0:rows, l, o:o + (n if n is not None else 1)]

    dma("sp", cf[:], D["consts"][:, :], (), ["cf"], "cf")
    dma("pool", cb[:], D["constb"][:, :], (), ["cb"], "cb")
    dma("sp", vec[:, 0, :], D["vecs"][0], (), ["vec"], "vec0")
    dma("sp", vec[:, 1, :], D["vecs"][1], (), ["vec"], "vec1")
    dma("sp", flg[:], D["flags"][:, :], (), ["flg"], "flg")
    P.flush()

    def phase_mod(after):
        with Scope() as S:
            setpools(m=[0, 1])
            cond = S.sb("cond", [128, 8])
            scb = S.sb("scb", [128, 8], BF16)
            wm = [S.sb("wm%d" % i, [128, 8, 1536], BF16) for i in range(2)]
            dma("sp", cond[:], D["cond"][:, :], (), ["cond"], "cond")
            act(scb[:], cond[:], AF.Silu, ["cond"], ["scb"])
            it = 0
            for l in range(DEPTH):
                b = bank("m")
                for pc in range(4):
                    w_ = wm[it % 2]
                    wk = "wm%d" % (it % 2)
                    it += 1
                    dma("pool", w_[:], D["wmod"][l, :, :, pc * 1536:(pc + 1) * 1536], (), [wk], wk)
                    for nn in range(12):
                        n = pc * 12 + nn
                        for kc in range(8):
                            mm(psf[b][:, n:n + 1], w_[:, kc, nn * 128:(nn + 1) * 128], scb[:, kc:kc + 1], kc == 0, kc == 7,
                               [wk, "scb"], [pk(b)])
                tt("dve", modt[:, l, :], psf[b][:, 0:48], V(l, "b_mod", 48), ADD, [pk(b), "vec"], ["modt"])
                for (dst, gname, sc0, gt0, pname) in ((0, "g_pre_mix", 8, 16, "g_post_mix"), (8, "g_pre_ffn", 32, 40, "g_post_ffn")):
                    stt("dve", der[:, l, dst:dst + 8], modt[:, l, sc0:sc0 + 8], 1.0, V(l, gname, 8), ADD, MUL,
                        ["modt", "vec"], ["der"])
                    tt("dve", der[:, l, 16 + dst:24 + dst], modt[:, l, gt0:gt0 + 8], V(l, pname, 8), MUL,
                       ["modt", "vec"], ["der"])
            after()

    def make_row(l, col0, dst, S):
        for hh in range(2):
            b = bank("m")
            for k4 in range(4):
                kc = hh * 4 + k4
                mm(psf[b][:, k4 * 128:(k4 + 1) * 128], der[:, l, col0 + kc:col0 + kc + 1].to_broadcast([128, 128]),
                   identf, True, True, ["der", "cf"], [pk(b)])
            cp("dve", dst[:, hh * 512:(hh + 1) * 512], psf[b][:, :], [pk(b)], ["row"])

    def phase_norm(src, l, gcol, shcol):
        with Scope() as S:
            setpools(t=[0, 1, 2, 3])
            xt = S.sb("xt", [128, 8, 1024])
            st_ = S.sb("nst", [128, 8, 4])

            def s1(g):
                for j4 in range(4):
                    t = 4 * g + j4
                    j = t % 8
                    xk = "xt%d" % j
                    dma("sp", xt[:, j, :], src[t * 128:(t + 1) * 128, :], (), [xk], xk)
                    mset("dve", st_[:, j, 0:1], 0.0, ["st%d" % j])
                    act(junk[:, :], xt[:, j, :], AF.Square, [xk], ["junk", "st%d" % j], accum=st_[:, j, 0:1])
                    act(st_[:, j, 1:2], st_[:, j, 0:1], AF.Sqrt, ["st%d" % j], ["st%d" % j], bias=EPS, scale=1.0 / 1024)
                    recip(st_[:, j, 2:3], st_[:, j, 1:2], ["st%d" % j], ["st%d" % j])
                    ts("dve", xt[:, j, :], xt[:, j, :], st_[:, j, 2:3], MUL, [xk, "st%d" % j], [xk])

            def s2(g):
                for kc in range(8):
                    b = bank("t")
                    for j4 in range(4):
                        j = (4 * g + j4) % 8
                        tp(psf[b][:, j4 * 128:(j4 + 1) * 128], xt[:, j, kc * 128:(kc + 1) * 128], identf,
                           ["xt%d" % j, "cf"], [pk(b)])
                    if kc % 2 == 0:
                        act(hT[:, kc, g * 512:(g + 1) * 512], psf[b][:, :], AF.Identity, [pk(b), "der", "modt"],
                            ["h%d" % g], scale=gcol[:, kc:kc + 1], bias=shcol[:, kc:kc + 1])
                    else:
                        ts("dve", hT[:, kc, g * 512:(g + 1) * 512], psf[b][:, :], gcol[:, kc:kc + 1], MUL,
                           [pk(b), "der", "modt"], ["h%d" % g], s2=shcol[:, kc:kc + 1], op1=ADD)

            s1(0)
            for g in range(NG):
                if g + 1 < NG:
                    s1(g + 1)
                s2(g)

    WP = 256

    def load_w(Wt, src, ncols, order=None):
        npieces = (ncols + WP - 1) // WP
        for i in (order if order is not None else range(npieces)):
            c0, c1 = i * WP, min(ncols, (i + 1) * WP)
            dma("pool", Wt[:, :, c0:c1], src[:, :, c0:c1], (), ["W_%d" % i], "W_%d" % i)

    def wkeys(c0, n):
        return ["W_%d" % i for i in range(c0 // WP, (c0 + n - 1) // WP + 1)]

    def proj_fm(wt, wk, c0, M, g, pool):
        b = bank(pool)
        wks = wkeys(c0, M) if wk == "W" else [wk]
        for kc in range(8):
            mm(psf[b][0:M, :], wt[:, kc, c0:c0 + M], hT[:, kc, g * 512:(g + 1) * 512], kc == 0, kc == 7,
               wks + ["h%d" % g], [pk(b)])
        return b

    def proj_tm(wt, wk, c0, N, t, b, o0=0):
        wks = wkeys(c0, N) if wk == "W" else [wk]
        for kc in range(8):
            mm(psf[b][:, o0:o0 + N], hT[:, kc, t * 128:(t + 1) * 128], wt[:, kc, c0:c0 + N], kc == 0, kc == 7,
               wks + ["h%d" % (t // 4)], [pk(b)])

    def conv(raw, rk_, w0, w1, w2, bcol, nw0, nw2, y, yk):
        act(y[:, :], raw[:, 1:T + 1], AF.Identity, [rk_, "vec"], [yk], scale=w1, bias=bcol)
        stt("dve", y[:, :], raw[:, 0:T], w0, y[:, :], MUL, ADD, [rk_, yk, "vec"], [yk])
        stt("dve", y[:, :], raw[:, 2:T + 2], w2, y[:, :], MUL, ADD, [rk_, yk, "vec"], [yk])
        yv = y[:, :].rearrange("p (m s) -> p m s", s=256)
        rv_ = raw[:, 0:T].rearrange("p (m s) -> p m s", s=256)
        stt("dve", yv[:, 1:8, 0:1], rv_[:, 1:8, 0:1], nw0, yv[:, 1:8, 0:1], MUL, ADD, [rk_, yk, "nw"], [yk])
        rv2 = raw[:, 2:T + 2].rearrange("p (m s) -> p m s", s=256)
        stt("dve", yv[:, 0:7, 255:256], rv2[:, 0:7, 255:256], nw2, yv[:, 0:7, 255:256], MUL, ADD, [rk_, yk, "nw"], [yk])

    def scan(S, l, nh, nkt, dv, QT, KT, KTM, Vt, la, lnw, s_in, s_out, epilogue, const_decay=False):
        hkt = nh // nkt
        hph = hkt // 2
        n2, n4, n6 = 2 * nh, 4 * nh, 6 * nh
        sm = S.sb("sm", [128, NT, n4])
        bia = S.sb("bia", [128, NT, n2])
        ex = S.sb("ex", [128, NT, n6])
        ar = ex
        dcc = S.sb("dcc", [128, NT, n2])
        tE = S.sb("tE", [128, NT, nh])
        Sbf = S.sb("Sbf", [128, NT, 512], BF16)
        Sst = S.sb("Sst", [128, 2, 512])
        Sbb = S.sb("Sbb", [128, 2, 512], BF16)
        nb_ = 1 if const_decay else 2
        Dm2 = [S.sb("Dm%d" % i, [128, n2, 128]) for i in range(nb_)]
        LL2 = [S.sb("LL%d" % i, [128, nh, 128]) for i in range(nb_)]
        MT2 = [S.sb("MT%d" % i, [128, nh, 128], BF16) for i in range(2)]
        Oa2 = [S.sb("Oa%d" % i, [128, 512]) for i in range(2)]
        tmpo2 = [S.sb("tmpo%d" % i, [128, 512]) for i in range(2)]
        tmps = S.sb("tmps", [128, 512])
        tmps2 = [S.sb("tmq%d" % i, [128, 512]) for i in range(2)]
        Vw = S.sb("Vw", [128, 2, 512], BF16)

        la_hi = S.sb("la_hi", [128, NT, n2], BF16)
        la_lo = S.sb("la_lo", [128, NT, n2], BF16)
        cp("dve", la_hi[:, :, :], la[:, :, :], ["la"], ["lahl"])
        tt("dve", bia[:, :, :], la[:, :, :], la_hi[:, :, :], SUB, ["la", "lahl"], ["bia"])
        cp("dve", la_lo[:, :, :], bia[:, :, :], ["bia"], ["lahl"])
        setpools(row=[0])
        bA = bank("row")
        for c in range(NT):
            mm(psf[bA][:, c * n4:c * n4 + n2], triU, la[:, c, :], True, True, ["cf", "la"], [pk(bA)])
            mm(psf[bA][:, c * n4 + n2:(c + 1) * n4], onesf, la[:, c, :], True, True, ["cf", "la"], [pk(bA)])
        act(sm[:, :, :], psf[bA][:, 0:NT * n4].rearrange("p (c q) -> p c q", c=NT), AF.Copy, [pk(bA)], ["sm"])
        Af, Ab = sm[:, :, 0:nh], sm[:, :, nh:n2]
        tf_, tb_ = sm[:, :, n2:n2 + nh], sm[:, :, n2 + nh:n4]
        tt("dve", tE[:, :, :], Ab, la[:, :, nh:n2], SUB, ["sm", "la"], ["tE"])
        if lnw is not None:
            tt("dve", bia[:, :, 0:nh], lnw[:, :, 0:nh], Af, SUB, ["sm", "la"], ["bia"])
            tt("dve", bia[:, :, nh:n2], lnw[:, :, nh:n2], tE[:, :, :], ADD, ["tE", "la"], ["bia"])
        else:
            ts("dve", bia[:, :, 0:nh], Af, -1.0, MUL, ["sm"], ["bia"])
            cp("dve", bia[:, :, nh:n2], tE[:, :, :], ["tE"], ["bia"])
        cp("dve", ar[:, :, 0:nh], Af, ["sm"], ["ar0"])
        tt("dve", ar[:, :, nh:n2], tb_, tE[:, :, :], SUB, ["sm", "tE"], ["ar1"])
        tt("dve", ar[:, :, n2:n2 + nh], bia[:, :, 0:nh], tf_, ADD, ["bia", "sm"], ["ar2"])
        cp("dve", ar[:, :, n2 + nh:n4], bia[:, :, nh:n2], ["bia"], ["ar3"])
        cp("dve", ar[:, :, n4:n6], sm[:, :, n2:n4], ["sm"], ["ar4"])
        act(ex[:, :, :], ar[:, :, :], AF.Exp, ["ar0", "ar1", "ar2", "ar3", "ar4"], ["ex", "ar0", "ar1", "ar2", "ar3", "ar4"])
        tt("dve", dcc[:, :, 0:nh], ex[:, :, n4:n4 + nh], flg[:, 0:16].unsqueeze(2).to_broadcast([128, NT, nh]), MUL,
           ["ex", "flg"], ["dcc"])
        tt("dve", dcc[:, :, nh:n2], ex[:, :, n4 + nh:n6], flg[:, 16:32].unsqueeze(2).to_broadcast([128, NT, nh]), MUL,
           ["ex", "flg"], ["dcc"])

        def bc3(ap2):
            return ap2.unsqueeze(2).to_broadcast([128, nh, dv])

        def v3(ap):
            return ap.rearrange("p (h e) -> p h e", h=nh)

        w_ = hkt * dv

        def st_pre(c, wcol0):
            j = c % 2
            vk = "Vw%d" % j
            tt("dve", v3(Vw[:, j, :]), v3(Vt[:, c, :]), bc3(ex[:, c, wcol0:wcol0 + nh]), MUL, ["V", "ex"], [vk])
            b = bank("s")
            for kt in range(nkt):
                mm(psf[b][:, kt * w_:(kt + 1) * w_], KTM(kt, c), Vw[:, j, kt * w_:(kt + 1) * w_], True, True,
                   ["KTM", vk], [pk(b)])
            return b

        def st_post(c, d, cur, b, last_out):
            tt("dve", v3(tmps[:]), v3(Sst[:, cur, :]), bc3(dcc[:, c, d * nh:(d + 1) * nh]), MUL,
               ["S%d" % cur, "dcc"], ["tmps"])
            tt("dve", Sst[:, 1 - cur, :], tmps[:], psf[b][:, :], ADD, ["tmps", pk(b)], ["S%d" % (1 - cur)])
            if last_out is not None:
                dma("sp", last_out, Sst[:, 1 - cur, :], ["S%d" % (1 - cur)], [], "so%d" % (1 - cur))

        if dbg.get("ssd_stop") == 2:
            return
        setpools(s=[1, 2, 3])
        dma("sp", Sst[:, 0, :], s_in[0], (), ["S0"], "sin")
        cur = 0
        bn = st_pre(0, n2)
        for c in range(NT):
            bcur = bn
            if c + 1 < NT:
                bn = st_pre(c + 1, n2)
            act(Sbf[:, c, :], Sst[:, cur, :], AF.Copy, ["S%d" % cur, "flg"], ["Sbf%d" % c], scale=flg[:, c:c + 1])
            st_post(c, 0, cur, bcur, s_out[0, c // 2] if c % 2 == 1 else None)
            cur = 1 - cur
        if dbg.get("ssd_stop") == 3:
            return
        setpools(row=[0, 1], g=[2], o=[3], f=[4], bb=[5], s=[6])
        dma("sp", Sst[:, cur, :], s_in[1], (), ["S%d" % cur], "sin")
        state = {"cur": cur}

        def stageA(c):
            cur = state["cur"]
            sk = "Sbb%d" % (c % 2)
            act(Sbb[:, c % 2, :], Sst[:, cur, :], AF.Copy, ["S%d" % cur, "flg"], [sk], scale=flg[:, 16 + c:17 + c])
            bs_ = st_pre(c, n2 + nh)
            st_post(c, 1, cur, bs_, s_out[1, c // 2] if c % 2 == 0 else None)
            state["cur"] = 1 - cur
            return sk

        def stageR(c):
            pb = 0 if const_decay else c % 2
            Dm, LL = Dm2[pb], LL2[pb]
            dmk, llk = "Dm%d" % pb, "LL%d" % pb
            for q0 in range(0, n2, 4):
                b = bank("row")
                for q in range(q0, q0 + 4):
                    o_ = psf[b][:, (q - q0) * 128:(q - q0 + 1) * 128]
                    tr_ = triUb if q < nh else ntriSb
                    mm(o_, la_hi[:, c, q:q + 1].to_broadcast([128, 128]), tr_, True, False, ["lahl", "cb"], [pk(b)])
                    mm(o_, la_lo[:, c, q:q + 1].to_broadcast([128, 128]), tr_, False, False, ["lahl", "cb"], [pk(b)])
                    mm(o_, identb, nmUb if q < nh else nmLb, False, True, ["cb"], [pk(b)])
                tt("dve", Dm[:, q0:q0 + 4, :], psf[b][:, :].rearrange("p (q i) -> p q i", q=4),
                   bia[:, c, q0:q0 + 4].unsqueeze(2).to_broadcast([128, 4, 128]), ADD, [pk(b), "bia"], [dmk + "_%d" % q0])
            dks = [dmk + "_%d" % q0 for q0 in range(0, n2, 4)]
            act(Dm[:, :, :], Dm[:, :, :], AF.Exp, dks, dks)
            tt("dve", LL[:, :, :], Dm[:, 0:nh, :], Dm[:, nh:n2, :], ADD, dks, [llk])

        def stageA2(c, sk):
            pb = c % 2
            pl = 0 if const_decay else pb
            LL, MT, Oa, tmpo = LL2[pl], MT2[pb], Oa2[pb], tmpo2[pb]
            llk, mtk, oak, tmk = "LL%d" % pl, "MT%d" % pb, "Oa%d" % pb, "tmpo%d" % pb
            bg = bank("g")
            for kt in range(nkt):
                for hf in range(2):
                    i_ = kt * 2 + hf
                    mm(psf[bg][:, i_ * 128:(i_ + 1) * 128], KT(kt, c), QT(kt, hf, c), True, True, ["QK"], [pk(bg)])
            for kt in range(nkt):
                for hf in range(2):
                    i_ = kt * 2 + hf
                    h0 = kt * hkt + hf * hph
                    if hph > 1:
                        tt("dve", MT[:, h0:h0 + hph, :], LL[:, h0:h0 + hph, :],
                           psf[bg][:, i_ * 128:(i_ + 1) * 128].unsqueeze(1).to_broadcast([128, hph, 128]), MUL,
                           [llk, pk(bg)], [mtk + "_%d" % i_])
                    else:
                        tt("dve", MT[:, h0, :], LL[:, h0, :], psf[bg][:, i_ * 128:(i_ + 1) * 128], MUL,
                           [llk, pk(bg)], [mtk + "_%d" % i_])
            bo, bf_, bb_ = bank("o"), bank("f"), bank("bb")
            for h in range(nh):
                mm(psf[bo][:, h * dv:(h + 1) * dv], MT[:, h, :], Vt[:, c, h * dv:(h + 1) * dv], True, True,
                   [mtk + "_%d" % ((h // hkt) * 2 + (h % hkt) // hph), "V"], [pk(bo)])
            for kt in range(nkt):
                for hf in range(2):
                    c0 = kt * hkt * dv + hf * hph * dv
                    c1 = c0 + hph * dv
                    mm(psf[bf_][:, c0:c1], QT(kt, hf, c), Sbf[:, c, c0:c1], True, True, ["QK", "Sbf%d" % c], [pk(bf_)])
                    mm(psf[bb_][:, c0:c1], QT(kt, hf, c), Sbb[:, c % 2, c0:c1], True, True, ["QK", sk], [pk(bb_)])
            act(Oa[:], psf[bo][:, :], AF.Copy, [pk(bo)], [oak])
            tt("dve", v3(tmpo[:]), v3(psf[bf_][:, :]), bc3(ex[:, c, 0:nh]), MUL, [pk(bf_), "ex"], [tmk])
            tt("dve", v3(tmps2[pb][:]), v3(psf[bb_][:, :]), bc3(ex[:, c, nh:n2]), MUL, [pk(bb_), "ex"], ["tq%d" % pb])
            return (Oa, tmpo, oak, tmk, pb)

        def stageB(c, ctx):
            Oa, tmpo, oak, tmk, pb = ctx
            tt("dve", tmpo[:], tmpo[:], tmps2[pb][:], ADD, [tmk, "tq%d" % pb], [tmk])
            tt("dve", Oa[:], Oa[:], tmpo[:], ADD, [oak, tmk], [oak])
            epilogue(c, Oa, tmpo, oak, tmk)

        prev = None
        stageR(NT - 1)
        if not const_decay:
            stageR(NT - 2)
        for c in range(NT - 1, -1, -1):
            sk = stageA(c)
            ctx = stageA2(c, sk)
            if c >= 2 and not const_decay:
                stageR(c - 2)
            if prev is not None:
                stageB(*prev)
            prev = (c, ctx)
        stageB(*prev)

    def to_fm(S, c, oTM, ok, dst, dk):
        for k in range(4):
            tp(psb[:, k * 128:(k + 1) * 128], oTM[:, k * 128:(k + 1) * 128], identb, [ok, "cb"], ["psb"])
        cp("dve", dst[:, 0:4, c * 128:(c + 1) * 128], psb[:, 0:512].rearrange("p (k t) -> p k t", k=4), ["psb"], [dk])

    def phase_ssd(l, o_ssd):
        with Scope() as S1:
            xsTM = S1.sb("xsTM", [128, NT, 512], BF16)
            BT = S1.sb("BT", [128, T], BF16)
            CT = [S1.sb("CT%d" % i, [128, T], BF16) for i in range(2)]
            mset("pool", CT[0][64:128, :], 0.0, ["QK"])
            mset("pool", CT[1][0:64, :], 0.0, ["QK"])
            BTM = S1.sb("BTM", [128, NT, 128], BF16)
            zs = S1.sb("zs", [128, NT, 512], BF16)
            la = S1.sb("la", [128, NT, 16])
            lnw = S1.sb("lnw", [128, NT, 16])
            nw = S1.sb("nw", [128, 12])
            with Scope() as S:
                setpools(a=[0, 1, 2, 3], t=[4, 5])
                W = S.sb("W", [128, 8, 1296], BF16)
                raw = [S.sb("raw%d" % i, [128, T + 2]) for i in range(2)]
                ycv = S.sb("ycv", [128, T])
                xact = S.sb("xact", [128, T], BF16)
                dtr = S.sb("dtr", [128, NT, 16])
                ea = S.sb("ea", [128, 16])
                load_w(W, D["win"][l, :, :, 0:1296], 1296, order=[2, 0, 1, 3, 4, 5])
                for i in range(2):
                    mset("pool", raw[i][:, 0:1], 0.0, ["raw%d" % i])
                    mset("pool", raw[i][:, T + 1:T + 2], 0.0, ["raw%d" % i])
                stt("dve", nw[:, 0:6], V(l, "ssd_cw", 6), -1.0, flg[:, 32:33].to_broadcast([128, 6]), MUL, MUL,
                    ["vec", "flg"], ["nw"])
                stt("dve", nw[:, 6:12], vec[:, l, _VOFF["ssd_cw"] + 12:_VOFF["ssd_cw"] + 18], -1.0,
                    flg[:, 32:33].to_broadcast([128, 6]), MUL, MUL, ["vec", "flg"], ["nw"])
                cw0 = _VOFF["ssd_cw"]
                zq = []
                for ch in range(6):
                    rw = raw[ch % 2]
                    rk_ = "raw%d" % (ch % 2)
                    for g in range(NG):
                        b = proj_fm(W, "W", 512 + ch * 128, 128, g, "a")
                        act(rw[:, 1 + g * 512:1 + (g + 1) * 512], psf[b][:, :], AF.Copy, [pk(b)], [rk_])
                    conv(rw, rk_, vec[:, l, cw0 + ch:cw0 + ch + 1], vec[:, l, cw0 + 6 + ch:cw0 + 7 + ch],
                         vec[:, l, cw0 + 12 + ch:cw0 + 13 + ch], V(l, "ssd_cb", 6)[:, ch:ch + 1],
                         nw[:, ch:ch + 1], nw[:, 6 + ch:7 + ch], ycv, "ycv")
                    for t in range(ch * 3, min(NT, ch * 3 + 3)):
                        bz = bank("a")
                        proj_tm(W, "W", 0, 512, t, bz)
                        zq.append((t, bz))
                    dstx = xact if ch < 4 else BT
                    dk = "xact" if ch < 4 else ("QK")
                    if ch < 5:
                        act(dstx[:, :], ycv[:, :], AF.Silu, ["ycv"], [dk])
                    else:
                        act(CT[0][0:64, :], ycv[0:64, :], AF.Silu, ["ycv"], [dk])
                        act(CT[1][64:128, :], ycv[64:128, :], AF.Silu, ["ycv"], [dk])
                    while zq:
                        t_, bz_ = zq.pop(0)
                        act(zs[:, t_, :], psf[bz_][:, :], AF.Silu, [pk(bz_)], ["zs"])
                    if ch < 5:
                        for t0 in range(0, NT, 8):
                            for j in range(8):
                                tp(psb[:, j * 128:(j + 1) * 128], dstx[:, (t0 + j) * 128:(t0 + j + 1) * 128], identb,
                                   [dk, "cb"], ["psb"])
                            if ch < 4:
                                cp("dve", xsTM[:, t0:t0 + 8, ch * 128:(ch + 1) * 128],
                                   psb[:, :].rearrange("p (j t) -> p j t", j=8), ["psb"], ["V"])
                            else:
                                cp("dve", BTM[:, t0:t0 + 8, :], psb[:, :].rearrange("p (j t) -> p j t", j=8),
                                   ["psb"], ["KTM"])
                b = bank("t")
                for t in range(NT):
                    proj_tm(W, "W", 1280, 16, t, b, o0=t * 16)
                tt("dve", dtr[:, :, :], psf[b][:, 0:256].rearrange("p (t q) -> p t q", t=NT),
                   V(l, "dt_bias", 16).unsqueeze(1).to_broadcast([128, NT, 16]), ADD, [pk(b), "vec"], ["dtr"])
                act(dtr[:, :, :], dtr[:, :, :], AF.Exp, ["dtr"], ["dtr"])
                act(dtr[:, :, :], dtr[:, :, :], AF.Ln, ["dtr"], ["dtr"], bias=1.0)
                act(lnw[:, :, :], dtr[:, :, :], AF.Ln, ["dtr"], ["la"])
                act(ea[:], V(l, "a_log", 16), AF.Exp, ["vec"], ["ea"])
                stt("dve", la[:, :, :], dtr[:, :, :], -1.0, ea[:].unsqueeze(1).to_broadcast([128, NT, 16]), MUL, MUL,
                    ["dtr", "ea"], ["la"])
                for t in range(18, NT):
                    b = bank("a")
                    proj_tm(W, "W", 0, 512, t, b)
                    act(zs[:, t, :], psf[b][:, :], AF.Silu, [pk(b)], ["zs"])
            if dbg.get("ssd_stop") == 1:
                return
            with Scope() as S:
                st2 = S.sb("st2", [128, 4])
                oTM2 = [S.sb("oTM%d" % i, [128, 512], BF16) for i in range(2)]

                def epi(c, Oa, tmpo, oak, tmk):
                    oTM, otk = oTM2[c % 2], "oTM%d" % (c % 2)
                    tt("dve", tmpo[:].rearrange("p (h e) -> p h e", h=8), xsTM[:, c, :].rearrange("p (h e) -> p h e", h=8),
                       V(l, "ssd_d", 8).unsqueeze(2).to_broadcast([128, 8, 64]), MUL, ["V", "vec"], [tmk])
                    tt("dve", Oa[:], Oa[:], tmpo[:], ADD, [oak, tmk], [oak])
                    tt("dve", Oa[:], Oa[:], zs[:, c, :], MUL, [oak, "zs"], [oak])
                    mset("dve", st2[:, 0:1], 0.0, ["st2"])
                    act(junk[:, 0:512], Oa[:], AF.Square, [oak], ["junk", "st2"], accum=st2[:, 0:1])
                    act(st2[:, 1:2], st2[:, 0:1], AF.Ln, ["st2"], ["st2"], bias=EPS, scale=1.0 / 512)
                    act(st2[:, 2:3], st2[:, 1:2], AF.Exp, ["st2"], ["st2"], scale=-0.5)
                    stt("dve", oTM[:], Oa[:], st2[:, 2:3], V(l, "ssd_nw", 512), MUL, MUL, [oak, "st2", "vec"], [otk])
                    to_fm(S, c, oTM, otk, o_ssd, "o_ssd")
                scan(S, l, 8, 1, 64,
                     lambda kt, hf, c: CT[hf][:, c * 128:(c + 1) * 128],
                     lambda kt, c: BT[:, c * 128:(c + 1) * 128],
                     lambda kt, c: BTM[:, c, :], xsTM, la, lnw, D["s_ssd"][l], O["o_ssd"][l], epi)

    def rope_fm(S, b_x, b_p, rows, g, rope, dst, p, dk, pre=1.0, tmp=None):
        t1, t2 = tmp
        act(t1[rows, :], psf[b_x][rows, :], AF.Copy, [pk(b_x)], ["t1"], scale=pre)
        act(t2[rows, :], psf[b_p][rows, :], AF.Copy, [pk(b_p)], ["t2"], scale=pre)
        tt("dve", t1[rows, :], t1[rows, :], rope[rows, 0, g * 512:(g + 1) * 512], MUL, ["t1", "rope"], ["t1"])
        tt("pool", t2[rows, :], t2[rows, :], rope[rows, 1, g * 512:(g + 1) * 512], MUL, ["t2", "rope"], ["t2"])
        gs = slice(g * 512, (g + 1) * 512)
        if isinstance(dst, list):
            tt("dve", dst[0][0:64, p, gs], t1[0:64, :], t2[0:64, :], ADD, ["t1", "t2"], [dk])
            tt("dve", dst[1][64:128, p, gs], t1[64:128, :], t2[64:128, :], ADD, ["t1", "t2"], [dk])
        else:
            tt("dve", dst[:, p, gs], t1[rows, :], t2[rows, :], ADD, ["t1", "t2"], [dk])

    def phase_ret(l, o_ret):
        with Scope() as S1:
            rqT = [S1.sb("rqT%d" % i, [128, 2, T], BF16) for i in range(2)]
            mset("pool", rqT[0][64:128, :, :], 0.0, ["QK"])
            mset("pool", rqT[1][0:64, :, :], 0.0, ["QK"])
            rkT = S1.sb("rkT", [128, 2, T], BF16)
            rkTM = S1.sb("rkTM", [128, NT, 256], BF16)
            rvTM = S1.sb("rvTM", [128, NT, 512], BF16)
            rgs = S1.sb("rgs", [128, NT, 512], BF16)
            la = S1.sb("la", [128, NT, 8])
            with Scope() as S:
                setpools(a=[0, 1, 2, 3], t=[4, 5])
                W = S.sb("W", [128, 8, 1024], BF16)
                rope = S.sb("rope", [128, 2, T], BF16)
                t1 = S.sb("t1", [128, 512])
                t2 = S.sb("t2", [128, 512])
                l8 = S.sb("l8", [128, 8])
                dma("pool", rope[:], D["rope64"][:, :, :], (), ["rope"], "rope")
                load_w(W, D["win"][l, :, :, 1296:2320], 1024)
                Wb = S.sb("Wb", [128, 8, 1024], BF16)
                for i_ in range(2):
                    dma("pool", Wb[:, :, i_ * 512:(i_ + 1) * 512], D["win"][l, :, :, 2320 + i_ * 512:2320 + (i_ + 1) * 512],
                        (), ["Wb%d" % i_], "Wb%d" % i_)
                act(l8[:], V(l, "ret_logit", 8), AF.Exp, ["vec"], ["l8"], scale=-1.0)
                act(l8[:], l8[:], AF.Ln, ["l8"], ["l8"], bias=1.0)
                ts("dve", la[:, :, :], l8[:].unsqueeze(1).to_broadcast([128, NT, 8]), -1.0, MUL, ["l8"], ["la"])
                ti = [0]
                for (c0, dst, pre) in ((0, rqT, 1.0), (512, rkT, 0.125)):
                    for p in range(2):
                        for g in range(NG):
                            bx = proj_fm(W, "W", c0 + p * 128, 128, g, "a")
                            bp = proj_fm(W, "W", c0 + 256 + p * 128, 128, g, "a")
                            rope_fm(S, bx, bp, slice(0, 128), g, rope, dst, p, "QK", pre, (t1, t2))
                            t_ = ti[0]
                            ti[0] += 1
                            if t_ < NT:
                                b = bank("a")
                                proj_tm(Wb, "Wb0", 0, 512, t_, b)
                                cp("dve", rvTM[:, t_, :], psf[b][:, :], [pk(b)], ["V"])
                                b = bank("a")
                                proj_tm(Wb, "Wb1", 512, 512, t_, b)
                                act(rgs[:, t_, :], psf[b][:, :], AF.Silu, [pk(b)], ["rgs"])
                for p in range(2):
                    for t0 in range(0, NT, 8):
                        for j in range(8):
                            tp(psb[:, j * 128:(j + 1) * 128], rkT[:, p, (t0 + j) * 128:(t0 + j + 1) * 128], identb,
                               ["QK", "cb"], ["psb"])
                        cp("dve", rkTM[:, t0:t0 + 8, p * 128:(p + 1) * 128], psb[:, :].rearrange("p (j t) -> p j t", j=8),
                           ["psb"], ["KTM"])
                for t in range(ti[0], NT):
                    b = bank("a")
                    proj_tm(Wb, "Wb0", 0, 512, t, b)
                    cp("dve", rvTM[:, t, :], psf[b][:, :], [pk(b)], ["V"])
                    b = bank("a")
                    proj_tm(Wb, "Wb1", 512, 512, t, b)
                    act(rgs[:, t, :], psf[b][:, :], AF.Silu, [pk(b)], ["rgs"])
            with Scope() as S:
                st4 = S.sb("st4", [128, 4, 4])
                oTM2 = [S.sb("oTM%d" % i, [128, 512], BF16) for i in range(2)]

                def epi(c, Oa, tmpo, oak, tmk):
                    oTM, otk = oTM2[c % 2], "oTM%d" % (c % 2)
                    o3 = Oa[:].rearrange("p (h e) -> p h e", h=4)
                    rsum(st4[:, 0, :], o3, [oak], ["st4"])
                    act(junk[:, 0:512], Oa[:], AF.Square, [oak], ["junk"])
                    rsum(st4[:, 1, :], junk[:, 0:512].rearrange("p (h e) -> p h e", h=4), ["junk"], ["st4"])
                    ts("dve", st4[:, 0, :], st4[:, 0, :], 1.0 / 128, MUL, ["st4"], ["st4"])
                    tt("dve", st4[:, 2, :], st4[:, 0, :], st4[:, 0, :], MUL, ["st4"], ["st4"])
                    stt("dve", st4[:, 1, :], st4[:, 1, :], 1.0 / 128, st4[:, 2, :], MUL, SUB, ["st4"], ["st4"])
                    act(st4[:, 2, :], st4[:, 1, :], AF.Ln, ["st4"], ["st4"], bias=EPS)
                    act(st4[:, 3, :], st4[:, 2, :], AF.Exp, ["st4"], ["st4"], scale=-0.5)
                    tt("dve", o3, o3, st4[:, 0, :].unsqueeze(2).to_broadcast([128, 4, 128]), SUB, [oak, "st4"], [oak])
                    tt("dve", o3, o3, st4[:, 3, :].unsqueeze(2).to_broadcast([128, 4, 128]), MUL, [oak, "st4"], [oak])
                    tt("dve", Oa[:], Oa[:], V(l, "ret_gw", 512), MUL, [oak, "vec"], [oak])
                    tt("dve", oTM[:], Oa[:], rgs[:, c, :], MUL, [oak, "rgs"], [otk])
                    to_fm(S, c, oTM, otk, o_ret, "o_ret")
                scan(S, l, 4, 2, 128,
                     lambda kt, hf, c: rqT[hf][:, kt, c * 128:(c + 1) * 128],
                     lambda kt, c: rkT[:, kt, c * 128:(c + 1) * 128],
                     lambda kt, c: rkTM[:, c, kt * 128:(kt + 1) * 128], rvTM, la, None,
                     D["s_ret"][l], O["o_ret"][l], epi, const_decay=True)

    def attention(S, nheads, krows, scale, QTf, KTf, Vf, dst, par):
        setpools(s=[0, 1, 2, 3], o=[4, 5], n=[6])
        NPT = 7
        DEPTH_ = 5
        PT = [S.sb("PT%d" % i, [128, 512], BF16) for i in range(NPT)]
        Osb2 = [S.sb("Osb%d" % i, [128, 512]) for i in range(2)]
        rec2 = [S.sb("rec%d" % i, [128, 512]) for i in range(2)]
        seq = [(h, qg, kb) for h in range(nheads) for qg in range(NG) for kb in range(18)]
        cur_o = {}
        fin_q = []
        FIN_DELAY = 10

        def emit_pv(n):
            h, qg, kb = seq[n]
            odd = par(h)
            if kb == 0:
                cur_o[(h, qg)] = bank("o")
            bo = cur_o[(h, qg)]
            pt, ptk = PT[n % NPT], "PT%d" % (n % NPT)
            v_ = Vf(h, kb)
            if odd:
                mm(psf[bo][0:128, :], v_[:, 64:192], pt[:], kb == 0, kb == 17, ["Vx", ptk], [pk(bo)])
            else:
                mm(psf[bo][0:65, :], v_[:, 0:65], pt[:], kb == 0, kb == 17, ["Vx", ptk], [pk(bo)])
            if kb < 17:
                return
            pi = (h * NG + qg) % 2
            rec, Osb = rec2[pi], Osb2[pi]
            rk_, ok_ = "rec%d" % pi, "Osb%d" % pi
            if odd:
                recip(rec[0:1, :], psf[bo][0:1, :], [pk(bo)], [rk_])
                R_ = slice(64, 128)
            else:
                recip(rec[64:65, :], psf[bo][64:65, :], [pk(bo)], [rk_])
                R_ = slice(0, 64)
            act(Osb[R_, :], psf[bo][R_, :], AF.Copy, [pk(bo)], [ok_])

            def fin():
                bn = bank("n")
                if odd:
                    mm(psf[bn][0:128, :], onesf[0:1, 0:128], rec[0:1, :], True, True, [rk_, "cf"], [pk(bn)])
                else:
                    mm(psf[bn][0:64, :], onesf[64:65, 0:64], rec[64:65, :], True, True, [rk_, "cf"], [pk(bn)])
                tt("dve", dst(h, qg), Osb[R_, :], psf[bn][R_, :], MUL, [ok_, pk(bn)], ["oatt"])
            fin_q.append((n + FIN_DELAY, fin))

        for n, (h, qg, kb) in enumerate(seq):
            q_, k_ = QTf(h), KTf(h)
            bs = bank("s")
            mm(psf[bs][:, :], k_[0:krows, kb * 128:(kb + 1) * 128], q_[0:krows, qg * 512:(qg + 1) * 512], True, True,
               ["Q", "K", "Qaug", "Kaug"], [pk(bs)])
            act(PT[n % NPT][:], psf[bs][:, :], AF.Exp, [pk(bs)], ["PT%d" % (n % NPT)], scale=scale)
            if n >= DEPTH_:
                emit_pv(n - DEPTH_)
                while fin_q and fin_q[0][0] <= n - DEPTH_:
                    fin_q.pop(0)[1]()
        for n in range(max(0, len(seq) - DEPTH_), len(seq)):
            emit_pv(n)
        while fin_q:
            fin_q.pop(0)[1]()

    def headnorm(S, b_x, rows, ones_ap, nacc, tmp):
        sq, rs = tmp
        bm = bank("n")
        for i, b in enumerate(b_x):
            act(sq[rows, i, :], psf[b][rows, :], AF.Square, [pk(b)], ["sq"])
        for i, b in enumerate(b_x):
            mm(psf[bm][rows, :], ones_ap, sq[rows, i, :], i == 0, i == len(b_x) - 1, ["sq", "cb"], [pk(bm)])
        act(rs[rows, :], psf[bm][rows, :], AF.Ln, [pk(bm)], ["rs"], bias=EPS)
        act(rs[rows, :], rs[rows, :], AF.Exp, ["rs"], ["rs"], scale=-0.5)

    def phase_gqa(l, o_att):
      for hb in range(2):
        with Scope() as S1:
            QTb = S1.sb("QTb", [128, 4, T], BF16)
            KTb = S1.sb("KTb", [128, T + 256], BF16)
            Vx = S1.sb("Vx", [128, 18, 192], BF16)
            with Scope() as S:
                setpools(a=[0, 1, 2, 3], n=[4], t=[5, 6])
                W = S.sb("W", [128, 8, 1408], BF16)
                rope = S.sb("rope", [128, 2, T], BF16)
                t1 = S.sb("t1", [128, 512]); t2 = S.sb("t2", [128, 512])
                sq = S.sb("sq", [128, 1, 512], BF16); rs = S.sb("rs", [128, 512])
                kst = S.sb("kst", [128, NT, 64]); vst = S.sb("vst", [128, NT, 64])
                dma("pool", rope[:], D["rope64"][:, :, :], (), ["rope"], "rope")
                load_w(W, D["win"][l, :, :, 3344:4752], 1408, order=[hb, 2 + hb, 4, 5])
                for j in range(4):
                    dma("pool", QTb[64:73, j, :], D["augq"][:, :], (), ["Qaug"], "aug")
                dma("pool", KTb[64:73, :], D["augk"][:, :], (), ["Kaug"], "aug")
                dma("pool", KTb[0:64, 0:256], D["ckT"][l, hb], (), ["Kaug"], "aug")
                mset("pool", Vx[:, :, 64:65], 1.0, ["Vx"])
                mset("pool", Vx[:, :, 65:128], 0.0, ["Vx"])
                dma("pool", Vx[:, 0:2, 0:64], D["cv"][l][:, hb * 64:(hb + 1) * 64].rearrange("(kb p) d -> p kb d", p=128),
                    (), ["Vx"], "aug")
                dma("pool", Vx[:, 0:2, 128:192], D["cv"][l][:, hb * 64:(hb + 1) * 64].rearrange("(kb p) d -> p kb d", p=128),
                    (), ["Vx"], "aug")
                P.seal("aug")
                R = slice(0, 64)
                for (isk, hl, c0, cp0, nname, npname) in [(False, j, 0, 512, "qn", "qnp") for j in range(4)] + \
                        [(True, 0, 1024, 1152, "kn", "knp")]:
                    hg = (hb * 4 + hl) if not isk else hb
                    for g in range(NG):
                        bx = proj_fm(W, "W", c0 + hg * 64, 64, g, "a")
                        bp = proj_fm(W, "W", cp0 + hg * 64, 64, g, "a")
                        headnorm(S, [bx], R, blk64[0:64, 0:64], 1, (sq, rs))
                        act(t1[R, :], psf[bx][R, :], AF.Copy, [pk(bx), "vec"], ["t1"], scale=V(l, nname, 1, 64))
                        act(t2[R, :], psf[bp][R, :], AF.Copy, [pk(bp), "vec"], ["t2"], scale=V(l, npname, 1, 64))
                        tt("dve", t1[R, :], t1[R, :], rs[R, :], MUL, ["t1", "rs"], ["t1"])
                        tt("dve", t2[R, :], t2[R, :], rs[R, :], MUL, ["t2", "rs"], ["t2"])
                        if isk:
                            bt = bank("t")
                            for j in range(4):
                                tp(psf[bt][:, j * 64:(j + 1) * 64], t1[R, j * 128:(j + 1) * 128], identf[0:64, 0:64],
                                   ["t1", "cf"], [pk(bt)])
                            cp("dve", kst[:, g * 4:(g + 1) * 4, :], psf[bt][:, 0:256].rearrange("p (j d) -> p j d", j=4),
                               [pk(bt)], ["kst"])
                        tt("dve", t1[R, :], t1[R, :], rope[R, 0, g * 512:(g + 1) * 512], MUL, ["t1", "rope"], ["t1"])
                        tt("pool", t2[R, :], t2[R, :], rope[R, 1, g * 512:(g + 1) * 512], MUL, ["t2", "rope"], ["t2"])
                        if isk:
                            tt("dve", KTb[R, 256 + g * 512:256 + (g + 1) * 512], t1[R, :], t2[R, :], ADD, ["t1", "t2"], ["K"])
                        else:
                            tt("dve", QTb[R, hl, g * 512:(g + 1) * 512], t1[R, :], t2[R, :], ADD, ["t1", "t2"], ["Q"])
                dma("sp", O["o_ck"][l][:, hb * 64:(hb + 1) * 64].rearrange("(t p) f -> p t f", p=128), kst[:, :, :],
                    ["kst"], [], "kst")
                for t in range(NT):
                    b = bank("a")
                    proj_tm(W, "W", 1280 + hb * 64, 64, t, b)
                    act(vst[:, t, :], psf[b][:, 0:64], AF.Copy, [pk(b)], ["vst"])
                    cp("dve", Vx[:, 2 + t, 0:64], psf[b][:, 0:64], [pk(b)], ["Vx"])
                    cp("dve", Vx[:, 2 + t, 128:192], psf[b][:, 0:64], [pk(b)], ["Vx"])
                dma("sp", O["o_cv"][l][:, hb * 64:(hb + 1) * 64].rearrange("(t p) f -> p t f", p=128), vst[:, :, :],
                    ["vst"], [], "vst")
            with Scope() as S:
                attention(S, 4, 73, 0.125, lambda j: QTb[:, j, :], lambda j: KTb[:, :],
                          lambda j, kb: Vx[:, kb, :],
                          lambda j, qg: o_att[(j % 2) * 64:(j % 2) * 64 + 64, (hb * 4 + j) // 2, qg * 512:(qg + 1) * 512],
                          lambda j: j % 2)

    def phase_mla(l, o_mla):
        with Scope() as S1:
            mcqn = S1.sb("mcqn", [128, 3, T], BF16)
            ckvT = S1.sb("ckvT", [128, 2, T + 256], BF16)
            krT = S1.sb("krT", [128, T + 256], BF16)
            rope = S1.sb("rope", [128, 2, T], BF16)
            wuq = S1.sb("wuq", [128, 3, 1536], BF16)
            wuk = S1.sb("wuk", [128, 2, 512], BF16)
            wuv = S1.sb("wuv", [128, 2, 512], BF16)
            dma("pool", rope[:], D["rope32"][:, :, :], (), ["rope"], "rope")
            dma("pool", wuq[:], D["wuq"][l], (), ["wu"], "wuq")
            dma("pool", wuk[:], D["wuk"][l], (), ["wu"], "wuk")
            dma("pool", wuv[:], D["wuv"][l], (), ["wu"], "wuv")
            for c in range(2):
                dma("pool", ckvT[:, c, 0:256], D["cckvT"][l, c * 128:(c + 1) * 128, :], (), ["ckvT"], "ck%d" % c)
            dma("pool", krT[0:32, 0:256], D["ckrT"][l], (), ["krT"], "ckr")
            with Scope() as S:
                setpools(a=[0, 1, 2], n=[3], t=[4, 5], x=[6])
                W = S.sb("W", [128, 8, 704], BF16)
                t1 = S.sb("t1", [128, 512]); t2 = S.sb("t2", [128, 512])
                sq = S.sb("sq", [128, 3, 512], BF16); rs = S.sb("rs", [128, 512])
                cst = S.sb("cst", [128, 4, 256]); kst = S.sb("kst", [128, NT, 32])
                load_w(W, D["win"][l, :, :, 4752:5456], 704)
                A_ = slice(0, 128)
                for g in range(NG):
                    bs = [proj_fm(W, "W", c * 128, 128, g, "a") for c in range(3)]
                    headnorm(S, bs, A_, o384, 3, (sq, rs))
                    for c in range(3):
                        stt("dve", mcqn[:, c, g * 512:(g + 1) * 512], psf[bs[c]][:, :], V(l, "mqn", 3)[:, c:c + 1], rs[:, :],
                            MUL, MUL, [pk(bs[c]), "rs", "vec"], ["mcqn"])
                    bs = [proj_fm(W, "W", 384 + c * 128, 128, g, "a") for c in range(2)]
                    headnorm(S, bs, A_, o256, 2, (sq, rs))
                    for c in range(2):
                        stt("dve", t1[:, :], psf[bs[c]][:, :], V(l, "mkvn", 2)[:, c:c + 1], rs[:, :], MUL, MUL,
                            [pk(bs[c]), "rs", "vec"], ["t1"])
                        cp("pool", ckvT[:, c, 256 + g * 512:256 + (g + 1) * 512], t1[:, :], ["t1"], ["ckvT"])
                        bt = bank("t")
                        for j in range(4):
                            tp(psf[bt][:, j * 128:(j + 1) * 128], t1[:, j * 128:(j + 1) * 128], identf, ["t1", "cf"], [pk(bt)])
                        cp("dve", cst[:, :, c * 128:(c + 1) * 128], psf[bt][:, :].rearrange("p (j d) -> p j d", j=4),
                           [pk(bt)], ["cst"])
                    dma("sp", O["o_ckv"][l, g * 512:(g + 1) * 512, :].rearrange("(t p) f -> p t f", p=128), cst[:, :, :],
                        ["cst"], [], "cst")
                    R = slice(0, 32)
                    bx = proj_fm(W, "W", 640, 32, g, "x")
                    act(t1[R, :], psf[bx][R, :], AF.Copy, [pk(bx)], ["t1"])
                    bp = proj_fm(W, "W", 672, 32, g, "x")
                    act(t2[R, :], psf[bp][R, :], AF.Copy, [pk(bp)], ["t2"])
                    bt = bank("t")
                    for j in range(4):
                        tp(psf[bt][:, j * 32:(j + 1) * 32], t1[R, j * 128:(j + 1) * 128], identf[0:32, 0:32], ["t1", "cf"], [pk(bt)])
                    cp("dve", kst[:, g * 4:(g + 1) * 4, :], psf[bt][:, 0:128].rearrange("p (j d) -> p j d", j=4), [pk(bt)], ["kst"])
                    tt("dve", t1[R, :], t1[R, :], rope[R, 0, g * 512:(g + 1) * 512], MUL, ["t1", "rope"], ["t1"])
                    tt("pool", t2[R, :], t2[R, :], rope[R, 1, g * 512:(g + 1) * 512], MUL, ["t2", "rope"], ["t2"])
                    tt("dve", krT[R, 256 + g * 512:256 + (g + 1) * 512], t1[R, :], t2[R, :], ADD, ["t1", "t2"], ["krT"])
                dma("sp", O["o_kr"][l].rearrange("(t p) f -> p t f", p=128), kst[:, :, :], ["kst"], [], "kst")
            for hb in range(4):
                with Scope() as S2:
                    QM = S2.sb("QM", [128, 2, T], BF16)
                    KM = S2.sb("KM", [128, 2, T + 256], BF16)
                    VM = S2.sb("VM", [128, 18, 2, 192], BF16)
                    if True:
                        S = S2
                        setpools(a=[0, 1, 2, 3], b=[4, 5])
                        t1 = S.sb("t1", [128, 512]); t2 = S.sb("t2", [128, 512])
                        mset("pool", VM[:, :, :, 64:65], 1.0, ["Vx"])
                        mset("pool", VM[:, :, :, 65:128], 0.0, ["Vx"])
                        for j in range(2):
                            dma("pool", QM[96:105, j, :], D["augq"][:, :], (), ["Qaug"], "aug")
                            dma("pool", KM[96:105, j, :], D["augk"][:, :], (), ["Kaug"], "aug")
                        P.seal("aug")
                        for j in range(2):
                            h = hb * 2 + j
                            dma("sp", KM[64:96, j, :], krT[0:32, :], ["krT"], ["Kaug"], "krc%d" % j)
                            for kg in range(5):
                                n_ = 512 if kg < 4 else 256
                                b = bank("a")
                                for c in range(2):
                                    mm(psf[b][0:64, 0:n_], wuk[:, c, h * 64:(h + 1) * 64], ckvT[:, c, kg * 512:kg * 512 + n_],
                                       c == 0, c == 1, ["wu", "ckvT"], [pk(b)])
                                act(KM[0:64, j, kg * 512:kg * 512 + n_], psf[b][0:64, 0:n_], AF.Copy, [pk(b)], ["K"])
                            for g in range(NG):
                                ba = bank("a")
                                bb2 = bank("b")
                                for c in range(3):
                                    mm(psf[ba][0:96, :], wuq[:, c, h * 192:h * 192 + 96], mcqn[:, c, g * 512:(g + 1) * 512],
                                       c == 0, c == 2, ["wu", "mcqn"], [pk(ba)])
                                for c in range(3):
                                    mm(psf[bb2][0:96, :], wuq[:, c, h * 192 + 96:h * 192 + 192], mcqn[:, c, g * 512:(g + 1) * 512],
                                       c == 0, c == 2, ["wu", "mcqn"], [pk(bb2)])
                                act(QM[0:64, j, g * 512:(g + 1) * 512], psf[ba][0:64, :], AF.Copy, [pk(ba)], ["Q"])
                                R2 = slice(64, 96)
                                tt("dve", t1[R2, :], psf[ba][R2, :], rope[R2, 0, g * 512:(g + 1) * 512], MUL, [pk(ba), "rope"], ["t1"])
                                tt("dve", t2[R2, :], psf[bb2][R2, :], rope[R2, 1, g * 512:(g + 1) * 512], MUL, [pk(bb2), "rope"], ["t2"])
                                tt("dve", QM[R2, j, g * 512:(g + 1) * 512], t1[R2, :], t2[R2, :], ADD, ["t1", "t2"], ["Q"])
                        for kb in range(18):
                            b = bank("a")
                            for c in range(2):
                                mm(psf[b][:, 0:128], ckvT[:, c, kb * 128:(kb + 1) * 128], wuv[:, c, hb * 128:(hb + 1) * 128],
                                   c == 0, c == 1, ["wu", "ckvT"], [pk(b)])
                            cp("dve", VM[:, kb, :, 0:64], psf[b][:, 0:128].rearrange("p (j d) -> p j d", j=2), [pk(b)], ["Vx"])
                            cp("dve", VM[:, kb, :, 128:192], psf[b][:, 0:128].rearrange("p (j d) -> p j d", j=2), [pk(b)], ["Vx"])
                    with Scope() as S:
                        attention(S, 2, 105, 96.0 ** -0.5, lambda j: QM[:, j, :], lambda j: KM[:, j, :],
                                  lambda j, kb: VM[:, kb, j, :],
                                  lambda j, qg: o_mla[j * 64:j * 64 + 64, hb, qg * 512:(qg + 1) * 512],
                                  lambda j: j)

    def resid_tile(S, t, banks, lhs_fn, nk, rhs_fn, src, dst, grow, bufs, rkeys):
        xt, st_ = bufs
        j = t % xt.shape[1]
        xk = "xr%d" % j
        dma("sp", xt[:, j, :], src[t * 128:(t + 1) * 128, :], (), [xk], xk)
        for hf in range(2):
            b = banks[hf]
            for k in range(nk):
                mm(psf[b][:, :], lhs_fn(k, t), rhs_fn(k, hf), k == 0, k == nk - 1, rkeys, [pk(b)])
        sk = "rst%d" % j
        mset("dve", st_[:, j, 0:2], 0.0, [sk])
        for hf in range(2):
            act(junk[:, hf * 512:(hf + 1) * 512], psf[banks[hf]][:, :], AF.Square, [pk(banks[hf])], ["junk", sk],
                accum=st_[:, j, hf:hf + 1])
        tt("dve", st_[:, j, 2:3], st_[:, j, 0:1], st_[:, j, 1:2], ADD, [sk], [sk])
        act(st_[:, j, 3:4], st_[:, j, 2:3], AF.Sqrt, [sk], [sk], bias=EPS, scale=1.0 / 1024)
        recip(st_[:, j, 4:5], st_[:, j, 3:4], [sk], [sk])
        for hf in range(2):
            stt("dve", junk[:, hf * 512:(hf + 1) * 512], psf[banks[hf]][:, :], st_[:, j, 4:5], grow[:, hf * 512:(hf + 1) * 512],
                MUL, MUL, [pk(banks[hf]), sk, "row"], ["junk"])
        tt("dve", xt[:, j, :], xt[:, j, :], junk[:, :], ADD, [xk, "junk"], [xk])
        dma("sp", dst[t * 128:(t + 1) * 128, :], xt[:, j, :], [xk], [], xk)

    def phase_merge(l, obr, src, dst):
        with Scope() as S1:
            merged = S1.sb("merged", [128, 8, T], BF16)
            with Scope() as S:
                setpools(g=[0, 1, 2], p=[3, 4, 5])
                wms = [S.sb("wms%d" % i, [128, 4, 8, 128], BF16) for i in range(2)]
                wb01 = [S.sb("wb01%d" % i, [128, 4, 4, 128], BF16) for i in range(2)]
                Gs = S.sb("Gs", [128, 512]); acc = S.sb("acc", [128, 512]); tm = S.sb("tm", [128, 512])
                for n in range(8):
                    i = n % 2
                    wk = "wmg%d" % i
                    for b_ in range(4):
                        dma("pool", wms[i][:, b_, :, :], D["wmerge"][l, :, :, b_ * 1024 + n * 128:b_ * 1024 + (n + 1) * 128],
                            (), [wk], wk)
                    for b_ in range(4):
                        dma("pool", wb01[i][:, b_, :, :], D["wbr01"][l, b_, :, :, n * 128:(n + 1) * 128], (), [wk], wk)
                    P.seal(wk)
                    for g in range(NG):
                        gs = slice(g * 512, (g + 1) * 512)
                        for b_ in range(4):
                            bg = bank("g")
                            for kc in range(8):
                                mm(psf[bg][:, :], wms[i][:, b_, kc, :], hT[:, kc, gs], kc == 0, kc == 7, [wk, "h%d" % g], [pk(bg)])
                            act(Gs[:], psf[bg][:, :], AF.Sigmoid, [pk(bg), "vec"], ["Gs"],
                                bias=V(l, "b_merge", 32)[:, b_ * 8 + n:b_ * 8 + n + 1])
                            bp = bank("p")
                            for kc in range(4):
                                mm(psf[bp][:, :], wb01[i][:, b_, kc, :], obr[b_][:, kc, gs], kc == 0, kc == 3, [wk, "obr"], [pk(bp)])
                            if b_ == 0:
                                tt("dve", acc[:], Gs[:], psf[bp][:, :], MUL, ["Gs", pk(bp)], ["acc"])
                            else:
                                tt("dve", tm[:], Gs[:], psf[bp][:, :], MUL, ["Gs", pk(bp)], ["tm"])
                                if b_ < 3:
                                    tt("dve", acc[:], acc[:], tm[:], ADD, ["acc", "tm"], ["acc"])
                                else:
                                    tt("dve", merged[:, n, gs], acc[:], tm[:], ADD, ["acc", "tm"], ["merged"])
            with Scope() as S:
                setpools(m=[0, 1], r=[2, 3, 4, 5])
                wo = S.sb("wo", [128, 8, 1024], BF16)
                grow = S.sb("grow", [128, 1024])
                xt = S.sb("xr", [128, 4, 1024]); st_ = S.sb("rst", [128, 4, 8])
                dma("pool", wo[:], D["wout"][l], (), ["wo"], "wo")
                make_row(l, 16, grow, S)
                for t in range(NT):
                    banks = [bank("r"), bank("r")]
                    resid_tile(S, t, banks, lambda k, t_: merged[:, k, t_ * 128:(t_ + 1) * 128], 8,
                               lambda k, hf: wo[:, k, hf * 512:(hf + 1) * 512], src, dst, grow, (xt, st_), ["merged", "wo"])

    def phase_ffn(l, src, dst):
        with Scope() as S1:
            actT = S1.sb("actT", [128, 22, T], BF16)
            with Scope() as S:
                setpools(a=[0, 1, 2, 3, 4, 5])
                wu = [S.sb("wu%d" % i, [128, 8, 2, 128], BF16) for i in range(2)]
                raw = [S.sb("raw%d" % i, [128, T + 2]) for i in range(2)]
                yu = S.sb("yu", [128, T]); yg = S.sb("yg", [128, T])
                nw = S.sb("nwf", [128, 88])
                for i in range(2):
                    mset("pool", raw[i][:, 0:1], 0.0, ["raw%d" % i])
                    mset("pool", raw[i][:, T + 1:T + 2], 0.0, ["raw%d" % i])
                cw0 = _VOFF["ffn_cw"]
                stt("dve", nw[:, 0:44], vec[:, l, cw0:cw0 + 44], -1.0, flg[:, 32:33].to_broadcast([128, 44]), MUL, MUL,
                    ["vec", "flg"], ["nw"])
                stt("dve", nw[:, 44:88], vec[:, l, cw0 + 88:cw0 + 132], -1.0, flg[:, 32:33].to_broadcast([128, 44]), MUL, MUL,
                    ["vec", "flg"], ["nw"])
                for c in range(22):
                    i = c % 2
                    wk = "wu%d" % i
                    dma("pool", wu[i][:, :, 0, :], D["wup"][l, :, :, c * 128:(c + 1) * 128], (), [wk], wk)
                    dma("pool", wu[i][:, :, 1, :], D["wup"][l, :, :, 2816 + c * 128:2816 + (c + 1) * 128], (), [wk], wk)
                    P.seal(wk)
                    for which in range(2):
                        ch = c + 22 * which
                        rw, rk_ = raw[which], "raw%d" % which
                        for g in range(NG):
                            b = bank("a")
                            for kc in range(8):
                                mm(psf[b][:, :], wu[i][:, kc, which, :], hT[:, kc, g * 512:(g + 1) * 512], kc == 0, kc == 7,
                                   [wk, "h%d" % g], [pk(b)])
                            act(rw[:, 1 + g * 512:1 + (g + 1) * 512], psf[b][:, :], AF.Copy, [pk(b)], [rk_])
                        y_, yk = (yu, "yu") if which == 0 else (yg, "yg")
                        conv(rw, rk_, vec[:, l, cw0 + ch:cw0 + ch + 1], vec[:, l, cw0 + 44 + ch:cw0 + 45 + ch],
                             vec[:, l, cw0 + 88 + ch:cw0 + 89 + ch], V(l, "ffn_cb", 44)[:, ch:ch + 1],
                             nw[:, ch:ch + 1], nw[:, 44 + ch:45 + ch], y_, yk)
                    act(yg[:, :], yg[:, :], AF.Silu, ["yg"], ["yg"])
                    tt("dve", actT[:, c, :], yu[:, :], yg[:, :], MUL, ["yu", "yg"], ["actT"])
            with Scope() as S:
                setpools(m=[0, 1], r=[2, 3, 4, 5])
                wd = S.sb("wd", [128, 22, 1024], BF16)
                grow = S.sb("grow", [128, 1024])
                xt = S.sb("xr", [128, 2, 1024]); st_ = S.sb("rst", [128, 2, 8])
                for q in range(2):
                    dma("pool", wd[:, q * 11:(q + 1) * 11, :], D["wdown"][l, :, q * 11:(q + 1) * 11, :], (), ["wd"], "wd%d" % q)
                make_row(l, 24, grow, S)
                for t in range(NT):
                    banks = [bank("r"), bank("r")]
                    resid_tile(S, t, banks, lambda k, t_: actT[:, k, t_ * 128:(t_ + 1) * 128], 22,
                               lambda k, hf: wd[:, k, hf * 512:(hf + 1) * 512], src, dst, grow, (xt, st_), ["actT", "wd"])

    dbg = dbg or {}
    stop = dbg.get("stop")
    nlayers = dbg.get("layers", DEPTH)

    def dump(name, t, keys=()):
        shp = list(t.shape)
        dd = nc.dram_tensor("dbg_" + name, shp, F32, kind="ExternalOutput").ap()
        idx = tuple(slice(None) for _ in shp)
        dma("pool", dd[idx], t[idx], list(keys), [], "dbg_" + name)

    xin = D["x"]
    for l in range(nlayers):
        if l == 0:
            phase_mod(lambda: phase_norm(xin, 0, der[:, 0, 0:8], modt[:, 0, 0:8]))
        else:
            phase_norm(xin, l, der[:, l, 0:8], modt[:, l, 0:8])
        if stop == "norm":
            dump("h", hT); P.flush(); return
        with Scope() as SA:
            o_ssd = SA.sb("o_ssd", [128, 4, T], BF16)
            if "ssd" not in dbg.get("skip", ()):
                phase_ssd(l, o_ssd)
            if stop == "ssd":
                dump("o_ssd", o_ssd); P.flush(); return
            with Scope() as SB:
                o_ret = SB.sb("o_ret", [128, 4, T], BF16)
                if "ret" not in dbg.get("skip", ()):
                    phase_ret(l, o_ret)
                if stop == "ret":
                    dump("o_ret", o_ret); P.flush(); return
                with Scope() as SC:
                    o_mla = SC.sb("o_mla", [128, 4, T], BF16)
                    if "mla" not in dbg.get("skip", ()):
                        phase_mla(l, o_mla)
                    if stop == "mla":
                        dump("o_mla", o_mla); P.flush(); return
                    with Scope() as SD:
                        o_att = SD.sb("o_att", [128, 4, T], BF16)
                        if "gqa" not in dbg.get("skip", ()):
                            phase_gqa(l, o_att)
                        if stop == "gqa":
                            dump("o_att", o_att); P.flush(); return
                        if stop == "mixers":
                            dump("o_ssd", o_ssd); dump("o_ret", o_ret); dump("o_mla", o_mla); dump("o_att", o_att)
                            P.flush(); return
                        phase_merge(l, [o_ssd, o_ret, o_att, o_mla], xin, xa)
        if stop == "merge":
            P.flush(); return
        phase_norm(xa, l, der[:, l, 8:16], modt[:, l, 24:32])
        dst = xb if l == 0 else O["y"]
        if nlayers == 1:
            dst = O["y"]
        phase_ffn(l, xa, dst)
        xin = xb
    P.flush()


_NC_CACHE = {}


def kernel(**inputs):
    shared, percore = _host_prep(inputs)
    if "nc" not in _NC_CACHE:
        _NC_CACHE["nc"] = build_nc()
    nc = _NC_CACHE["nc"]
    in_maps = [dict(shared, **percore[c]) for c in range(8)]
    res = run_bass_kernel_spmd(nc, in_maps, core_ids=list(range(8)))
    R = res.results
    f = np.float32
    y_prompt = np.stack([R[c]["y"] for c in range(4)]).reshape(32, 256, 1024).astype(f)
    y_sample = np.stack([R[c]["y"] for c in range(4, 8)]).reshape(4, 2048, 1024).astype(f)
    st_ssd = np.zeros((32, 2, 2, 8, 64, 64), f)
    st_ret = np.zeros((32, 2, 2, 4, 64, 128), f)
    ck = np.zeros((32, 2, 256, 2, 64), f)
    cv = np.zeros((32, 2, 256, 2, 64), f)
    cc = np.zeros((32, 2, 256, 256), f)
    cr = np.zeros((32, 2, 256, 32), f)
    for c in range(4):
        os_, or_ = R[c]["o_ssd"], R[c]["o_ret"]
        for h in range(8):
            g = h // 4
            st_ssd[8 * c:8 * c + 8, :, :, h] = os_[:, :, :, g * 64:(g + 1) * 64, h * 64:(h + 1) * 64].transpose(2, 0, 1, 3, 4)
        for h in range(4):
            kt, lh = h // 2, h % 2
            st_ret[8 * c:8 * c + 8, :, :, h] = or_[:, :, :, lh * 64:(lh + 1) * 64,
                                                   kt * 256 + lh * 128:kt * 256 + (lh + 1) * 128].transpose(2, 0, 1, 3, 4)
        ck[8 * c:8 * c + 8] = R[c]["o_ck"].reshape(2, 8, 256, 2, 64).transpose(1, 0, 2, 3, 4)
        cv[8 * c:8 * c + 8] = R[c]["o_cv"].reshape(2, 8, 256, 2, 64).transpose(1, 0, 2, 3, 4)
        cc[8 * c:8 * c + 8] = R[c]["o_ckv"].reshape(2, 8, 256, 256).transpose(1, 0, 2, 3)
        cr[8 * c:8 * c + 8] = R[c]["o_kr"].reshape(2, 8, 256, 32).transpose(1, 0, 2, 3)
    return (y_prompt, y_sample, st_ssd, st_ret, ck, cv, cc, cr)
```

```python
import numpy as np
from contextlib import ExitStack
import concourse.bass as bass
import concourse.mybir as mybir
from concourse.bass_utils import run_bass_kernel_spmd

F32 = mybir.dt.float32
BF16 = mybir.dt.bfloat16
AF = mybir.ActivationFunctionType
ALU = mybir.AluOpType
AX = mybir.AxisListType

T = 2048
NT = 16
NG = 4
DEPTH = 2
EPS = 1e-6
NEG = -30000.0
BIG = 512.0

_VOFF = {}
_NV = 0


def _vreg(name, n):
    global _NV
    _VOFF[name] = _NV
    _NV += n


for _n, _c in [("b_mod", 48), ("g_pre_mix", 8), ("g_post_mix", 8), ("g_pre_ffn", 8), ("g_post_ffn", 8),
               ("ssd_cw", 18), ("ssd_cb", 6), ("ffn_cw", 132), ("ffn_cb", 44), ("b_merge", 32),
               ("qn", 1), ("qnp", 1), ("kn", 1), ("knp", 1), ("mqn", 3), ("mkvn", 2),
               ("dt_bias", 16), ("a_log", 16), ("ssd_d", 8), ("ret_logit", 8),
               ("ssd_nw", 512), ("ret_gw", 512)]:
    _vreg(_n, _c)

_COFF = {"ident": 0, "triU": 128, "ntriS": 256, "nmU": 384, "nmL": 512, "ones": 640, "idx": 768}
_NC = 772
_CBOFF = {"ident": 0, "blk64": 128, "o384": 256, "o256": 384, "o128": 512}
_NCB = 1152

ENGS = ("pe", "act", "dve", "pool", "sp")


class Op:
    __slots__ = ("eng", "fn", "deps", "signal", "sigval", "slot", "dmaval")


class Prog:
    def __init__(self, nc, es):
        self.nc = nc
        self.es = es
        self.ops = {e: [] for e in ENGS}
        self.lastw = {}
        self.rd = {}
        self.esem = {e: es.enter_context(nc.semaphore("sem_" + e)) for e in ENGS}
        self.ecnt = {e: 0 for e in ENGS}
        self.dsem = {}
        self.dcnt = {}
        self.pend_dma = []
        self.waited = {e: {} for e in ENGS}
        self.out_dmas = []
        self.sealed = {}
        self.nops = 0

    def add(self, eng, fn, r=(), w=(), slot=None):
        op = Op()
        op.eng = eng
        op.fn = fn
        op.signal = False
        op.sigval = None
        op.slot = slot
        op.dmaval = None
        deps = []
        for k in r:
            d = self.lastw.get(k)
            if d is not None:
                deps.append(d)
        for k in w:
            d = self.lastw.get(k)
            if d is not None:
                deps.append(d)
            rr = self.rd.get(k)
            if rr:
                deps.extend(rr[0].values())
                deps.extend(rr[1])
        op.deps = [d for d in deps if d is not op and not (d.eng == "pe" and eng == "pe" and d.slot is None
                                                          and slot is None)]
        for k in w:
            self.lastw[k] = op
            self.rd[k] = [{}, []]
        for k in r:
            rr = self.rd.setdefault(k, [{}, []])
            if slot is not None:
                rr[1].append(op)
            else:
                rr[0][eng] = op
        if slot is not None:
            if slot not in self.dsem:
                self.dsem[slot] = self.es.enter_context(self.nc.semaphore("d_" + slot))
                self.dcnt[slot] = 0
            self.dcnt[slot] += 16
            op.dmaval = self.dcnt[slot]
            self.pend_dma.append(op)
        self.ops[eng].append(op)
        self.nops += 1
        return op

    def seal(self, slot):
        tot = self.dcnt.get(slot)
        if tot is None:
            return
        grp = [op for op in self.pend_dma if op.slot == slot and op.dmaval > self.sealed.get(slot, 0)]
        gs = set(id(o) for o in grp)
        for op in grp:
            op.dmaval = tot
            op.deps = [d for d in op.deps if id(d) not in gs]
        self.sealed[slot] = tot

    def flush(self):
        lasts = [self.ops[e][-1] for e in ENGS if self.ops[e]]
        b1 = Op()
        b1.eng, b1.fn, b1.signal, b1.sigval, b1.slot, b1.dmaval = "sp", None, False, None, None, None
        b1.deps = [d for d in lasts] + list(self.pend_dma)
        self.ops["sp"].append(b1)
        for e in ENGS:
            if e == "sp":
                continue
            b = Op()
            b.eng, b.fn, b.signal, b.sigval, b.slot, b.dmaval = e, None, False, None, None, None
            b.deps = [b1]
            self.ops[e].append(b)
        for e in ENGS:
            for op in self.ops[e]:
                for d in op.deps:
                    if d.slot is None:
                        d.signal = True
        for e in ENGS:
            for op in self.ops[e]:
                if op.slot is None and op.signal:
                    self.ecnt[e] += 1
                    op.sigval = self.ecnt[e]
        nc = self.nc
        with nc.Block() as block:
            def mk(ename):
                def body(eng):
                    wt = self.waited[ename]
                    for op in self.ops[ename]:
                        for d in op.deps:
                            if d.slot is not None:
                                sem, val = self.dsem[d.slot], d.dmaval
                                key = "d_" + d.slot
                            else:
                                sem, val = self.esem[d.eng], d.sigval
                                key = d.eng
                            if wt.get(key, 0) >= val:
                                continue
                            eng.wait_ge(sem, val)
                            wt[key] = val
                        if op.fn is None:
                            if op.signal:
                                eng.nop().then_inc(self.esem[ename], 1)
                            continue
                        ins = op.fn(eng)
                        if op.slot is not None:
                            ins.then_inc(self.dsem[op.slot], 16)
                        elif op.signal:
                            ins.then_inc(self.esem[ename], 1)
                return body
            block.tensor(mk("pe"))
            block.scalar(mk("act"))
            block.vector(mk("dve"))
            block.gpsimd(mk("pool"))
            block.sync(mk("sp"))
        self.ops = {e: [] for e in ENGS}
        self.lastw = {}
        self.rd = {}
        self.pend_dma = []


def _rope_tables(L, dim):
    d_axis = dim // 2
    n_rows = L // 64
    rows = np.repeat(np.arange(n_rows, dtype=np.float32), 64)
    cols = np.tile(np.arange(64, dtype=np.float32), n_rows)
    inv = (np.float32(10000.0) ** (-np.arange(0, d_axis, 2, dtype=np.float32) / np.float32(d_axis))).astype(np.float32)
    ar = rows[:, None] * inv[None, :]
    ac = cols[:, None] * inv[None, :]
    cos = np.concatenate([np.cos(ar), np.cos(ar), np.cos(ac), np.cos(ac)], axis=1).astype(np.float32)
    sin = np.concatenate([-np.sin(ar), np.sin(ar), -np.sin(ac), np.sin(ac)], axis=1).astype(np.float32)
    return cos.T.copy(), sin.T.copy()


def _perm(dim):
    q = dim // 4
    return np.concatenate([np.arange(q, 2 * q), np.arange(0, q), np.arange(3 * q, 4 * q), np.arange(2 * q, 3 * q)])


def _pk(a):
    K, N = a.shape
    return np.ascontiguousarray(a.reshape(K // 128, 128, N).transpose(1, 0, 2))


def _col8(v):
    return np.ascontiguousarray(v.reshape(-1, 128).T)


def _host_prep(inp):
    f = np.float32
    A = {k: np.asarray(v, dtype=f) for k, v in inp.items()}
    shared = {}
    p64, p32 = _perm(64), _perm(32)
    win, wuq, wuk, wuv, vecs = [], [], [], [], []
    offs = np.cumsum([0, 512, 768, 16, 256, 256, 512, 512, 512, 128, 128, 384, 288])
    for l in range(DEPTH):
        W = A["w_in"][l]
        z, xbc, dtr, rq, rk, rv, rg, aq, ak, av, mcq, mckv = [W[:, offs[i]:offs[i + 1]] for i in range(12)]

        def hp(m, p):
            nh = m.shape[1] // len(p)
            return m.reshape(1024, nh, len(p))[:, :, p].reshape(1024, -1)
        ckv, kr = mckv[:, :256], mckv[:, 256:]
        ext = np.concatenate([z, xbc, dtr,
                              rq, hp(rq, p64), rk, hp(rk, p64), rv, rg,
                              aq, hp(aq, p64), ak, hp(ak, p64), av,
                              mcq, ckv, kr, hp(kr, p32)], axis=1)
        assert ext.shape[1] == 5456
        win.append(_pk(ext))
        U = A["mla_w_uq"][l].reshape(384, 8, 96)
        ua = np.zeros((384, 8, 192), f)
        ua[:, :, 0:96] = U
        ua[:, :, 96 + 64:192] = U[:, :, 64:96][:, :, p32]
        wuq.append(_pk(ua.reshape(384, 1536)))
        KV = A["mla_w_ukv"][l].reshape(256, 8, 128)
        wuk.append(_pk(np.ascontiguousarray(KV[:, :, :64]).reshape(256, 512)))
        wuv.append(_pk(np.ascontiguousarray(KV[:, :, 64:]).reshape(256, 512)))
        V = np.zeros((128, _NV), f)

        def put(name, arr):
            arr = np.asarray(arr, f)
            V[:arr.shape[0], _VOFF[name]:_VOFF[name] + arr.shape[1]] = arr
        put("b_mod", A["b_mod"][l].reshape(48, 128).T)
        for nm in ("g_pre_mix", "g_post_mix", "g_pre_ffn", "g_post_ffn"):
            put(nm, _col8(A[nm][l]))
        cw = A["ssd_conv_w"][l]
        put("ssd_cw", np.concatenate([cw[j].reshape(6, 128).T for j in range(3)], axis=1))
        put("ssd_cb", A["ssd_conv_b"][l].reshape(6, 128).T)
        fw = A["ffn_conv_w"][l]
        put("ffn_cw", np.concatenate([fw[j].reshape(44, 128).T for j in range(3)], axis=1))
        put("ffn_cb", A["ffn_conv_b"][l].reshape(44, 128).T)
        put("b_merge", A["b_merge"][l].reshape(32, 128).T)
        qn, kn = A["att_q_norm"][l], A["att_k_norm"][l]
        put("qn", qn[:, None]); put("qnp", qn[p64][:, None]); put("kn", kn[:, None]); put("knp", kn[p64][:, None])
        put("mqn", A["mla_q_norm"][l].reshape(3, 128).T)
        put("mkvn", A["mla_kv_norm"][l].reshape(2, 128).T)
        bc = lambda v: np.broadcast_to(np.asarray(v, f).reshape(1, -1), (128, np.asarray(v).size))
        put("dt_bias", bc(A["ssd_dt_bias"][l])); put("a_log", bc(A["ssd_a_log"][l]))
        put("ssd_d", bc(A["ssd_d"][l])); put("ret_logit", bc(A["ret_decay_logit"][l]))
        put("ssd_nw", bc(A["ssd_norm_w"][l])); put("ret_gw", bc(A["ret_gn_w"][l]))
        vecs.append(V)
    shared["win"] = np.stack(win)
    shared["wuq"] = np.stack(wuq)
    shared["wuk"] = np.stack(wuk)
    shared["wuv"] = np.stack(wuv)
    shared["vecs"] = np.stack(vecs)
    shared["wmod"] = np.stack([_pk(A["w_mod"][l]) for l in range(DEPTH)])
    shared["wmerge"] = np.stack([_pk(A["w_merge"][l]) for l in range(DEPTH)])
    shared["wout"] = np.stack([_pk(A["w_out"][l]) for l in range(DEPTH)])
    shared["wup"] = np.stack([_pk(A["w_ffn_up"][l]) for l in range(DEPTH)])
    shared["wdown"] = np.stack([_pk(A["w_ffn_down"][l]) for l in range(DEPTH)])
    shared["wbr01"] = np.stack([np.stack([_pk(A[nm][l]) for nm in ("w_br_ssd", "w_br_ret", "w_br_att", "w_br_mla")])
                                for l in range(DEPTH)])
    C = np.zeros((128, _NC), f)
    ii = np.arange(128)
    C[:, 0:128] = np.eye(128)
    C[:, 128:256] = (ii[:, None] <= ii[None, :])
    C[:, 256:384] = -1.0 * (ii[:, None] < ii[None, :])
    C[:, 384:512] = np.where(ii[:, None] <= ii[None, :], 0.0, NEG)
    C[:, 512:640] = np.where(ii[:, None] >= ii[None, :], 0.0, NEG)
    C[:, 640:768] = 1.0
    shared["consts"] = C
    CB = np.zeros((128, _NCB), f)
    CB[:, 0:128] = np.eye(128)
    CB[0:64, 128:192] = 1.0 / 64
    CB[64:128, 192:256] = 1.0 / 64
    CB[:, 256:384] = 1.0 / 384
    CB[:, 384:512] = 1.0 / 256
    CB[:, 512:640] = 1.0 / 128
    CB[:, 640:768] = (ii[:, None] <= ii[None, :])
    CB[:, 768:896] = -1.0 * (ii[:, None] < ii[None, :])
    CB[:, 896:1024] = np.where(ii[:, None] <= ii[None, :], 0.0, NEG)
    CB[:, 1024:1152] = np.where(ii[:, None] >= ii[None, :], 0.0, NEG)
    shared["constb"] = CB

    cos64, sin64 = _rope_tables(T, 64)
    cos32, sin32 = _rope_tables(T, 32)
    percore = []
    for c in range(8):
        prompt = c < 4
        d = {}
        if prompt:
            d["x"] = np.ascontiguousarray(A["x_prompt"][8 * c:8 * c + 8].reshape(T, 1024))
            d["cond"] = _col8(A["c_ctx"])
            d["s_ssd"] = np.zeros((2, 2, 128, 512), f)
            d["s_ret"] = np.zeros((2, 2, 128, 512), f)
            d["ckT"] = np.zeros((2, 2, 64, 256), f)
            d["cv"] = np.zeros((2, 256, 128), f)
            d["cckvT"] = np.zeros((2, 256, 256), f)
            d["ckrT"] = np.zeros((2, 32, 256), f)
            r64 = np.zeros((128, 2, T), f); r64[:, 0] = 1.0
            r32 = np.zeros((128, 2, T), f); r32[:, 0] = 1.0
            aq = np.zeros((9, T), f)
            ak = np.zeros((9, T + 256), f)
            for s in range(8):
                aq[s, 256 * s:256 * (s + 1)] = BIG
                ak[s, 256 + 256 * s:256 + 256 * (s + 1)] = 1.0
            aq[8] = -BIG
            ak[8] = 1.0
            fl = np.zeros((128, 40), f)
            fl[:, 0:16] = (np.arange(16) % 2 == 1)
            fl[:, 16:32] = (np.arange(16) % 2 == 0)
            fl[:, 32] = 1.0
        else:
            b = c - 4
            d["x"] = np.ascontiguousarray(A["x_sample"][b])
            d["cond"] = _col8(A["c"][b])
            ss = A["state_ssd"][b]
            S = np.zeros((2, 2, 128, 512), f)
            for h in range(8):
                g = h // 4
                S[:, :, g * 64:(g + 1) * 64, h * 64:(h + 1) * 64] = ss[:, :, h]
            d["s_ssd"] = S
            sr = A["state_ret"][b]
            R = np.zeros((2, 2, 128, 512), f)
            for h in range(4):
                kt, lh = h // 2, h % 2
                R[:, :, lh * 64:(lh + 1) * 64, kt * 256 + lh * 128: kt * 256 + (lh + 1) * 128] = sr[:, :, h]
            d["s_ret"] = R
            d["ckT"] = np.ascontiguousarray(A["cache_att_k"][b].transpose(0, 2, 3, 1))
            d["cv"] = np.ascontiguousarray(A["cache_att_v"][b].reshape(2, 256, 128))
            d["cckvT"] = np.ascontiguousarray(A["cache_mla_ckv"][b].transpose(0, 2, 1))
            d["ckrT"] = np.ascontiguousarray(A["cache_mla_krope"][b].transpose(0, 2, 1))
            r64 = np.zeros((128, 2, T), f)
            r64[0:64, 0] = cos64; r64[64:128, 0] = cos64; r64[0:64, 1] = sin64; r64[64:128, 1] = sin64
            r32 = np.zeros((128, 2, T), f)
            r32[0:32, 0] = cos32; r32[64:96, 0] = cos32; r32[0:32, 1] = sin32; r32[64:96, 1] = sin32
            aq = np.zeros((9, T), f)
            ak = np.zeros((9, T + 256), f)
            fl = np.zeros((128, 40), f)
            fl[:, 0:32] = 1.0
        d["rope64"] = r64
        d["rope32"] = r32
        d["augq"] = aq
        d["augk"] = ak
        d["flags"] = fl
        percore.append(d)
    return shared, percore


def build_nc(dbg=None):
    nc = bass.Bass("TRN2", target_bir_lowering=False)
    es = ExitStack()
    with es:
        _build(nc, es, dbg)
    return nc


def _build(nc, es, dbg):
    dbg = dbg or {}
    P = Prog(nc, es)

    def din(name, shape):
        return nc.dram_tensor(name, list(shape), F32, kind="ExternalInput").ap()

    def dout(name, shape):
        return nc.dram_tensor(name, list(shape), F32, kind="ExternalOutput").ap()

    D = {}
    for nm, shp in [("x", (T, 1024)), ("cond", (128, 8)), ("s_ssd", (2, 2, 128, 512)), ("s_ret", (2, 2, 128, 512)),
                    ("ckT", (2, 2, 64, 256)), ("cv", (2, 256, 128)), ("cckvT", (2, 256, 256)), ("ckrT", (2, 32, 256)),
                    ("rope64", (128, 2, T)), ("rope32", (128, 2, T)), ("augq", (9, T)), ("augk", (9, T + 256)),
                    ("flags", (128, 40)),
                    ("win", (2, 128, 8, 5456)), ("wuq", (2, 128, 3, 1536)), ("wuk", (2, 128, 2, 512)),
                    ("wuv", (2, 128, 2, 512)), ("vecs", (2, 128, _NV)), ("wmod", (2, 128, 8, 6144)),
                    ("wmerge", (2, 128, 8, 4096)), ("wout", (2, 128, 8, 1024)), ("wup", (2, 128, 8, 5632)),
                    ("wdown", (2, 128, 22, 1024)), ("wbr01", (2, 4, 128, 4, 1024)),
                    ("consts", (128, _NC)), ("constb", (128, _NCB))]:
        D[nm] = din(nm, shp)
    O = {}
    for nm, shp in [("y", (T, 1024)), ("o_ssd", (2, 2, 8, 128, 512)), ("o_ret", (2, 2, 8, 128, 512)),
                    ("o_ck", (2, T, 128)), ("o_cv", (2, T, 128)), ("o_ckv", (2, T, 256)), ("o_kr", (2, T, 32))]:
        O[nm] = dout(nm, shp)
    xa = nc.dram_tensor("xa_scr", [T, 1024], F32, kind="Internal").ap()
    xb = nc.dram_tensor("xb_scr", [T, 1024], F32, kind="Internal").ap()

    _uid = [0]

    class Scope:
        def __init__(self):
            self.st = ExitStack()

        def __enter__(self):
            self.st.__enter__()
            return self

        def sb(self, name, shape, dt=F32):
            _uid[0] += 1
            return self.st.enter_context(nc.sbuf_tensor("s%d_%s" % (_uid[0], name), list(shape), dt))

        def __exit__(self, *a):
            P.flush()
            return self.st.__exit__(*a)

    psf = [es.enter_context(nc.psum_tensor("psf%d" % i, [128, 512], F32)) for i in range(7)]
    psb = es.enter_context(nc.psum_tensor("psb7", [128, 1024], BF16))
    pools = {}

    def setpools(**kw):
        pools.clear()
        for k, v in kw.items():
            pools[k] = [list(v), 0]

    def bank(pool):
        p = pools[pool]
        b = p[0][p[1] % len(p[0])]
        p[1] += 1
        return b

    def pk(b):
        return "ps%d" % b

    def act(out, in_, func, r, w, bias=None, scale=None, accum=None):
        kw = {}
        if bias is not None:
            kw["bias"] = bias
        if scale is not None:
            kw["scale"] = scale
        if accum is not None:
            kw["accum_out"] = accum
        return P.add("act", lambda e: e.activation(out, in_, func, **kw), r, w)

    def tt(eng, out, a, b, op, r, w):
        return P.add(eng, lambda e: e.tensor_tensor(out, a, b, op), r, w)

    def ts(eng, out, a, s1, op0, r, w, s2=None, op1=None):
        if op1 is None:
            return P.add(eng, lambda e: e.tensor_scalar(out, a, s1, None, op0), r, w)
        return P.add(eng, lambda e: e.tensor_scalar(out, a, s1, s2, op0, op1), r, w)

    def stt(eng, out, a, s, b, op0, op1, r, w):
        return P.add(eng, lambda e: e.scalar_tensor_tensor(out, a, s, b, op0, op1), r, w)

    def cp(eng, out, in_, r, w):
        return P.add(eng, lambda e: e.tensor_copy(out, in_), r, w)

    def recip(out, in_, r, w):
        return P.add("dve", lambda e: e.reciprocal(out, in_), r, w)

    def mset(eng, ap, val, w):
        return P.add(eng, lambda e: e.memset(ap, val), (), w)

    def mm(out, lhsT, rhs, st, sp, r, w):
        return P.add("pe", lambda e: e.matmul(out, lhsT, rhs, start=st, stop=sp), r, w)

    def tp(out, in_, ident, r, w):
        return P.add("pe", lambda e: e.transpose(out, in_, ident), r, w)

    def dma(eng, out, in_, r, w, slot):
        return P.add(eng, lambda e: e.dma_start(out=out, in_=in_), r, w, slot=slot)

    def rsum(out, in_, r, w):
        return P.add("dve", lambda e: e.reduce_sum(out, in_, AX.X), r, w)

    MUL, ADD, SUB = ALU.mult, ALU.add, ALU.subtract

    def psb_(name, shape, dt=F32):
        return es.enter_context(nc.sbuf_tensor("p_" + name, list(shape), dt))

    cf = psb_("cf", [128, _NC])
    cb = psb_("cb", [128, _NCB], BF16)
    vec = psb_("vec", [128, 2, _NV])
    flg = psb_("flg", [128, 40])
    hT = psb_("hT", [128, 8, T], BF16)
    modt = psb_("modt", [128, 2, 48])
    der = psb_("der", [128, 2, 48])
    junk = psb_("junk", [128, 1024])

    identf = cf[:, 0:128]
    triU = cf[:, 128:256]
    ntriS = cf[:, 256:384]
    nmU = cf[:, 384:512]
    nmL = cf[:, 512:640]
    onesf = cf[:, 640:768]
    identb = cb[:, 0:128]
    blk64 = cb[:, 128:256]
    o384 = cb[:, 256:384]
    o256 = cb[:, 384:512]
    triUb = cb[:, 640:768]
    ntriSb = cb[:, 768:896]
    nmUb = cb[:, 896:1024]
    nmLb = cb[:, 1024:1152]

    def V(l, name, n=None, rows=128):
        o = _VOFF[name]
        return vec[0:rows, l, o:o + (n if n is not None else 1)]

    dma("sp", cf[:], D["consts"][:, :], (), ["cf"], "cf")
    dma("pool", cb[:], D["constb"][:, :], (), ["cb"], "cb")
    dma("sp", vec[:, 0, :], D["vecs"][0], (), ["vec"], "vec0")
    dma("sp", vec[:, 1, :], D["vecs"][1], (), ["vec"], "vec1")
    dma("sp", flg[:], D["flags"][:, :], (), ["flg"], "flg")
    P.flush()

    def phase_mod(after):
        with Scope() as S:
            setpools(m=[0, 1])
            cond = S.sb("cond", [128, 8])
            scb = S.sb("scb", [128, 8], BF16)
            wm = [S.sb("wm%d" % i, [128, 8, 1536], BF16) for i in range(2)]
            dma("sp", cond[:], D["cond"][:, :], (), ["cond"], "cond")
            act(scb[:], cond[:], AF.Silu, ["cond"], ["scb"])
            it = 0
            for l in range(DEPTH):
                b = bank("m")
                for pc in range(4):
                    w_ = wm[it % 2]
                    wk = "wm%d" % (it % 2)
                    it += 1
                    dma("pool", w_[:], D["wmod"][l, :, :, pc * 1536:(pc + 1) * 1536], (), [wk], wk)
                    for nn in range(12):
                        n = pc * 12 + nn
                        for kc in range(8):
                            mm(psf[b][:, n:n + 1], w_[:, kc, nn * 128:(nn + 1) * 128], scb[:, kc:kc + 1], kc == 0, kc == 7,
                               [wk, "scb"], [pk(b)])
                tt("dve", modt[:, l, :], psf[b][:, 0:48], V(l, "b_mod", 48), ADD, [pk(b), "vec"], ["modt"])
                for (dst, gname, sc0, gt0, pname) in ((0, "g_pre_mix", 8, 16, "g_post_mix"), (8, "g_pre_ffn", 32, 40, "g_post_ffn")):
                    stt("dve", der[:, l, dst:dst + 8], modt[:, l, sc0:sc0 + 8], 1.0, V(l, gname, 8), ADD, MUL,
                        ["modt", "vec"], ["der"])
                    tt("dve", der[:, l, 16 + dst:24 + dst], modt[:, l, gt0:gt0 + 8], V(l, pname, 8), MUL,
                       ["modt", "vec"], ["der"])
            after()

    def make_row(l, col0, dst, S):
        for hh in range(2):
            b = bank("m")
            for k4 in range(4):
                kc = hh * 4 + k4
                mm(psf[b][:, k4 * 128:(k4 + 1) * 128], der[:, l, col0 + kc:col0 + kc + 1].to_broadcast([128, 128]),
                   identf, True, True, ["der", "cf"], [pk(b)])
            cp("dve", dst[:, hh * 512:(hh + 1) * 512], psf[b][:, :], [pk(b)], ["row"])

    def phase_norm(src, l, gcol, shcol):
        with Scope() as S:
            setpools(t=[0, 1, 2, 3])
            xt = S.sb("xt", [128, 8, 1024])
            st_ = S.sb("nst", [128, 8, 4])

            def s1(g):
                for j4 in range(4):
                    t = 4 * g + j4
                    j = t % 8
                    xk = "xt%d" % j
                    dma("sp", xt[:, j, :], src[t * 128:(t + 1) * 128, :], (), [xk], xk)
                    mset("dve", st_[:, j, 0:1], 0.0, ["st%d" % j])
                    act(junk[:, :], xt[:, j, :], AF.Square, [xk], ["junk", "st%d" % j], accum=st_[:, j, 0:1])
                    act(st_[:, j, 1:2], st_[:, j, 0:1], AF.Sqrt, ["st%d" % j], ["st%d" % j], bias=EPS, scale=1.0 / 1024)
                    recip(st_[:, j, 2:3], st_[:, j, 1:2], ["st%d" % j], ["st%d" % j])
                    ts("dve", xt[:, j, :], xt[:, j, :], st_[:, j, 2:3], MUL, [xk, "st%d" % j], [xk])

            def s2(g):
                for kc in range(8):
                    b = bank("t")
                    for j4 in range(4):
                        j = (4 * g + j4) % 8
                        tp(psf[b][:, j4 * 128:(j4 + 1) * 128], xt[:, j, kc * 128:(kc + 1) * 128], identf,
                           ["xt%d" % j, "cf"], [pk(b)])
                    if kc % 2 == 0:
                        act(hT[:, kc, g * 512:(g + 1) * 512], psf[b][:, :], AF.Identity, [pk(b), "der", "modt"],
                            ["h%d" % g], scale=gcol[:, kc:kc + 1], bias=shcol[:, kc:kc + 1])
                    else:
                        ts("dve", hT[:, kc, g * 512:(g + 1) * 512], psf[b][:, :], gcol[:, kc:kc + 1], MUL,
                           [pk(b), "der", "modt"], ["h%d" % g], s2=shcol[:, kc:kc + 1], op1=ADD)

            s1(0)
            for g in range(NG):
                if g + 1 < NG:
                    s1(g + 1)
                s2(g)

    WP = 256

    def load_w(Wt, src, ncols, order=None):
        npieces = (ncols + WP - 1) // WP
        for i in (order if order is not None else range(npieces)):
            c0, c1 = i * WP, min(ncols, (i + 1) * WP)
            dma("pool", Wt[:, :, c0:c1], src[:, :, c0:c1], (), ["W_%d" % i], "W_%d" % i)

    def wkeys(c0, n):
        return ["W_%d" % i for i in range(c0 // WP, (c0 + n - 1) // WP + 1)]

    def proj_fm(wt, wk, c0, M, g, pool):
        b = bank(pool)
        wks = wkeys(c0, M) if wk == "W" else [wk]
        for kc in range(8):
            mm(psf[b][0:M, :], wt[:, kc, c0:c0 + M], hT[:, kc, g * 512:(g + 1) * 512], kc == 0, kc == 7,
               wks + ["h%d" % g], [pk(b)])
        return b

    def proj_tm(wt, wk, c0, N, t, b, o0=0):
        wks = wkeys(c0, N) if wk == "W" else [wk]
        for kc in range(8):
            mm(psf[b][:, o0:o0 + N], hT[:, kc, t * 128:(t + 1) * 128], wt[:, kc, c0:c0 + N], kc == 0, kc == 7,
               wks + ["h%d" % (t // 4)], [pk(b)])

    def conv(raw, rk_, w0, w1, w2, bcol, nw0, nw2, y, yk):
        act(y[:, :], raw[:, 1:T + 1], AF.Identity, [rk_, "vec"], [yk], scale=w1, bias=bcol)
        stt("dve", y[:, :], raw[:, 0:T], w0, y[:, :], MUL, ADD, [rk_, yk, "vec"], [yk])
        stt("dve", y[:, :], raw[:, 2:T + 2], w2, y[:, :], MUL, ADD, [rk_, yk, "vec"], [yk])
        yv = y[:, :].rearrange("p (m s) -> p m s", s=256)
        rv_ = raw[:, 0:T].rearrange("p (m s) -> p m s", s=256)
        stt("dve", yv[:, 1:8, 0:1], rv_[:, 1:8, 0:1], nw0, yv[:, 1:8, 0:1], MUL, ADD, [rk_, yk, "nw"], [yk])
        rv2 = raw[:, 2:T + 2].rearrange("p (m s) -> p m s", s=256)
        stt("dve", yv[:, 0:7, 255:256], rv2[:, 0:7, 255:256], nw2, yv[:, 0:7, 255:256], MUL, ADD, [rk_, yk, "nw"], [yk])

    def scan(S, l, nh, nkt, dv, QT, KT, KTM, Vt, la, lnw, s_in, s_out, epilogue, const_decay=False):
        hkt = nh // nkt
        hph = hkt // 2
        n2, n4, n6 = 2 * nh, 4 * nh, 6 * nh
        sm = S.sb("sm", [128, NT, n4])
        bia = S.sb("bia", [128, NT, n2])
        ex = S.sb("ex", [128, NT, n6])
        ar = ex
        dcc = S.sb("dcc", [128, NT, n2])
        tE = S.sb("tE", [128, NT, nh])
        Sbf = S.sb("Sbf", [128, NT, 512], BF16)
        Sst = S.sb("Sst", [128, 2, 512])
        Sbb = S.sb("Sbb", [128, 2, 512], BF16)
        nb_ = 1 if const_decay else 2
        Dm2 = [S.sb("Dm%d" % i, [128, n2, 128]) for i in range(nb_)]
        LL2 = [S.sb("LL%d" % i, [128, nh, 128]) for i in range(nb_)]
        MT2 = [S.sb("MT%d" % i, [128, nh, 128], BF16) for i in range(2)]
        Oa2 = [S.sb("Oa%d" % i, [128, 512]) for i in range(2)]
        tmpo2 = [S.sb("tmpo%d" % i, [128, 512]) for i in range(2)]
        tmps = S.sb("tmps", [128, 512])
        tmps2 = [S.sb("tmq%d" % i, [128, 512]) for i in range(2)]
        Vw = S.sb("Vw", [128, 2, 512], BF16)

        la_hi = S.sb("la_hi", [128, NT, n2], BF16)
        la_lo = S.sb("la_lo", [128, NT, n2], BF16)
        cp("dve", la_hi[:, :, :], la[:, :, :], ["la"], ["lahl"])
        tt("dve", bia[:, :, :], la[:, :, :], la_hi[:, :, :], SUB, ["la", "lahl"], ["bia"])
        cp("dve", la_lo[:, :, :], bia[:, :, :], ["bia"], ["lahl"])
        setpools(row=[0])
        bA = bank("row")
        for c in range(NT):
            mm(psf[bA][:, c * n4:c * n4 + n2], triU, la[:, c, :], True, True, ["cf", "la"], [pk(bA)])
            mm(psf[bA][:, c * n4 + n2:(c + 1) * n4], onesf, la[:, c, :], True, True, ["cf", "la"], [pk(bA)])
        act(sm[:, :, :], psf[bA][:, 0:NT * n4].rearrange("p (c q) -> p c q", c=NT), AF.Copy, [pk(bA)], ["sm"])
        Af, Ab = sm[:, :, 0:nh], sm[:, :, nh:n2]
        tf_, tb_ = sm[:, :, n2:n2 + nh], sm[:, :, n2 + nh:n4]
        tt("dve", tE[:, :, :], Ab, la[:, :, nh:n2], SUB, ["sm", "la"], ["tE"])
        if lnw is not None:
            tt("dve", bia[:, :, 0:nh], lnw[:, :, 0:nh], Af, SUB, ["sm", "la"], ["bia"])
            tt("dve", bia[:, :, nh:n2], lnw[:, :, nh:n2], tE[:, :, :], ADD, ["tE", "la"], ["bia"])
        else:
            ts("dve", bia[:, :, 0:nh], Af, -1.0, MUL, ["sm"], ["bia"])
            cp("dve", bia[:, :, nh:n2], tE[:, :, :], ["tE"], ["bia"])
        cp("dve", ar[:, :, 0:nh], Af, ["sm"], ["ar0"])
        tt("dve", ar[:, :, nh:n2], tb_, tE[:, :, :], SUB, ["sm", "tE"], ["ar1"])
        tt("dve", ar[:, :, n2:n2 + nh], bia[:, :, 0:nh], tf_, ADD, ["bia", "sm"], ["ar2"])
        cp("dve", ar[:, :, n2 + nh:n4], bia[:, :, nh:n2], ["bia"], ["ar3"])
        cp("dve", ar[:, :, n4:n6], sm[:, :, n2:n4], ["sm"], ["ar4"])
        act(ex[:, :, :], ar[:, :, :], AF.Exp, ["ar0", "ar1", "ar2", "ar3", "ar4"], ["ex", "ar0", "ar1", "ar2", "ar3", "ar4"])
        tt("dve", dcc[:, :, 0:nh], ex[:, :, n4:n4 + nh], flg[:, 0:16].unsqueeze(2).to_broadcast([128, NT, nh]), MUL,
           ["ex", "flg"], ["dcc"])
        tt("dve", dcc[:, :, nh:n2], ex[:, :, n4 + nh:n6], flg[:, 16:32].unsqueeze(2).to_broadcast([128, NT, nh]), MUL,
           ["ex", "flg"], ["dcc"])

        def bc3(ap2):
            return ap2.unsqueeze(2).to_broadcast([128, nh, dv])

        def v3(ap):
            return ap.rearrange("p (h e) -> p h e", h=nh)

        w_ = hkt * dv

        def st_pre(c, wcol0):
            j = c % 2
            vk = "Vw%d" % j
            tt("dve", v3(Vw[:, j, :]), v3(Vt[:, c, :]), bc3(ex[:, c, wcol0:wcol0 + nh]), MUL, ["V", "ex"], [vk])
            b = bank("s")
            for kt in range(nkt):
                mm(psf[b][:, kt * w_:(kt + 1) * w_], KTM(kt, c), Vw[:, j, kt * w_:(kt + 1) * w_], True, True,
                   ["KTM", vk], [pk(b)])
            return b

        def st_post(c, d, cur, b, last_out):
            tt("dve", v3(tmps[:]), v3(Sst[:, cur, :]), bc3(dcc[:, c, d * nh:(d + 1) * nh]), MUL,
               ["S%d" % cur, "dcc"], ["tmps"])
            tt("dve", Sst[:, 1 - cur, :], tmps[:], psf[b][:, :], ADD, ["tmps", pk(b)], ["S%d" % (1 - cur)])
            if last_out is not None:
                dma("sp", last_out, Sst[:, 1 - cur, :], ["S%d" % (1 - cur)], [], "so%d" % (1 - cur))

        if dbg.get("ssd_stop") == 2:
            return
        setpools(s=[1, 2, 3])
        dma("sp", Sst[:, 0, :], s_in[0], (), ["S0"], "sin")
        cur = 0
        bn = st_pre(0, n2)
        for c in range(NT):
            bcur = bn
            if c + 1 < NT:
                bn = st_pre(c + 1, n2)
            act(Sbf[:, c, :], Sst[:, cur, :], AF.Copy, ["S%d" % cur, "flg"], ["Sbf%d" % c], scale=flg[:, c:c + 1])
            st_post(c, 0, cur, bcur, s_out[0, c // 2] if c % 2 == 1 else None)
            cur = 1 - cur
        if dbg.get("ssd_stop") == 3:
            return
        setpools(row=[0, 1], g=[2], o=[3], f=[4], bb=[5], s=[6])
        dma("sp", Sst[:, cur, :], s_in[1], (), ["S%d" % cur], "sin")
        state = {"cur": cur}

        def stageA(c):
            cur = state["cur"]
            sk = "Sbb%d" % (c % 2)
            act(Sbb[:, c % 2, :], Sst[:, cur, :], AF.Copy, ["S%d" % cur, "flg"], [sk], scale=flg[:, 16 + c:17 + c])
            bs_ = st_pre(c, n2 + nh)
            st_post(c, 1, cur, bs_, s_out[1, c // 2] if c % 2 == 0 else None)
            state["cur"] = 1 - cur
            return sk

        def stageR(c):
            pb = 0 if const_decay else c % 2
            Dm, LL = Dm2[pb], LL2[pb]
            dmk, llk = "Dm%d" % pb, "LL%d" % pb
            for q0 in range(0, n2, 4):
                b = bank("row")
                for q in range(q0, q0 + 4):
                    o_ = psf[b][:, (q - q0) * 128:(q - q0 + 1) * 128]
                    tr_ = triUb if q < nh else ntriSb
                    mm(o_, la_hi[:, c, q:q + 1].to_broadcast([128, 128]), tr_, True, False, ["lahl", "cb"], [pk(b)])
                    mm(o_, la_lo[:, c, q:q + 1].to_broadcast([128, 128]), tr_, False, False, ["lahl", "cb"], [pk(b)])
                    mm(o_, identb, nmUb if q < nh else nmLb, False, True, ["cb"], [pk(b)])
                tt("dve", Dm[:, q0:q0 + 4, :], psf[b][:, :].rearrange("p (q i) -> p q i", q=4),
                   bia[:, c, q0:q0 + 4].unsqueeze(2).to_broadcast([128, 4, 128]), ADD, [pk(b), "bia"], [dmk + "_%d" % q0])
            dks = [dmk + "_%d" % q0 for q0 in range(0, n2, 4)]
            act(Dm[:, :, :], Dm[:, :, :], AF.Exp, dks, dks)
            tt("dve", LL[:, :, :], Dm[:, 0:nh, :], Dm[:, nh:n2, :], ADD, dks, [llk])

        def stageA2(c, sk):
            pb = c % 2
            pl = 0 if const_decay else pb
            LL, MT, Oa, tmpo = LL2[pl], MT2[pb], Oa2[pb], tmpo2[pb]
            llk, mtk, oak, tmk = "LL%d" % pl, "MT%d" % pb, "Oa%d" % pb, "tmpo%d" % pb
            bg = bank("g")
            for kt in range(nkt):
                for hf in range(2):
                    i_ = kt * 2 + hf
                    mm(psf[bg][:, i_ * 128:(i_ + 1) * 128], KT(kt, c), QT(kt, hf, c), True, True, ["QK"], [pk(bg)])
            for kt in range(nkt):
                for hf in range(2):
                    i_ = kt * 2 + hf
                    h0 = kt * hkt + hf * hph
                    if hph > 1:
                        tt("dve", MT[:, h0:h0 + hph, :], LL[:, h0:h0 + hph, :],
                           psf[bg][:, i_ * 128:(i_ + 1) * 128].unsqueeze(1).to_broadcast([128, hph, 128]), MUL,
                           [llk, pk(bg)], [mtk + "_%d" % i_])
                    else:
                        tt("dve", MT[:, h0, :], LL[:, h0, :], psf[bg][:, i_ * 128:(i_ + 1) * 128], MUL,
                           [llk, pk(bg)], [mtk + "_%d" % i_])
            bo, bf_, bb_ = bank("o"), bank("f"), bank("bb")
            for h in range(nh):
                mm(psf[bo][:, h * dv:(h + 1) * dv], MT[:, h, :], Vt[:, c, h * dv:(h + 1) * dv], True, True,
                   [mtk + "_%d" % ((h // hkt) * 2 + (h % hkt) // hph), "V"], [pk(bo)])
            for kt in range(nkt):
                for hf in range(2):
                    c0 = kt * hkt * dv + hf * hph * dv
                    c1 = c0 + hph * dv
                    mm(psf[bf_][:, c0:c1], QT(kt, hf, c), Sbf[:, c, c0:c1], True, True, ["QK", "Sbf%d" % c], [pk(bf_)])
                    mm(psf[bb_][:, c0:c1], QT(kt, hf, c), Sbb[:, c % 2, c0:c1], True, True, ["QK", sk], [pk(bb_)])
            act(Oa[:], psf[bo][:, :], AF.Copy, [pk(bo)], [oak])
            tt("dve", v3(tmpo[:]), v3(psf[bf_][:, :]), bc3(ex[:, c, 0:nh]), MUL, [pk(bf_), "ex"], [tmk])
            tt("dve", v3(tmps2[pb][:]), v3(psf[bb_][:, :]), bc3(ex[:, c, nh:n2]), MUL, [pk(bb_), "ex"], ["tq%d" % pb])
            return (Oa, tmpo, oak, tmk, pb)

        def stageB(c, ctx):
            Oa, tmpo, oak, tmk, pb = ctx
            tt("dve", tmpo[:], tmpo[:], tmps2[pb][:], ADD, [tmk, "tq%d" % pb], [tmk])
            tt("dve", Oa[:], Oa[:], tmpo[:], ADD, [oak, tmk], [oak])
            epilogue(c, Oa, tmpo, oak, tmk)

        prev = None
        stageR(NT - 1)
        if not const_decay:
            stageR(NT - 2)
        for c in range(NT - 1, -1, -1):
            sk = stageA(c)
            ctx = stageA2(c, sk)
            if c >= 2 and not const_decay:
                stageR(c - 2)
            if prev is not None:
                stageB(*prev)
            prev = (c, ctx)
        stageB(*prev)

    def to_fm(S, c, oTM, ok, dst, dk):
        for k in range(4):
            tp(psb[:, k * 128:(k + 1) * 128], oTM[:, k * 128:(k + 1) * 128], identb, [ok, "cb"], ["psb"])
        cp("dve", dst[:, 0:4, c * 128:(c + 1) * 128], psb[:, 0:512].rearrange("p (k t) -> p k t", k=4), ["psb"], [dk])

    def phase_ssd(l, o_ssd):
        with Scope() as S1:
            xsTM = S1.sb("xsTM", [128, NT, 512], BF16)
            BT = S1.sb("BT", [128, T], BF16)
            CT = [S1.sb("CT%d" % i, [128, T], BF16) for i in range(2)]
            mset("pool", CT[0][64:128, :], 0.0, ["QK"])
            mset("pool", CT[1][0:64, :], 0.0, ["QK"])
            BTM = S1.sb("BTM", [128, NT, 128], BF16)
            zs = S1.sb("zs", [128, NT, 512], BF16)
            la = S1.sb("la", [128, NT, 16])
            lnw = S1.sb("lnw", [128, NT, 16])
            nw = S1.sb("nw", [128, 12])
            with Scope() as S:
                setpools(a=[0, 1, 2, 3], t=[4, 5])
                W = S.sb("W", [128, 8, 1296], BF16)
                raw = [S.sb("raw%d" % i, [128, T + 2]) for i in range(2)]
                ycv = S.sb("ycv", [128, T])
                xact = S.sb("xact", [128, T], BF16)
                dtr = S.sb("dtr", [128, NT, 16])
                ea = S.sb("ea", [128, 16])
                load_w(W, D["win"][l, :, :, 0:1296], 1296, order=[2, 0, 1, 3, 4, 5])
                for i in range(2):
                    mset("pool", raw[i][:, 0:1], 0.0, ["raw%d" % i])
                    mset("pool", raw[i][:, T + 1:T + 2], 0.0, ["raw%d" % i])
                stt("dve", nw[:, 0:6], V(l, "ssd_cw", 6), -1.0, flg[:, 32:33].to_broadcast([128, 6]), MUL, MUL,
                    ["vec", "flg"], ["nw"])
                stt("dve", nw[:, 6:12], vec[:, l, _VOFF["ssd_cw"] + 12:_VOFF["ssd_cw"] + 18], -1.0,
                    flg[:, 32:33].to_broadcast([128, 6]), MUL, MUL, ["vec", "flg"], ["nw"])
                cw0 = _VOFF["ssd_cw"]
                zq = []
                for ch in range(6):
                    rw = raw[ch % 2]
                    rk_ = "raw%d" % (ch % 2)
                    for g in range(NG):
                        b = proj_fm(W, "W", 512 + ch * 128, 128, g, "a")
                        act(rw[:, 1 + g * 512:1 + (g + 1) * 512], psf[b][:, :], AF.Copy, [pk(b)], [rk_])
                    conv(rw, rk_, vec[:, l, cw0 + ch:cw0 + ch + 1], vec[:, l, cw0 + 6 + ch:cw0 + 7 + ch],
                         vec[:, l, cw0 + 12 + ch:cw0 + 13 + ch], V(l, "ssd_cb", 6)[:, ch:ch + 1],
                         nw[:, ch:ch + 1], nw[:, 6 + ch:7 + ch], ycv, "ycv")
                    for t in range(ch * 3, min(NT, ch * 3 + 3)):
                        bz = bank("a")
                        proj_tm(W, "W", 0, 512, t, bz)
                        zq.append((t, bz))
                    dstx = xact if ch < 4 else BT
                    dk = "xact" if ch < 4 else ("QK")
                    if ch < 5:
                        act(dstx[:, :], ycv[:, :], AF.Silu, ["ycv"], [dk])
                    else:
                        act(CT[0][0:64, :], ycv[0:64, :], AF.Silu, ["ycv"], [dk])
                        act(CT[1][64:128, :], ycv[64:128, :], AF.Silu, ["ycv"], [dk])
                    while zq:
                        t_, bz_ = zq.pop(0)
                        act(zs[:, t_, :], psf[bz_][:, :], AF.Silu, [pk(bz_)], ["zs"])
                    if ch < 5:
                        for t0 in range(0, NT, 8):
                            for j in range(8):
                                tp(psb[:, j * 128:(j + 1) * 128], dstx[:, (t0 + j) * 128:(t0 + j + 1) * 128], identb,
                                   [dk, "cb"], ["psb"])
                            if ch < 4:
                                cp("dve", xsTM[:, t0:t0 + 8, ch * 128:(ch + 1) * 128],
                                   psb[:, :].rearrange("p (j t) -> p j t", j=8), ["psb"], ["V"])
                            else:
                                cp("dve", BTM[:, t0:t0 + 8, :], psb[:, :].rearrange("p (j t) -> p j t", j=8),
                                   ["psb"], ["KTM"])
                b = bank("t")
                for t in range(NT):
                    proj_tm(W, "W", 1280, 16, t, b, o0=t * 16)
                tt("dve", dtr[:, :, :], psf[b][:, 0:256].rearrange("p (t q) -> p t q", t=NT),
                   V(l, "dt_bias", 16).unsqueeze(1).to_broadcast([128, NT, 16]), ADD, [pk(b), "vec"], ["dtr"])
                act(dtr[:, :, :], dtr[:, :, :], AF.Exp, ["dtr"], ["dtr"])
                act(dtr[:, :, :], dtr[:, :, :], AF.Ln, ["dtr"], ["dtr"], bias=1.0)
                act(lnw[:, :, :], dtr[:, :, :], AF.Ln, ["dtr"], ["la"])
                act(ea[:], V(l, "a_log", 16), AF.Exp, ["vec"], ["ea"])
                stt("dve", la[:, :, :], dtr[:, :, :], -1.0, ea[:].unsqueeze(1).to_broadcast([128, NT, 16]), MUL, MUL,
                    ["dtr", "ea"], ["la"])
                for t in range(18, NT):
                    b = bank("a")
                    proj_tm(W, "W", 0, 512, t, b)
                    act(zs[:, t, :], psf[b][:, :], AF.Silu, [pk(b)], ["zs"])
            if dbg.get("ssd_stop") == 1:
                return
            with Scope() as S:
                st2 = S.sb("st2", [128, 4])
                oTM2 = [S.sb("oTM%d" % i, [128, 512], BF16) for i in range(2)]

                def epi(c, Oa, tmpo, oak, tmk):
                    oTM, otk = oTM2[c % 2], "oTM%d" % (c % 2)
                    tt("dve", tmpo[:].rearrange("p (h e) -> p h e", h=8), xsTM[:, c, :].rearrange("p (h e) -> p h e", h=8),
                       V(l, "ssd_d", 8).unsqueeze(2).to_broadcast([128, 8, 64]), MUL, ["V", "vec"], [tmk])
                    tt("dve", Oa[:], Oa[:], tmpo[:], ADD, [oak, tmk], [oak])
                    tt("dve", Oa[:], Oa[:], zs[:, c, :], MUL, [oak, "zs"], [oak])
                    mset("dve", st2[:, 0:1], 0.0, ["st2"])
                    act(junk[:, 0:512], Oa[:], AF.Square, [oak], ["junk", "st2"], accum=st2[:, 0:1])
                    act(st2[:, 1:2], st2[:, 0:1], AF.Ln, ["st2"], ["st2"], bias=EPS, scale=1.0 / 512)
                    act(st2[:, 2:3], st2[:, 1:2], AF.Exp, ["st2"], ["st2"], scale=-0.5)
                    stt("dve", oTM[:], Oa[:], st2[:, 2:3], V(l, "ssd_nw", 512), MUL, MUL, [oak, "st2", "vec"], [otk])
                    to_fm(S, c, oTM, otk, o_ssd, "o_ssd")
                scan(S, l, 8, 1, 64,
                     lambda kt, hf, c: CT[hf][:, c * 128:(c + 1) * 128],
                     lambda kt, c: BT[:, c * 128:(c + 1) * 128],
                     lambda kt, c: BTM[:, c, :], xsTM, la, lnw, D["s_ssd"][l], O["o_ssd"][l], epi)

    def rope_fm(S, b_x, b_p, rows, g, rope, dst, p, dk, pre=1.0, tmp=None):
        t1, t2 = tmp
        act(t1[rows, :], psf[b_x][rows, :], AF.Copy, [pk(b_x)], ["t1"], scale=pre)
        act(t2[rows, :], psf[b_p][rows, :], AF.Copy, [pk(b_p)], ["t2"], scale=pre)
        tt("dve", t1[rows, :], t1[rows, :], rope[rows, 0, g * 512:(g + 1) * 512], MUL, ["t1", "rope"], ["t1"])
        tt("pool", t2[rows, :], t2[rows, :], rope[rows, 1, g * 512:(g + 1) * 512], MUL, ["t2", "rope"], ["t2"])
        gs = slice(g * 512, (g + 1) * 512)
        if isinstance(dst, list):
            tt("dve", dst[0][0:64, p, gs], t1[0:64, :], t2[0:64, :], ADD, ["t1", "t2"], [dk])
            tt("dve", dst[1][64:128, p, gs], t1[64:128, :], t2[64:128, :], ADD, ["t1", "t2"], [dk])
        else:
            tt("dve", dst[:, p, gs], t1[rows, :], t2[rows, :], ADD, ["t1", "t2"], [dk])

    def phase_ret(l, o_ret):
        with Scope() as S1:
            rqT = [S1.sb("rqT%d" % i, [128, 2, T], BF16) for i in range(2)]
            mset("pool", rqT[0][64:128, :, :], 0.0, ["QK"])
            mset("pool", rqT[1][0:64, :, :], 0.0, ["QK"])
            rkT = S1.sb("rkT", [128, 2, T], BF16)
            rkTM = S1.sb("rkTM", [128, NT, 256], BF16)
            rvTM = S1.sb("rvTM", [128, NT, 512], BF16)
            rgs = S1.sb("rgs", [128, NT, 512], BF16)
            la = S1.sb("la", [128, NT, 8])
            with Scope() as S:
                setpools(a=[0, 1, 2, 3], t=[4, 5])
                W = S.sb("W", [128, 8, 1024], BF16)
                rope = S.sb("rope", [128, 2, T], BF16)
                t1 = S.sb("t1", [128, 512])
                t2 = S.sb("t2", [128, 512])
                l8 = S.sb("l8", [128, 8])
                dma("pool", rope[:], D["rope64"][:, :, :], (), ["rope"], "rope")
                load_w(W, D["win"][l, :, :, 1296:2320], 1024)
                Wb = S.sb("Wb", [128, 8, 1024], BF16)
                for i_ in range(2):
                    dma("pool", Wb[:, :, i_ * 512:(i_ + 1) * 512], D["win"][l, :, :, 2320 + i_ * 512:2320 + (i_ + 1) * 512],
                        (), ["Wb%d" % i_], "Wb%d" % i_)
                act(l8[:], V(l, "ret_logit", 8), AF.Exp, ["vec"], ["l8"], scale=-1.0)
                act(l8[:], l8[:], AF.Ln, ["l8"], ["l8"], bias=1.0)
                ts("dve", la[:, :, :], l8[:].unsqueeze(1).to_broadcast([128, NT, 8]), -1.0, MUL, ["l8"], ["la"])
                ti = [0]
                for (c0, dst, pre) in ((0, rqT, 1.0), (512, rkT, 0.125)):
                    for p in range(2):
                        for g in range(NG):
                            bx = proj_fm(W, "W", c0 + p * 128, 128, g, "a")
                            bp = proj_fm(W, "W", c0 + 256 + p * 128, 128, g, "a")
                            rope_fm(S, bx, bp, slice(0, 128), g, rope, dst, p, "QK", pre, (t1, t2))
                            t_ = ti[0]
                            ti[0] += 1
                            if t_ < NT:
                                b = bank("a")
                                proj_tm(Wb, "Wb0", 0, 512, t_, b)
                                cp("dve", rvTM[:, t_, :], psf[b][:, :], [pk(b)], ["V"])
                                b = bank("a")
                                proj_tm(Wb, "Wb1", 512, 512, t_, b)
                                act(rgs[:, t_, :], psf[b][:, :], AF.Silu, [pk(b)], ["rgs"])
                for p in range(2):
                    for t0 in range(0, NT, 8):
                        for j in range(8):
                            tp(psb[:, j * 128:(j + 1) * 128], rkT[:, p, (t0 + j) * 128:(t0 + j + 1) * 128], identb,
                               ["QK", "cb"], ["psb"])
                        cp("dve", rkTM[:, t0:t0 + 8, p * 128:(p + 1) * 128], psb[:, :].rearrange("p (j t) -> p j t", j=8),
                           ["psb"], ["KTM"])
                for t in range(ti[0], NT):
                    b = bank("a")
                    proj_tm(Wb, "Wb0", 0, 512, t, b)
                    cp("dve", rvTM[:, t, :], psf[b][:, :], [pk(b)], ["V"])
                    b = bank("a")
                    proj_tm(Wb, "Wb1", 512, 512, t, b)
                    act(rgs[:, t, :], psf[b][:, :], AF.Silu, [pk(b)], ["rgs"])
            with Scope() as S:
                st4 = S.sb("st4", [128, 4, 4])
                oTM2 = [S.sb("oTM%d" % i, [128, 512], BF16) for i in range(2)]

                def epi(c, Oa, tmpo, oak, tmk):
                    oTM, otk = oTM2[c % 2], "oTM%d" % (c % 2)
                    o3 = Oa[:].rearrange("p (h e) -> p h e", h=4)
                    rsum(st4[:, 0, :], o3, [oak], ["st4"])
                    act(junk[:, 0:512], Oa[:], AF.Square, [oak], ["junk"])
                    rsum(st4[:, 1, :], junk[:, 0:512].rearrange("p (h e) -> p h e", h=4), ["junk"], ["st4"])
                    ts("dve", st4[:, 0, :], st4[:, 0, :], 1.0 / 128, MUL, ["st4"], ["st4"])
                    tt("dve", st4[:, 2, :], st4[:, 0, :], st4[:, 0, :], MUL, ["st4"], ["st4"])
                    stt("dve", st4[:, 1, :], st4[:, 1, :], 1.0 / 128, st4[:, 2, :], MUL, SUB, ["st4"], ["st4"])
                    act(st4[:, 2, :], st4[:, 1, :], AF.Ln, ["st4"], ["st4"], bias=EPS)
                    act(st4[:, 3, :], st4[:, 2, :], AF.Exp, ["st4"], ["st4"], scale=-0.5)
                    tt("dve", o3, o3, st4[:, 0, :].unsqueeze(2).to_broadcast([128, 4, 128]), SUB, [oak, "st4"], [oak])
                    tt("dve", o3, o3, st4[:, 3, :].unsqueeze(2).to_broadcast([128, 4, 128]), MUL, [oak, "st4"], [oak])
                    tt("dve", Oa[:], Oa[:], V(l, "ret_gw", 512), MUL, [oak, "vec"], [oak])
                    tt("dve", oTM[:], Oa[:], rgs[:, c, :], MUL, [oak, "rgs"], [otk])
                    to_fm(S, c, oTM, otk, o_ret, "o_ret")
                scan(S, l, 4, 2, 128,
                     lambda kt, hf, c: rqT[hf][:, kt, c * 128:(c + 1) * 128],
                     lambda kt, c: rkT[:, kt, c * 128:(c + 1) * 128],
                     lambda kt, c: rkTM[:, c, kt * 128:(kt + 1) * 128], rvTM, la, None,
                     D["s_ret"][l], O["o_ret"][l], epi, const_decay=True)

    def attention(S, nheads, krows, scale, QTf, KTf, Vf, dst, par):
        setpools(s=[0, 1, 2, 3], o=[4, 5], n=[6])
        NPT = 7
        DEPTH_ = 5
        PT = [S.sb("PT%d" % i, [128, 512], BF16) for i in range(NPT)]
        Osb2 = [S.sb("Osb%d" % i, [128, 512]) for i in range(2)]
        rec2 = [S.sb("rec%d" % i, [128, 512]) for i in range(2)]
        seq = [(h, qg, kb) for h in range(nheads) for qg in range(NG) for kb in range(18)]
        cur_o = {}
        fin_q = []
        FIN_DELAY = 10

        def emit_pv(n):
            h, qg, kb = seq[n]
            odd = par(h)
            if kb == 0:
                cur_o[(h, qg)] = bank("o")
            bo = cur_o[(h, qg)]
            pt, ptk = PT[n % NPT], "PT%d" % (n % NPT)
            v_ = Vf(h, kb)
            if odd:
                mm(psf[bo][0:128, :], v_[:, 64:192], pt[:], kb == 0, kb == 17, ["Vx", ptk], [pk(bo)])
            else:
                mm(psf[bo][0:65, :], v_[:, 0:65], pt[:], kb == 0, kb == 17, ["Vx", ptk], [pk(bo)])
            if kb < 17:
                return
            pi = (h * NG + qg) % 2
            rec, Osb = rec2[pi], Osb2[pi]
            rk_, ok_ = "rec%d" % pi, "Osb%d" % pi
            if odd:
                recip(rec[0:1, :], psf[bo][0:1, :], [pk(bo)], [rk_])
                R_ = slice(64, 128)
            else:
                recip(rec[64:65, :], psf[bo][64:65, :], [pk(bo)], [rk_])
                R_ = slice(0, 64)
            act(Osb[R_, :], psf[bo][R_, :], AF.Copy, [pk(bo)], [ok_])

            def fin():
                bn = bank("n")
                if odd:
                    mm(psf[bn][0:128, :], onesf[0:1, 0:128], rec[0:1, :], True, True, [rk_, "cf"], [pk(bn)])
                else:
                    mm(psf[bn][0:64, :], onesf[64:65, 0:64], rec[64:65, :], True, True, [rk_, "cf"], [pk(bn)])
                tt("dve", dst(h, qg), Osb[R_, :], psf[bn][R_, :], MUL, [ok_, pk(bn)], ["oatt"])
            fin_q.append((n + FIN_DELAY, fin))

        for n, (h, qg, kb) in enumerate(seq):
            q_, k_ = QTf(h), KTf(h)
            bs = bank("s")
            mm(psf[bs][:, :], k_[0:krows, kb * 128:(kb + 1) * 128], q_[0:krows, qg * 512:(qg + 1) * 512], True, True,
               ["Q", "K", "Qaug", "Kaug"], [pk(bs)])
            act(PT[n % NPT][:], psf[bs][:, :], AF.Exp, [pk(bs)], ["PT%d" % (n % NPT)], scale=scale)
            if n >= DEPTH_:
                emit_pv(n - DEPTH_)
                while fin_q and fin_q[0][0] <= n - DEPTH_:
                    fin_q.pop(0)[1]()
        for n in range(max(0, len(seq) - DEPTH_), len(seq)):
            emit_pv(n)
        while fin_q:
            fin_q.pop(0)[1]()

    def headnorm(S, b_x, rows, ones_ap, nacc, tmp):
        sq, rs = tmp
        bm = bank("n")
        for i, b in enumerate(b_x):
            act(sq[rows, i, :], psf[b][rows, :], AF.Square, [pk(b)], ["sq"])
        for i, b in enumerate(b_x):
            mm(psf[bm][rows, :], ones_ap, sq[rows, i, :], i == 0, i == len(b_x) - 1, ["sq", "cb"], [pk(bm)])
        act(rs[rows, :], psf[bm][rows, :], AF.Ln, [pk(bm)], ["rs"], bias=EPS)
        act(rs[rows, :], rs[rows, :], AF.Exp, ["rs"], ["rs"], scale=-0.5)

    def phase_gqa(l, o_att):
      for hb in range(2):
        with Scope() as S1:
            QTb = S1.sb("QTb", [128, 4, T], BF16)
            KTb = S1.sb("KTb", [128, T + 256], BF16)
            Vx = S1.sb("Vx", [128, 18, 192], BF16)
            with Scope() as S:
                setpools(a=[0, 1, 2, 3], n=[4], t=[5, 6])
                W = S.sb("W", [128, 8, 1408], BF16)
                rope = S.sb("rope", [128, 2, T], BF16)
                t1 = S.sb("t1", [128, 512]); t2 = S.sb("t2", [128, 512])
                sq = S.sb("sq", [128, 1, 512], BF16); rs = S.sb("rs", [128, 512])
                kst = S.sb("kst", [128, NT, 64]); vst = S.sb("vst", [128, NT, 64])
                dma("pool", rope[:], D["rope64"][:, :, :], (), ["rope"], "rope")
                load_w(W, D["win"][l, :, :, 3344:4752], 1408, order=[hb, 2 + hb, 5, 4])
                itc, vtc = [0], [0]
                for j in range(4):
                    dma("pool", QTb[64:73, j, :], D["augq"][:, :], (), ["Qaug"], "aug")
                dma("pool", KTb[64:73, :], D["augk"][:, :], (), ["Kaug"], "aug")
                dma("pool", KTb[0:64, 0:256], D["ckT"][l, hb], (), ["Kaug"], "aug")
                mset("pool", Vx[:, :, 64:65], 1.0, ["Vx"])
                mset("pool", Vx[:, :, 65:128], 0.0, ["Vx"])
                dma("pool", Vx[:, 0:2, 0:64], D["cv"][l][:, hb * 64:(hb + 1) * 64].rearrange("(kb p) d -> p kb d", p=128),
                    (), ["Vx"], "aug")
                dma("pool", Vx[:, 0:2, 128:192], D["cv"][l][:, hb * 64:(hb + 1) * 64].rearrange("(kb p) d -> p kb d", p=128),
                    (), ["Vx"], "aug")
                P.seal("aug")
                R = slice(0, 64)
                for (isk, hl, c0, cp0, nname, npname) in [(False, j, 0, 512, "qn", "qnp") for j in range(4)] + \
                        [(True, 0, 1024, 1152, "kn", "knp")]:
                    hg = (hb * 4 + hl) if not isk else hb
                    for g in range(NG):
                        bx = proj_fm(W, "W", c0 + hg * 64, 64, g, "a")
                        bp = proj_fm(W, "W", cp0 + hg * 64, 64, g, "a")
                        headnorm(S, [bx], R, blk64[0:64, 0:64], 1, (sq, rs))
                        act(t1[R, :], psf[bx][R, :], AF.Copy, [pk(bx), "vec"], ["t1"], scale=V(l, nname, 1, 64))
                        act(t2[R, :], psf[bp][R, :], AF.Copy, [pk(bp), "vec"], ["t2"], scale=V(l, npname, 1, 64))
                        tt("dve", t1[R, :], t1[R, :], rs[R, :], MUL, ["t1", "rs"], ["t1"])
                        tt("dve", t2[R, :], t2[R, :], rs[R, :], MUL, ["t2", "rs"], ["t2"])
                        if isk:
                            bt = bank("t")
                            for j in range(4):
                                tp(psf[bt][:, j * 64:(j + 1) * 64], t1[R, j * 128:(j + 1) * 128], identf[0:64, 0:64],
                                   ["t1", "cf"], [pk(bt)])
                            cp("dve", kst[:, g * 4:(g + 1) * 4, :], psf[bt][:, 0:256].rearrange("p (j d) -> p j d", j=4),
                               [pk(bt)], ["kst"])
                        tt("dve", t1[R, :], t1[R, :], rope[R, 0, g * 512:(g + 1) * 512], MUL, ["t1", "rope"], ["t1"])
                        tt("pool", t2[R, :], t2[R, :], rope[R, 1, g * 512:(g + 1) * 512], MUL, ["t2", "rope"], ["t2"])
                        if isk:
                            tt("dve", KTb[R, 256 + g * 512:256 + (g + 1) * 512], t1[R, :], t2[R, :], ADD, ["t1", "t2"], ["K"])
                        else:
                            tt("dve", QTb[R, hl, g * 512:(g + 1) * 512], t1[R, :], t2[R, :], ADD, ["t1", "t2"], ["Q"])
                        itc[0] += 1
                        if itc[0] > 4 and vtc[0] < NT:
                            t = vtc[0]
                            vtc[0] += 1
                            b = bank("a")
                            proj_tm(W, "W", 1280 + hb * 64, 64, t, b)
                            act(vst[:, t, :], psf[b][:, 0:64], AF.Copy, [pk(b)], ["vst"])
                            cp("dve", Vx[:, 2 + t, 0:64], psf[b][:, 0:64], [pk(b)], ["Vx"])
                            cp("dve", Vx[:, 2 + t, 128:192], psf[b][:, 0:64], [pk(b)], ["Vx"])
                dma("sp", O["o_ck"][l][:, hb * 64:(hb + 1) * 64].rearrange("(t p) f -> p t f", p=128), kst[:, :, :],
                    ["kst"], [], "kst")
                for t in range(vtc[0], NT):
                    b = bank("a")
                    proj_tm(W, "W", 1280 + hb * 64, 64, t, b)
                    act(vst[:, t, :], psf[b][:, 0:64], AF.Copy, [pk(b)], ["vst"])
                    cp("dve", Vx[:, 2 + t, 0:64], psf[b][:, 0:64], [pk(b)], ["Vx"])
                    cp("dve", Vx[:, 2 + t, 128:192], psf[b][:, 0:64], [pk(b)], ["Vx"])
                dma("sp", O["o_cv"][l][:, hb * 64:(hb + 1) * 64].rearrange("(t p) f -> p t f", p=128), vst[:, :, :],
                    ["vst"], [], "vst")
            with Scope() as S:
                attention(S, 4, 73, 0.125, lambda j: QTb[:, j, :], lambda j: KTb[:, :],
                          lambda j, kb: Vx[:, kb, :],
                          lambda j, qg: o_att[(j % 2) * 64:(j % 2) * 64 + 64, (hb * 4 + j) // 2, qg * 512:(qg + 1) * 512],
                          lambda j: j % 2)

    def phase_mla(l, o_mla):
        with Scope() as S1:
            mcqn = S1.sb("mcqn", [128, 3, T], BF16)
            ckvT = S1.sb("ckvT", [128, 2, T + 256], BF16)
            krT = S1.sb("krT", [128, T + 256], BF16)
            rope = S1.sb("rope", [128, 2, T], BF16)
            wuq = S1.sb("wuq", [128, 3, 1536], BF16)
            wuk = S1.sb("wuk", [128, 2, 512], BF16)
            wuv = S1.sb("wuv", [128, 2, 512], BF16)
            dma("pool", rope[:], D["rope32"][:, :, :], (), ["rope"], "rope")
            dma("pool", wuq[:], D["wuq"][l], (), ["wu"], "wuq")
            dma("pool", wuk[:], D["wuk"][l], (), ["wu"], "wuk")
            dma("pool", wuv[:], D["wuv"][l], (), ["wu"], "wuv")
            for c in range(2):
                dma("pool", ckvT[:, c, 0:256], D["cckvT"][l, c * 128:(c + 1) * 128, :], (), ["ckvT"], "ck%d" % c)
            dma("pool", krT[0:32, 0:256], D["ckrT"][l], (), ["krT"], "ckr")
            with Scope() as S:
                setpools(a=[0, 1, 2], n=[3], t=[4, 5], x=[6])
                W = S.sb("W", [128, 8, 704], BF16)
                t1 = S.sb("t1", [128, 512]); t2 = S.sb("t2", [128, 512])
                sq = S.sb("sq", [128, 3, 512], BF16); rs = S.sb("rs", [128, 512])
                cst = S.sb("cst", [128, 4, 256]); kst = S.sb("kst", [128, NT, 32])
                load_w(W, D["win"][l, :, :, 4752:5456], 704)
                A_ = slice(0, 128)
                for g in range(NG):
                    bs = [proj_fm(W, "W", c * 128, 128, g, "a") for c in range(3)]
                    headnorm(S, bs, A_, o384, 3, (sq, rs))
                    for c in range(3):
                        stt("dve", mcqn[:, c, g * 512:(g + 1) * 512], psf[bs[c]][:, :], V(l, "mqn", 3)[:, c:c + 1], rs[:, :],
                            MUL, MUL, [pk(bs[c]), "rs", "vec"], ["mcqn"])
                    bs = [proj_fm(W, "W", 384 + c * 128, 128, g, "a") for c in range(2)]
                    headnorm(S, bs, A_, o256, 2, (sq, rs))
                    for c in range(2):
                        stt("dve", t1[:, :], psf[bs[c]][:, :], V(l, "mkvn", 2)[:, c:c + 1], rs[:, :], MUL, MUL,
                            [pk(bs[c]), "rs", "vec"], ["t1"])
                        cp("pool", ckvT[:, c, 256 + g * 512:256 + (g + 1) * 512], t1[:, :], ["t1"], ["ckvT"])
                        bt = bank("t")
                        for j in range(4):
                            tp(psf[bt][:, j * 128:(j + 1) * 128], t1[:, j * 128:(j + 1) * 128], identf, ["t1", "cf"], [pk(bt)])
                        cp("dve", cst[:, :, c * 128:(c + 1) * 128], psf[bt][:, :].rearrange("p (j d) -> p j d", j=4),
                           [pk(bt)], ["cst"])
                    dma("sp", O["o_ckv"][l, g * 512:(g + 1) * 512, :].rearrange("(t p) f -> p t f", p=128), cst[:, :, :],
                        ["cst"], [], "cst")
                    R = slice(0, 32)
                    bx = proj_fm(W, "W", 640, 32, g, "x")
                    act(t1[R, :], psf[bx][R, :], AF.Copy, [pk(bx)], ["t1"])
                    bp = proj_fm(W, "W", 672, 32, g, "x")
                    act(t2[R, :], psf[bp][R, :], AF.Copy, [pk(bp)], ["t2"])
                    bt = bank("t")
                    for j in range(4):
                        tp(psf[bt][:, j * 32:(j + 1) * 32], t1[R, j * 128:(j + 1) * 128], identf[0:32, 0:32], ["t1", "cf"], [pk(bt)])
                    cp("dve", kst[:, g * 4:(g + 1) * 4, :], psf[bt][:, 0:128].rearrange("p (j d) -> p j d", j=4), [pk(bt)], ["kst"])
                    tt("dve", t1[R, :], t1[R, :], rope[R, 0, g * 512:(g + 1) * 512], MUL, ["t1", "rope"], ["t1"])
                    tt("pool", t2[R, :], t2[R, :], rope[R, 1, g * 512:(g + 1) * 512], MUL, ["t2", "rope"], ["t2"])
                    tt("dve", krT[R, 256 + g * 512:256 + (g + 1) * 512], t1[R, :], t2[R, :], ADD, ["t1", "t2"], ["krT"])
                dma("sp", O["o_kr"][l].rearrange("(t p) f -> p t f", p=128), kst[:, :, :], ["kst"], [], "kst")
            for hb in range(4):
                with Scope() as S2:
                    QM = S2.sb("QM", [128, 2, T], BF16)
                    KM = S2.sb("KM", [128, 2, T + 256], BF16)
                    VM = S2.sb("VM", [128, 18, 2, 192], BF16)
                    if True:
                        S = S2
                        setpools(a=[0, 1, 2, 3], b=[4, 5])
                        t1 = S.sb("t1", [128, 512]); t2 = S.sb("t2", [128, 512])
                        mset("pool", VM[:, :, :, 64:65], 1.0, ["Vx"])
                        mset("pool", VM[:, :, :, 65:128], 0.0, ["Vx"])
                        for j in range(2):
                            dma("pool", QM[96:105, j, :], D["augq"][:, :], (), ["Qaug"], "aug")
                            dma("pool", KM[96:105, j, :], D["augk"][:, :], (), ["Kaug"], "aug")
                        P.seal("aug")
                        for j in range(2):
                            h = hb * 2 + j
                            dma("sp", KM[64:96, j, :], krT[0:32, :], ["krT"], ["Kaug"], "krc%d" % j)
                            for kg in range(5):
                                n_ = 512 if kg < 4 else 256
                                b = bank("a")
                                for c in range(2):
                                    mm(psf[b][0:64, 0:n_], wuk[:, c, h * 64:(h + 1) * 64], ckvT[:, c, kg * 512:kg * 512 + n_],
                                       c == 0, c == 1, ["wu", "ckvT"], [pk(b)])
                                act(KM[0:64, j, kg * 512:kg * 512 + n_], psf[b][0:64, 0:n_], AF.Copy, [pk(b)], ["K"])
                            for g in range(NG):
                                ba = bank("a")
                                bb2 = bank("b")
                                for c in range(3):
                                    mm(psf[ba][0:96, :], wuq[:, c, h * 192:h * 192 + 96], mcqn[:, c, g * 512:(g + 1) * 512],
                                       c == 0, c == 2, ["wu", "mcqn"], [pk(ba)])
                                for c in range(3):
                                    mm(psf[bb2][0:96, :], wuq[:, c, h * 192 + 96:h * 192 + 192], mcqn[:, c, g * 512:(g + 1) * 512],
                                       c == 0, c == 2, ["wu", "mcqn"], [pk(bb2)])
                                act(QM[0:64, j, g * 512:(g + 1) * 512], psf[ba][0:64, :], AF.Copy, [pk(ba)], ["Q"])
                                R2 = slice(64, 96)
                                tt("dve", t1[R2, :], psf[ba][R2, :], rope[R2, 0, g * 512:(g + 1) * 512], MUL, [pk(ba), "rope"], ["t1"])
                                tt("dve", t2[R2, :], psf[bb2][R2, :], rope[R2, 1, g * 512:(g + 1) * 512], MUL, [pk(bb2), "rope"], ["t2"])
                                tt("dve", QM[R2, j, g * 512:(g + 1) * 512], t1[R2, :], t2[R2, :], ADD, ["t1", "t2"], ["Q"])
                        for kb in range(18):
                            b = bank("a")
                            for c in range(2):
                                mm(psf[b][:, 0:128], ckvT[:, c, kb * 128:(kb + 1) * 128], wuv[:, c, hb * 128:(hb + 1) * 128],
                                   c == 0, c == 1, ["wu", "ckvT"], [pk(b)])
                            cp("dve", VM[:, kb, :, 0:64], psf[b][:, 0:128].rearrange("p (j d) -> p j d", j=2), [pk(b)], ["Vx"])
                            cp("dve", VM[:, kb, :, 128:192], psf[b][:, 0:128].rearrange("p (j d) -> p j d", j=2), [pk(b)], ["Vx"])
                    with Scope() as S:
                        attention(S, 2, 105, 96.0 ** -0.5, lambda j: QM[:, j, :], lambda j: KM[:, j, :],
                                  lambda j, kb: VM[:, kb, j, :],
                                  lambda j, qg: o_mla[j * 64:j * 64 + 64, hb, qg * 512:(qg + 1) * 512],
                                  lambda j: j)

    def resid_tile(S, t, banks, lhs_fn, nk, rhs_fn, src, dst, grow, bufs, rkeys):
        xt, st_ = bufs
        j = t % xt.shape[1]
        xk = "xr%d" % j
        dma("sp", xt[:, j, :], src[t * 128:(t + 1) * 128, :], (), [xk], xk)
        for hf in range(2):
            b = banks[hf]
            for k in range(nk):
                mm(psf[b][:, :], lhs_fn(k, t), rhs_fn(k, hf), k == 0, k == nk - 1, rkeys, [pk(b)])
        sk = "rst%d" % j
        mset("dve", st_[:, j, 0:2], 0.0, [sk])
        for hf in range(2):
            act(junk[:, hf * 512:(hf + 1) * 512], psf[banks[hf]][:, :], AF.Square, [pk(banks[hf])], ["junk", sk],
                accum=st_[:, j, hf:hf + 1])
        tt("dve", st_[:, j, 2:3], st_[:, j, 0:1], st_[:, j, 1:2], ADD, [sk], [sk])
        act(st_[:, j, 3:4], st_[:, j, 2:3], AF.Sqrt, [sk], [sk], bias=EPS, scale=1.0 / 1024)
        recip(st_[:, j, 4:5], st_[:, j, 3:4], [sk], [sk])
        for hf in range(2):
            stt("dve", junk[:, hf * 512:(hf + 1) * 512], psf[banks[hf]][:, :], st_[:, j, 4:5], grow[:, hf * 512:(hf + 1) * 512],
                MUL, MUL, [pk(banks[hf]), sk, "row"], ["junk"])
        tt("dve", xt[:, j, :], xt[:, j, :], junk[:, :], ADD, [xk, "junk"], [xk])
        dma("sp", dst[t * 128:(t + 1) * 128, :], xt[:, j, :], [xk], [], xk)

    def phase_merge(l, obr, src, dst):
        with Scope() as S1:
            merged = S1.sb("merged", [128, 8, T], BF16)
            with Scope() as S:
                setpools(g=[0, 1, 2], p=[3, 4, 5])
                wms = [S.sb("wms%d" % i, [128, 4, 8, 128], BF16) for i in range(2)]
                wb01 = [S.sb("wb01%d" % i, [128, 4, 4, 128], BF16) for i in range(2)]
                Gs = S.sb("Gs", [128, 512]); acc = S.sb("acc", [128, 512]); tm = S.sb("tm", [128, 512])
                for n in range(8):
                    i = n % 2
                    wk = "wmg%d" % i
                    for b_ in range(4):
                        dma("pool", wms[i][:, b_, :, :], D["wmerge"][l, :, :, b_ * 1024 + n * 128:b_ * 1024 + (n + 1) * 128],
                            (), [wk], wk)
                    for b_ in range(4):
                        dma("pool", wb01[i][:, b_, :, :], D["wbr01"][l, b_, :, :, n * 128:(n + 1) * 128], (), [wk], wk)
                    P.seal(wk)
                    for g in range(NG):
                        gs = slice(g * 512, (g + 1) * 512)
                        for b_ in range(4):
                            bg = bank("g")
                            for kc in range(8):
                                mm(psf[bg][:, :], wms[i][:, b_, kc, :], hT[:, kc, gs], kc == 0, kc == 7, [wk, "h%d" % g], [pk(bg)])
                            act(Gs[:], psf[bg][:, :], AF.Sigmoid, [pk(bg), "vec"], ["Gs"],
                                bias=V(l, "b_merge", 32)[:, b_ * 8 + n:b_ * 8 + n + 1])
                            bp = bank("p")
                            for kc in range(4):
                                mm(psf[bp][:, :], wb01[i][:, b_, kc, :], obr[b_][:, kc, gs], kc == 0, kc == 3, [wk, "obr"], [pk(bp)])
                            if b_ == 0:
                                tt("dve", acc[:], Gs[:], psf[bp][:, :], MUL, ["Gs", pk(bp)], ["acc"])
                            else:
                                tt("dve", tm[:], Gs[:], psf[bp][:, :], MUL, ["Gs", pk(bp)], ["tm"])
                                if b_ < 3:
                                    tt("dve", acc[:], acc[:], tm[:], ADD, ["acc", "tm"], ["acc"])
                                else:
                                    tt("dve", merged[:, n, gs], acc[:], tm[:], ADD, ["acc", "tm"], ["merged"])
            with Scope() as S:
                setpools(m=[0, 1], r=[2, 3, 4, 5])
                wo = S.sb("wo", [128, 8, 1024], BF16)
                grow = S.sb("grow", [128, 1024])
                xt = S.sb("xr", [128, 4, 1024]); st_ = S.sb("rst", [128, 4, 8])
                dma("pool", wo[:], D["wout"][l], (), ["wo"], "wo")
                make_row(l, 16, grow, S)
                for t in range(NT):
                    banks = [bank("r"), bank("r")]
                    resid_tile(S, t, banks, lambda k, t_: merged[:, k, t_ * 128:(t_ + 1) * 128], 8,
                               lambda k, hf: wo[:, k, hf * 512:(hf + 1) * 512], src, dst, grow, (xt, st_), ["merged", "wo"])

    def phase_ffn(l, src, dst):
        with Scope() as S1:
            actT = S1.sb("actT", [128, 22, T], BF16)
            with Scope() as S:
                setpools(a=[0, 1, 2, 3, 4, 5])
                wu = [S.sb("wu%d" % i, [128, 8, 2, 128], BF16) for i in range(2)]
                raw = [S.sb("raw%d" % i, [128, T + 2]) for i in range(2)]
                yu = S.sb("yu", [128, T]); yg = S.sb("yg", [128, T])
                nw = S.sb("nwf", [128, 88])
                for i in range(2):
                    mset("pool", raw[i][:, 0:1], 0.0, ["raw%d" % i])
                    mset("pool", raw[i][:, T + 1:T + 2], 0.0, ["raw%d" % i])
                cw0 = _VOFF["ffn_cw"]
                stt("dve", nw[:, 0:44], vec[:, l, cw0:cw0 + 44], -1.0, flg[:, 32:33].to_broadcast([128, 44]), MUL, MUL,
                    ["vec", "flg"], ["nw"])
                stt("dve", nw[:, 44:88], vec[:, l, cw0 + 88:cw0 + 132], -1.0, flg[:, 32:33].to_broadcast([128, 44]), MUL, MUL,
                    ["vec", "flg"], ["nw"])
                for c in range(22):
                    i = c % 2
                    wk = "wu%d" % i
                    dma("pool", wu[i][:, :, 0, :], D["wup"][l, :, :, c * 128:(c + 1) * 128], (), [wk], wk)
                    dma("pool", wu[i][:, :, 1, :], D["wup"][l, :, :, 2816 + c * 128:2816 + (c + 1) * 128], (), [wk], wk)
                    P.seal(wk)
                    for which in range(2):
                        ch = c + 22 * which
                        rw, rk_ = raw[which], "raw%d" % which
                        for g in range(NG):
                            b = bank("a")
                            for kc in range(8):
                                mm(psf[b][:, :], wu[i][:, kc, which, :], hT[:, kc, g * 512:(g + 1) * 512], kc == 0, kc == 7,
                                   [wk, "h%d" % g], [pk(b)])
                            act(rw[:, 1 + g * 512:1 + (g + 1) * 512], psf[b][:, :], AF.Copy, [pk(b)], [rk_])
                        y_, yk = (yu, "yu") if which == 0 else (yg, "yg")
                        conv(rw, rk_, vec[:, l, cw0 + ch:cw0 + ch + 1], vec[:, l, cw0 + 44 + ch:cw0 + 45 + ch],
                             vec[:, l, cw0 + 88 + ch:cw0 + 89 + ch], V(l, "ffn_cb", 44)[:, ch:ch + 1],
                             nw[:, ch:ch + 1], nw[:, 44 + ch:45 + ch], y_, yk)
                    act(yg[:, :], yg[:, :], AF.Silu, ["yg"], ["yg"])
                    tt("dve", actT[:, c, :], yu[:, :], yg[:, :], MUL, ["yu", "yg"], ["actT"])
            with Scope() as S:
                setpools(m=[0, 1], r=[2, 3, 4, 5])
                wd = S.sb("wd", [128, 22, 1024], BF16)
                grow = S.sb("grow", [128, 1024])
                xt = S.sb("xr", [128, 2, 1024]); st_ = S.sb("rst", [128, 2, 8])
                for q in range(2):
                    dma("pool", wd[:, q * 11:(q + 1) * 11, :], D["wdown"][l, :, q * 11:(q + 1) * 11, :], (), ["wd"], "wd%d" % q)
                make_row(l, 24, grow, S)
                for t in range(NT):
                    banks = [bank("r"), bank("r")]
                    resid_tile(S, t, banks, lambda k, t_: actT[:, k, t_ * 128:(t_ + 1) * 128], 22,
                               lambda k, hf: wd[:, k, hf * 512:(hf + 1) * 512], src, dst, grow, (xt, st_), ["actT", "wd"])

    dbg = dbg or {}
    stop = dbg.get("stop")
    nlayers = dbg.get("layers", DEPTH)

    def dump(name, t, keys=()):
        shp = list(t.shape)
        dd = nc.dram_tensor("dbg_" + name, shp, F32, kind="ExternalOutput").ap()
        idx = tuple(slice(None) for _ in shp)
        dma("pool", dd[idx], t[idx], list(keys), [], "dbg_" + name)

    xin = D["x"]
    for l in range(nlayers):
        if l == 0:
            phase_mod(lambda: phase_norm(xin, 0, der[:, 0, 0:8], modt[:, 0, 0:8]))
        else:
            phase_norm(xin, l, der[:, l, 0:8], modt[:, l, 0:8])
        if stop == "norm":
            dump("h", hT); P.flush(); return
        with Scope() as SA:
            o_ssd = SA.sb("o_ssd", [128, 4, T], BF16)
            if "ssd" not in dbg.get("skip", ()):
                phase_ssd(l, o_ssd)
            if stop == "ssd":
                dump("o_ssd", o_ssd); P.flush(); return
            with Scope() as SB:
                o_ret = SB.sb("o_ret", [128, 4, T], BF16)
                if "ret" not in dbg.get("skip", ()):
                    phase_ret(l, o_ret)
                if stop == "ret":
                    dump("o_ret", o_ret); P.flush(); return
                with Scope() as SC:
                    o_mla = SC.sb("o_mla", [128, 4, T], BF16)
                    if "mla" not in dbg.get("skip", ()):
                        phase_mla(l, o_mla)
                    if stop == "mla":
                        dump("o_mla", o_mla); P.flush(); return
                    with Scope() as SD:
                        o_att = SD.sb("o_att", [128, 4, T], BF16)
                        if "gqa" not in dbg.get("skip", ()):
                            phase_gqa(l, o_att)
                        if stop == "gqa":
                            dump("o_att", o_att); P.flush(); return
                        if stop == "mixers":
                            dump("o_ssd", o_ssd); dump("o_ret", o_ret); dump("o_mla", o_mla); dump("o_att", o_att)
                            P.flush(); return
                        phase_merge(l, [o_ssd, o_ret, o_att, o_mla], xin, xa)
        if stop == "merge":
            P.flush(); return
        phase_norm(xa, l, der[:, l, 8:16], modt[:, l, 24:32])
        dst = xb if l == 0 else O["y"]
        if nlayers == 1:
            dst = O["y"]
        phase_ffn(l, xa, dst)
        xin = xb
    P.flush()


_NC_CACHE = {}


def kernel(**inputs):
    shared, percore = _host_prep(inputs)
    if "nc" not in _NC_CACHE:
        _NC_CACHE["nc"] = build_nc()
    nc = _NC_CACHE["nc"]
    in_maps = [dict(shared, **percore[c]) for c in range(8)]
    res = run_bass_kernel_spmd(nc, in_maps, core_ids=list(range(8)))
    R = res.results
    f = np.float32
    y_prompt = np.stack([R[c]["y"] for c in range(4)]).reshape(32, 256, 1024).astype(f)
    y_sample = np.stack([R[c]["y"] for c in range(4, 8)]).reshape(4, 2048, 1024).astype(f)
    st_ssd = np.zeros((32, 2, 2, 8, 64, 64), f)
    st_ret = np.zeros((32, 2, 2, 4, 64, 128), f)
    ck = np.zeros((32, 2, 256, 2, 64), f)
    cv = np.zeros((32, 2, 256, 2, 64), f)
    cc = np.zeros((32, 2, 256, 256), f)
    cr = np.zeros((32, 2, 256, 32), f)
    for c in range(4):
        os_, or_ = R[c]["o_ssd"], R[c]["o_ret"]
        for h in range(8):
            g = h // 4
            st_ssd[8 * c:8 * c + 8, :, :, h] = os_[:, :, :, g * 64:(g + 1) * 64, h * 64:(h + 1) * 64].transpose(2, 0, 1, 3, 4)
        for h in range(4):
            kt, lh = h // 2, h % 2
            st_ret[8 * c:8 * c + 8, :, :, h] = or_[:, :, :, lh * 64:(lh + 1) * 64,
                                                   kt * 256 + lh * 128:kt * 256 + (lh + 1) * 128].transpose(2, 0, 1, 3, 4)
        ck[8 * c:8 * c + 8] = R[c]["o_ck"].reshape(2, 8, 256, 2, 64).transpose(1, 0, 2, 3, 4)
        cv[8 * c:8 * c + 8] = R[c]["o_cv"].reshape(2, 8, 256, 2, 64).transpose(1, 0, 2, 3, 4)
        cc[8 * c:8 * c + 8] = R[c]["o_ckv"].reshape(2, 8, 256, 256).transpose(1, 0, 2, 3)
        cr[8 * c:8 * c + 8] = R[c]["o_kr"].reshape(2, 8, 256, 32).transpose(1, 0, 2, 3)
    return (y_prompt, y_sample, st_ssd, st_ret, ck, cv, cc, cr)
```
